# Optimizing a Trainium2 kernel written in Bass

```python
import math
import jax, jax.numpy as jnp
from jax import lax
import numpy as np

D_MODEL = 2048
BATCH = 4
SEQ = 2048
DEPTH = 2
DEC_BATCH = 8
DEC_SEQ = 4
PAST_LEN = 16384
PAGE_SIZE = 128

HEAD_DIM = 64
D_MIX = D_MODEL
D_RWKV = D_MIX // 2
D_NSA = D_MIX - D_RWKV
H_RWKV = D_RWKV // HEAD_DIM
H_NSA = D_NSA // HEAD_DIM
NSA_KV_HEADS = 4
NSA_GROUP = H_NSA // NSA_KV_HEADS
NSA_KV_COLS = NSA_KV_HEADS * HEAD_DIM
CMP_BLOCK = 32
SEL_BLOCK = 64
SEL_TOPN = 16
WINDOW = 512
Q_BLOCK = 128
W_LORA = 64
A_LORA = 64
G_LORA = 160
RWKV_COLS = 3 * D_RWKV + W_LORA + A_LORA + G_LORA
NSA_COLS = D_NSA + 6 * NSA_KV_COLS + 3 * H_NSA
IN_COLS = RWKV_COLS + NSA_COLS
P_HEADS = 8
N_KEYS = 128
N_EXPERTS = N_KEYS * N_KEYS
P_DKEY = 256
P_TOPK = 16
P_TOK_BLOCK = 128
LN_EPS = 1e-5
GN_EPS = 64e-5
DEEPNORM_ALPHA = (2 * DEPTH) ** 0.25
DEEPNORM_BETA = (8 * DEPTH) ** -0.25
FORCE_SCORE = 1e4
NEG_INF = -1e30

kernel_name = 'hymba_rwkv7_nsa_peer_decode_step'


def layer_norm(x, g, b):
    xf = x.astype(jnp.float32)
    mu = xf.mean(-1, keepdims=True)
    var = jnp.mean(jnp.square(xf - mu), -1, keepdims=True)
    return ((xf - mu) * lax.rsqrt(var + LN_EPS) * g + b).astype(x.dtype)


def masked_softmax(s, mask):
    s = jnp.where(mask, s.astype(jnp.float32), NEG_INF)
    p = jax.nn.softmax(s, axis=-1)
    return jnp.where(mask, p, 0.0)


def rwkv_time_mix(xr, shift_prev, s0, p):
    B, T, _ = xr.shape
    dt = xr.dtype
    prev = jnp.concatenate([shift_prev.astype(dt), xr[:, :-1]], axis=1)
    xs = xr + p['rwkv_mu'] * (prev - xr)
    cuts = [D_RWKV, 2 * D_RWKV, 3 * D_RWKV, 3 * D_RWKV + W_LORA, 3 * D_RWKV + W_LORA + A_LORA]
    r, k, v, wl, al, gl = jnp.split(xs, cuts, axis=-1)
    w = -jax.nn.softplus(-(p['rwkv_w0'] + jnp.tanh(wl) @ p['rwkv_w2'])) - 0.5
    decay = jnp.exp(-jnp.exp(w))
    a = jax.nn.sigmoid(p['rwkv_a0'] + al @ p['rwkv_a2'])
    g = jax.nn.sigmoid(gl) @ p['rwkv_g2']
    hd = lambda t: t.reshape(B, T, H_RWKV, HEAD_DIM)
    kkf = hd(k * p['rwkv_k_k']).astype(jnp.float32)
    kk = (kkf * lax.rsqrt(jnp.maximum(jnp.sum(kkf * kkf, -1, keepdims=True), 1e-24))).astype(dt)
    k = k * (1.0 + (a - 1.0) * p['rwkv_k_a'])
    r, decay, k, v, a = hd(r), hd(decay), hd(k), hd(v), hd(a)

    def step(S, inp):
        r_t, d_t, k_t, v_t, kk_t, a_t = inp
        sk = jnp.einsum('bhvk,bhk->bhv', S, kk_t)
        S = (S * d_t[:, :, None, :] - sk[..., None] * (kk_t * a_t)[:, :, None, :]
             + v_t[..., None] * k_t[:, :, None, :])
        return S, jnp.einsum('bhvk,bhk->bhv', S, r_t)

    seq_first = lambda t: jnp.moveaxis(t, 1, 0)
    s_T, y = lax.scan(step, s0.astype(dt), (seq_first(r), seq_first(decay), seq_first(k),
                                            seq_first(v), seq_first(kk), seq_first(a)))
    y = jnp.moveaxis(y, 0, 1).astype(jnp.float32)
    mu = y.mean(-1, keepdims=True)
    var = jnp.mean(jnp.square(y - mu), -1, keepdims=True)
    yn = ((y - mu) * lax.rsqrt(var + GN_EPS)).reshape(B, T, D_RWKV) * p['rwkv_gn_g'] + p['rwkv_gn_b']
    bonus = jnp.sum(r * k * p['rwkv_r_k'], -1, keepdims=True) * v
    out = (yn.astype(dt) + bonus.reshape(B, T, D_RWKV)) * g
    return out, s_T, xr[:, -1:]


def nsa_mix(xn, past_cmp, past_sel, win_prefix, pos0, n_keep, cmp_w):
    B, T, _ = xn.shape
    dt = xn.dtype
    cuts = [D_NSA + i * NSA_KV_COLS for i in range(7)]
    q, kc, vc, ksl, vsl, kw, vw, gl = jnp.split(xn, cuts, axis=-1)
    kvh = lambda t: t.reshape(B, T, NSA_KV_HEADS, HEAD_DIM)
    new_cmp = jnp.stack([kvh(kc), kvh(vc)], axis=2)
    new_sel = jnp.stack([kvh(ksl), kvh(vsl)], axis=2)
    new_win = jnp.stack([kvh(kw), kvh(vw)], axis=2)
    L = past_cmp.shape[1] + T
    L_pad = -(-L // SEL_BLOCK) * SEL_BLOCK
    pad = lambda t: jnp.pad(t, ((0, 0), (0, L_pad - L), (0, 0), (0, 0), (0, 0)))
    full_cmp = pad(jnp.concatenate([past_cmp.astype(dt), new_cmp], axis=1))
    full_sel = pad(jnp.concatenate([past_sel.astype(dt), new_sel], axis=1))
    n_cmp = L_pad // CMP_BLOCK
    n_sel = L_pad // SEL_BLOCK
    cblk = full_cmp.reshape(B, n_cmp, CMP_BLOCK, 2, NSA_KV_HEADS, HEAD_DIM)
    kcb = jnp.einsum('bnlkd,l->bnkd', cblk[:, :, :, 0], cmp_w[0])
    vcb = jnp.einsum('bnlkd,l->bnkd', cblk[:, :, :, 1], cmp_w[1])
    sblk = full_sel.reshape(B, n_sel, SEL_BLOCK, 2, NSA_KV_HEADS, HEAD_DIM)
    ksb = jnp.transpose(sblk[:, :, :, 0], (0, 3, 1, 2, 4))
    vsb = jnp.transpose(sblk[:, :, :, 1], (0, 3, 1, 2, 4))
    n_prefix = win_prefix.shape[1]
    win_pad = jnp.concatenate([jnp.zeros((B, WINDOW, 2, NSA_KV_HEADS, HEAD_DIM), dt),
                               win_prefix.astype(dt), new_win], axis=1)
    q = q.reshape(B, T, NSA_KV_HEADS, NSA_GROUP, HEAD_DIM)
    gates = jax.nn.sigmoid(gl.reshape(B, T, NSA_KV_HEADS, NSA_GROUP, 3))
    qc = math.gcd(T, Q_BLOCK)
    n_chunks = T // qc
    top_n = min(SEL_TOPN, n_sel)
    scale = HEAD_DIM ** -0.5
    bi = jnp.arange(B)[:, None, None, None]
    hi = jnp.arange(NSA_KV_HEADS)[None, None, :, None]
    cmp_end = (jnp.arange(n_cmp) + 1) * CMP_BLOCK - 1
    blk = jnp.arange(n_sel)
    lw = WINDOW - 1 + qc
    jw = jnp.arange(lw)
    iw = jnp.arange(qc)

    def chunk(args):
        c, q_c, g_c = args
        qpos = pos0 + c * qc + jnp.arange(qc)
        q_c = q_c * scale
        s = jnp.einsum('bqkgd,bnkd->bqkgn', q_c, kcb)
        p_c = masked_softmax(s, (cmp_end[None, :] <= qpos[:, None])[None, :, None, None, :])
        o_cmp = jnp.einsum('bqkgn,bnkd->bqkgd', p_c.astype(dt), vcb)
        imp = p_c.sum(3).reshape(B, qc, NSA_KV_HEADS, n_sel, SEL_BLOCK // CMP_BLOCK).sum(-1)
        cur = (qpos // SEL_BLOCK)[None, :, None, None]
        forced = (blk == 0) | (blk == cur) | (blk == cur - 1)
        imp = jnp.where(blk > cur, -1.0, jnp.where(forced, FORCE_SCORE, imp))
        top_val, top_idx = lax.top_k(imp, top_n)
        k_g = ksb[bi, hi, top_idx]
        v_g = vsb[bi, hi, top_idx]
        s = jnp.einsum('bqkgd,bqknld->bqkgnl', q_c, k_g)
        kpos = top_idx[..., None] * SEL_BLOCK + jnp.arange(SEL_BLOCK)
        m = (top_val >= 0)[..., None] & (kpos <= qpos[None, :, None, None, None])
        p_s = masked_softmax(s.reshape(B, qc, NSA_KV_HEADS, NSA_GROUP, top_n * SEL_BLOCK),
                             m.reshape(B, qc, NSA_KV_HEADS, 1, top_n * SEL_BLOCK))
        o_sel = jnp.einsum('bqkgnl,bqknld->bqkgd',
                           p_s.reshape(B, qc, NSA_KV_HEADS, NSA_GROUP, top_n, SEL_BLOCK).astype(dt), v_g)
        start = n_prefix + c * qc + 1
        wkv = lax.dynamic_slice_in_dim(win_pad, start, lw, axis=1)
        s = jnp.einsum('bqkgd,bjkd->bqkgj', q_c, wkv[:, :, 0])
        m = ((jw[None, :] >= iw[:, None]) & (jw[None, :] <= iw[:, None] + WINDOW - 1)
             & (start + jw[None, :] >= WINDOW))
        p_w = masked_softmax(s, m[None, :, None, None, :])
        o_win = jnp.einsum('bqkgj,bjkd->bqkgd', p_w.astype(dt), wkv[:, :, 1])
        return g_c[..., 0:1] * o_cmp + g_c[..., 1:2] * o_sel + g_c[..., 2:3] * o_win

    chunks = lambda t: jnp.moveaxis(t.reshape(B, n_chunks, qc, *t.shape[2:]), 1, 0)
    out = lax.map(chunk, (jnp.arange(n_chunks), chunks(q), chunks(gates)))
    out = jnp.moveaxis(out, 0, 1).reshape(B, T, D_NSA)
    return out, new_cmp, new_sel, win_pad[:, -n_keep:]


def peer_ffn(x, wq, subkeys, u_tab, v_tab):
    B, T, D = x.shape
    n = B * T
    xt = x.reshape(n, D)
    q = (xt @ wq).reshape(n, P_HEADS, 2, P_DKEY // 2)
    s = jnp.einsum('nhcd,hckd->nhck', q, subkeys).astype(jnp.float32)
    hv, hix = lax.top_k(s, P_TOPK)
    cand = (hv[:, :, 0, :, None] + hv[:, :, 1, None, :]).reshape(n, P_HEADS, P_TOPK * P_TOPK)
    cand_e = (hix[:, :, 0, :, None] * N_KEYS + hix[:, :, 1, None, :]).reshape(n, P_HEADS, P_TOPK * P_TOPK)
    tv, tpos = lax.top_k(cand, P_TOPK)
    expert = jnp.take_along_axis(cand_e, tpos, axis=-1)
    gate = jax.nn.softmax(tv, axis=-1).astype(x.dtype)
    cb = math.gcd(n, P_TOK_BLOCK)

    def block(args):
        xb, eb, gb = args
        h = jax.nn.gelu(jnp.einsum('cd,chkd->chk', xb, u_tab[eb]), approximate=False)
        return jnp.einsum('chk,chkd->cd', gb * h, v_tab[eb])

    out = lax.map(block, (xt.reshape(n // cb, cb, D), expert.reshape(n // cb, cb, P_HEADS, P_TOPK),
                          gate.reshape(n // cb, cb, P_HEADS, P_TOPK)))
    return out.reshape(B, T, D)


def hybrid_layer(x, pos0, past_cmp, past_sel, win_prefix, shift_prev, rwkv_s0, n_keep, p):
    proj = x @ p['w_in']
    y_r, s_T, new_shift = rwkv_time_mix(proj[..., :RWKV_COLS], shift_prev, rwkv_s0, p)
    y_n, new_cmp, new_sel, new_win = nsa_mix(proj[..., RWKV_COLS:], past_cmp, past_sel, win_prefix,
                                             pos0, n_keep, p['nsa_cmp_w'])
    h = jnp.concatenate([y_r, y_n], axis=-1) @ p['w_out']
    x = layer_norm(DEEPNORM_ALPHA * x + h, p['ln1_g'], p['ln1_b'])
    f = peer_ffn(x, p['peer_wq'], p['peer_subkeys'], p['peer_u'], p['peer_v'])
    x = layer_norm(DEEPNORM_ALPHA * x + f, p['ln2_g'], p['ln2_b'])
    return x, (new_cmp, new_sel, new_win, s_T, new_shift)


def setup_inputs(seed: int = 0) -> dict:
    key = jax.random.key(seed)
    ks = jax.random.split(key, 40)
    n_pages = PAST_LEN // PAGE_SIZE
    n_used = DEC_BATCH * n_pages
    n_pool = n_used + n_used // 4
    win_buf = min(WINDOW, PAST_LEN)
    f32 = jnp.float32
    nrm = lambda k, shape, s: s * jax.random.normal(k, shape, f32)
    page_table = jax.random.permutation(ks[0], n_pool)[:n_used].reshape(DEC_BATCH, n_pages).astype(jnp.int32)
    kv_tail = (2, NSA_KV_HEADS, HEAD_DIM)
    return {
        'x_prompt': nrm(ks[1], (BATCH, SEQ, D_MODEL), 1.0),
        'x_sample': nrm(ks[2], (DEC_BATCH, DEC_SEQ, D_MODEL), 1.0),
        'cache_cmp_kv': nrm(ks[3], (DEPTH, n_pool, PAGE_SIZE) + kv_tail, 1.0),
        'cache_sel_kv': nrm(ks[4], (DEPTH, n_pool, PAGE_SIZE) + kv_tail, 1.0),
        'page_table': page_table,
        'state_win_kv': nrm(ks[5], (DEPTH, DEC_BATCH, win_buf) + kv_tail, 1.0),
        'state_rwkv': nrm(ks[6], (DEPTH, DEC_BATCH, H_RWKV, HEAD_DIM, HEAD_DIM), 0.3),
        'state_shift': nrm(ks[7], (DEPTH, DEC_BATCH, 1, RWKV_COLS), 1.0),
        'w_in': nrm(ks[8], (DEPTH, D_MODEL, IN_COLS), D_MODEL ** -0.5),
        'rwkv_mu': jax.random.uniform(ks[9], (DEPTH, RWKV_COLS), f32),
        'rwkv_w0': nrm(ks[10], (DEPTH, D_RWKV), 0.5) - 1.0,
        'rwkv_w2': nrm(ks[11], (DEPTH, W_LORA, D_RWKV), 0.5 * W_LORA ** -0.5),
        'rwkv_a0': nrm(ks[12], (DEPTH, D_RWKV), 0.5),
        'rwkv_a2': nrm(ks[13], (DEPTH, A_LORA, D_RWKV), 0.5 * A_LORA ** -0.5),
        'rwkv_g2': nrm(ks[14], (DEPTH, G_LORA, D_RWKV), G_LORA ** -0.5),
        'rwkv_k_k': 0.85 + nrm(ks[15], (DEPTH, D_RWKV), 0.05),
        'rwkv_k_a': 1.0 + nrm(ks[16], (DEPTH, D_RWKV), 0.05),
        'rwkv_r_k': nrm(ks[17], (DEPTH, H_RWKV, HEAD_DIM), 0.1),
        'rwkv_gn_g': 1.0 + nrm(ks[18], (DEPTH, D_RWKV), 0.01),
        'rwkv_gn_b': nrm(ks[19], (DEPTH, D_RWKV), 0.01),
        'nsa_cmp_w': (1.0 + nrm(ks[20], (DEPTH, 2, CMP_BLOCK), 0.1)) / CMP_BLOCK,
        'w_out': nrm(ks[21], (DEPTH, D_MIX, D_MODEL), DEEPNORM_BETA * D_MIX ** -0.5),
        'ln1_g': 1.0 + nrm(ks[22], (DEPTH, D_MODEL), 0.01),
        'ln1_b': nrm(ks[23], (DEPTH, D_MODEL), 0.01),
        'peer_wq': nrm(ks[24], (DEPTH, D_MODEL, P_HEADS * P_DKEY), D_MODEL ** -0.5),
        'peer_subkeys': nrm(ks[25], (DEPTH, P_HEADS, 2, N_KEYS, P_DKEY // 2), (P_DKEY // 2) ** -0.5),
        'peer_u': nrm(ks[26], (DEPTH, N_EXPERTS, D_MODEL), D_MODEL ** -0.5),
        'peer_v': nrm(ks[27], (DEPTH, N_EXPERTS, D_MODEL), DEEPNORM_BETA * P_HEADS ** -0.5),
        'ln2_g': 1.0 + nrm(ks[28], (DEPTH, D_MODEL), 0.01),
        'ln2_b': nrm(ks[29], (DEPTH, D_MODEL), 0.01),
    }


def reference(x_prompt, x_sample, cache_cmp_kv, cache_sel_kv, page_table, state_win_kv, state_rwkv,
              state_shift, w_in, rwkv_mu, rwkv_w0, rwkv_w2, rwkv_a0, rwkv_a2, rwkv_g2, rwkv_k_k,
              rwkv_k_a, rwkv_r_k, rwkv_gn_g, rwkv_gn_b, nsa_cmp_w, w_out, ln1_g, ln1_b, peer_wq,
              peer_subkeys, peer_u, peer_v, ln2_g, ln2_b):
    bp = x_prompt.shape[0]
    bs = x_sample.shape[0]
    dt = x_prompt.dtype
    n_pages = PAST_LEN // PAGE_SIZE
    past_len = n_pages * PAGE_SIZE
    n_keep = state_win_kv.shape[2]
    empty_kv = jnp.zeros((bp, 0, 2, NSA_KV_HEADS, HEAD_DIM), dt)
    zero_shift = jnp.zeros((bp, 1, RWKV_COLS), dt)
    zero_state = jnp.zeros((bp, H_RWKV, HEAD_DIM, HEAD_DIM), dt)
    yp, ys = x_prompt, x_sample
    st_p, st_s = [], []
    for l in range(DEPTH):
        p = {'w_in': w_in[l], 'rwkv_mu': rwkv_mu[l], 'rwkv_w0': rwkv_w0[l], 'rwkv_w2': rwkv_w2[l],
             'rwkv_a0': rwkv_a0[l], 'rwkv_a2': rwkv_a2[l], 'rwkv_g2': rwkv_g2[l],
             'rwkv_k_k': rwkv_k_k[l], 'rwkv_k_a': rwkv_k_a[l], 'rwkv_r_k': rwkv_r_k[l],
             'rwkv_gn_g': rwkv_gn_g[l], 'rwkv_gn_b': rwkv_gn_b[l], 'nsa_cmp_w': nsa_cmp_w[l],
             'w_out': w_out[l], 'ln1_g': ln1_g[l], 'ln1_b': ln1_b[l], 'peer_wq': peer_wq[l],
             'peer_subkeys': peer_subkeys[l], 'peer_u': peer_u[l], 'peer_v': peer_v[l],
             'ln2_g': ln2_g[l], 'ln2_b': ln2_b[l]}
        yp, sp = hybrid_layer(yp, 0, empty_kv, empty_kv, empty_kv, zero_shift, zero_state, n_keep, p)
        past_cmp = cache_cmp_kv[l, page_table].reshape(bs, past_len, 2, NSA_KV_HEADS, HEAD_DIM)
        past_sel = cache_sel_kv[l, page_table].reshape(bs, past_len, 2, NSA_KV_HEADS, HEAD_DIM)
        ys, ss = hybrid_layer(ys, past_len, past_cmp, past_sel, state_win_kv[l], state_shift[l],
                              state_rwkv[l], n_keep, p)
        st_p.append(sp)
        st_s.append(ss)
    stk = lambda sts, i: jnp.stack([s[i] for s in sts], axis=0)
    return (yp, ys, stk(st_p, 0), stk(st_p, 1), stk(st_p, 2), stk(st_p, 3), stk(st_p, 4),
            stk(st_s, 0), stk(st_s, 1), stk(st_s, 2), stk(st_s, 3), stk(st_s, 4))
```

```python
import contextlib
import itertools
import numpy as np
import concourse.bass as bass
import concourse.mybir as mybir
from concourse.bass_utils import run_bass_kernel_spmd

F32 = mybir.dt.float32
BF16 = mybir.dt.bfloat16
I32 = mybir.dt.int32
AF = mybir.ActivationFunctionType
ALU = mybir.AluOpType

NCORES = 8
D = 2048
KC = D // 128
T_P = 2048
T_S = 4
RW = 3360
IN_COLS = 5968
NSA0 = RW
KV0 = RW + 1024
DEPTH = 2


class Buf:
    __slots__ = ("w", "r")

    def __init__(self):
        self.w = None
        self.r = {}


class Eng:
    def __init__(self, ctx, nc, eng, name):
        self.nc = nc
        self.eng = eng
        self.key = name
        self.sem = ctx.enter_context(nc.semaphore("sem_" + name))
        self.cnt = 0
        self.seen = {}

    def need(self, tok):
        sem, val, key = tok
        if key == self.key and key == "pe":
            return None
        if self.seen.get(id(sem), 0) < val:
            self.seen[id(sem)] = val
            return (sem, val)
        return None

    def wait(self, tok):
        sem, val, key = tok
        if key == self.key and key == "pe":
            return
        if self.seen.get(id(sem), 0) < val:
            self.eng.wait_ge(sem, val)
            self.seen[id(sem)] = val


class DmaSem:
    def __init__(self, h, i):
        self.h = h
        self.val = 0
        self.key = "dma%d" % i


class K:
    def __init__(self, ctx, nc):
        self.ctx = ctx
        self.nc = nc
        self.pe = Eng(ctx, nc, nc.tensor, "pe")
        self.dve = Eng(ctx, nc, nc.vector, "dve")
        self.act = Eng(ctx, nc, nc.scalar, "act")
        self.pool = Eng(ctx, nc, nc.gpsimd, "pool")
        self.sp = Eng(ctx, nc, nc.sync, "sp")
        self.dsems = [DmaSem(ctx.enter_context(nc.semaphore("dsem%d" % i)), i) for i in range(32)]
        self.dnext = 0

    def _deps(self, reads, writes):
        deps = []
        for b in reads:
            if b.w is not None:
                deps.append(b.w)
        for b in writes:
            if b.w is not None:
                deps.append(b.w)
            deps.extend(b.r.values())
        return deps

    def _mark(self, tok, reads, writes):
        for b in writes:
            b.w = tok
            b.r = {}
        for b in reads:
            b.r[tok[2]] = tok

    def _waits(self, E, toks):
        best = {}
        for tok in toks:
            sem, val, key = tok
            if key == E.key and key == "pe":
                continue
            if E.seen.get(id(sem), 0) < val and best.get(id(sem), (None, 0))[1] < val:
                best[id(sem)] = (sem, val)
        needs = list(best.values())
        for sem, val in needs:
            E.seen[id(sem)] = val
        for sem, val in needs[:-1]:
            E.eng.wait_ge(sem, val)
        return needs[-1] if needs else None

    def op(self, E, fn, reads=(), writes=()):
        last = self._waits(E, self._deps(reads, writes))
        inst = fn(E.eng)
        if last is not None:
            inst.wait_op(last[0], last[1], "sem-ge")
        E.cnt += 1
        inst.then_inc(E.sem, 1)
        self._mark((E.sem, E.cnt, E.key), reads, writes)
        return inst

    def dma(self, Q, out, in_, reads=(), writes=(), **kw):
        ds = self.dsems[self.dnext]
        self.dnext = (self.dnext + 1) % len(self.dsems)
        toks = self._deps(reads, writes)
        if ds.val:
            toks.append((ds.h, ds.val, ds.key))
        last = self._waits(Q, toks)
        inst = Q.eng.dma_start(out=out, in_=in_, **kw)
        if last is not None:
            inst.wait_op(last[0], last[1], "sem-ge")
        ds.val += 16
        inst.then_inc(ds.h, 16)
        self._mark((ds.h, ds.val, ds.key), reads, writes)
        return inst

    def barrier(self):
        engs = (self.pe, self.dve, self.act, self.pool, self.sp)
        toks = [(E.sem, E.cnt, E.key) for E in engs if E.cnt]
        toks += [(ds.h, ds.val, ds.key) for ds in self.dsems if ds.val]
        for E in engs:
            for t in toks:
                E.wait(t)

    def finish(self):
        for ds in self.dsems:
            if ds.val:
                self.sp.wait((ds.h, ds.val, ds.key))
        for E in (self.pe, self.dve, self.act, self.pool):
            if E.cnt:
                self.sp.wait((E.sem, E.cnt, E.key))


def load_rwkv_consts(k, nc, sb, W):
    C = {}
    def bc(name, ap_row, n):
        t = sb("c_" + name, [128, n], F32)
        b = Buf()
        k.dma(k.sp, t[:], ap_row.partition_broadcast(128), writes=[b])
        C[name] = (t, b)
    bc("mu", W["rwkv_mu"], RW)
    bc("kkw", W["rwkv_k_k"], 1024)
    bc("ka", W["rwkv_k_a"], 1024)
    bc("rk", W["rwkv_r_k"], 1024)
    bc("gng", W["rwkv_gn_g"], 1024)
    bc("gnb", W["rwkv_gn_b"], 1024)
    w2e = sb("c_w2e", [65, 1024], F32); b1 = Buf()
    k.dma(k.sp, w2e[0:64, :], W["rwkv_w2"], writes=[b1])
    k.dma(k.sp, w2e[64:65, :], W["rwkv_w0"], writes=[b1])
    a2e = sb("c_a2e", [65, 1024], F32); b2 = Buf()
    k.dma(k.sp, a2e[0:64, :], W["rwkv_a2"], writes=[b2])
    k.dma(k.sp, a2e[64:65, :], W["rwkv_a0"], writes=[b2])
    g2a = sb("c_g2a", [128, 1024], F32); g2b = sb("c_g2b", [32, 1024], F32); b3 = Buf()
    k.dma(k.sp, g2a[:], W["rwkv_g2"][0:128, :], writes=[b3])
    k.dma(k.sp, g2b[:], W["rwkv_g2"][128:160, :], writes=[b3])
    C["w2e"] = (w2e, b1); C["a2e"] = (a2e, b2); C["g2"] = ((g2a, g2b), b3)
    return C


def rwkv_stage(k, nc, sb, ps, C, ident, b_ident, proj, row0, T, shift_prev, s0, opsd, ymix, o_state, tag, dbg=None):
    TS = 2
    mu, b_mu = C["mu"]; kkw, b_kkw = C["kkw"]; ka, b_ka = C["ka"]; rk, b_rk = C["rk"]
    gng, b_gng = C["gng"]; gnb, b_gnb = C["gnb"]
    w2e, b_w2e = C["w2e"]; a2e, b_a2e = C["a2e"]; (g2a, g2b), b_g2 = C["g2"]
    n = lambda s: "%s_%s" % (tag, s)
    xr = sb(n("xr"), [128, RW], F32); b_xr = Buf()
    xs = sb(n("xs"), [128, RW], F32); b_xs = Buf()
    lo = sb(n("lo"), [128, 2 * 65 + 160], F32); b_lo = Buf()
    lT = sb(n("lT"), [128, 4, 128], F32); b_lT = Buf()
    ops = sb(n("ops"), [128, 5, 1024], F32); b_ops = Buf()
    av = sb(n("a"), [128, 1024], F32); b_a = Buf()
    gv = sb(n("g"), [128, 1024], F32); b_g = Buf()
    t1 = sb(n("t1"), [128, 1024], F32); b_t1 = Buf()
    t2 = sb(n("t2"), [128, 1024], F32); b_t2 = Buf()
    sm = sb(n("sm"), [128, 64], F32); b_sm = Buf()
    VT = sb(n("VT"), [128, 8, 128], F32); b_VT = Buf()
    YT = sb(n("YT"), [128, 8, 128], F32); b_YT = Buf()
    SS = [sb(n("S%d" % i), [128, 512], F32) for i in range(2)]; b_SS = [Buf(), Buf()]
    S, b_S = SS[0], b_SS[0]
    sidx = [0]
    Sd = [sb(n("Sd%d" % i), [128, 512], F32) for i in range(2)]; b_Sd = [Buf(), Buf()]
    Ty = [sb(n("Ty%d" % i), [128, 512], F32) for i in range(2)]; b_Ty = [Buf(), Buf()]
    tmp = sb(n("tmp"), [128, 512], F32); b_tmp = Buf()
    tb = sb(n("tb"), [128, 512], F32); b_tb = Buf()
    t3 = [sb(n("t3%d" % i), [128, 512], F32) for i in range(2)]; b_t3 = [Buf(), Buf()]
    sk = sb(n("sk"), [128, 8], F32); b_sk = Buf()
    BC = [sb(n("BC%d" % i), [128, TS, 5, 512], F32) for i in range(2)]; b_BC = [Buf(), Buf()]
    yo = sb(n("yo"), [128, 1024], F32); b_yo = Buf()
    pz = [ps(n("pz%d" % i), [128, 512], F32) for i in range(4)]; b_pz = [Buf() for _ in range(4)]
    b_opsd = Buf()
    b_ymix = Buf()

    k.op(k.dve, lambda e: e.memset(lo[:, 64:65], 1.0), writes=[b_lo])
    k.op(k.dve, lambda e: e.memset(lo[:, 129:130], 1.0), writes=[b_lo])
    if s0 is None:
        k.op(k.dve, lambda e: e.memset(S[:], 0.0), writes=[b_S])
    else:
        k.dma(k.sp, S[:].rearrange("p (a c) -> p a c", a=8), s0.rearrange("(a b) v c -> (b v) a c", b=2), writes=[b_S])

    ntile = (T + 127) // 128
    for it in range(ntile):
        r0 = it * 128
        rows = min(128, T - r0)
        R = slice(0, rows)
        g0 = row0 + r0
        k.dma(k.sp, xr[R, :], proj[g0:g0 + rows, 0:RW], writes=[b_xr])
        if it == 0:
            k.dma(k.sp, xs[0:1, :], shift_prev, writes=[b_xs])
            if rows > 1:
                k.dma(k.sp, xs[1:rows, :], proj[g0:g0 + rows - 1, 0:RW], writes=[b_xs])
        else:
            k.dma(k.sp, xs[R, :], proj[g0 - 1:g0 + rows - 1, 0:RW], writes=[b_xs])
        k.op(k.dve, lambda e: e.tensor_tensor(out=xs[R, :], in0=xs[R, :], in1=xr[R, :], op=ALU.subtract), reads=[b_xr], writes=[b_xs])
        k.op(k.dve, lambda e: e.tensor_tensor(out=xs[R, :], in0=xs[R, :], in1=mu[R, :], op=ALU.mult), reads=[b_mu], writes=[b_xs])
        k.op(k.dve, lambda e: e.tensor_tensor(out=xs[R, :], in0=xs[R, :], in1=xr[R, :], op=ALU.add), reads=[b_xr], writes=[b_xs])
        r_, k_, v_ = xs[R, 0:1024], xs[R, 1024:2048], xs[R, 2048:3072]
        k.op(k.act, lambda e: e.activation(out=lo[R, 0:64], in_=xs[R, 3072:3136], func=AF.Tanh), reads=[b_xs], writes=[b_lo])
        k.op(k.act, lambda e: e.activation(out=lo[R, 130:290], in_=xs[R, 3200:3360], func=AF.Sigmoid), reads=[b_xs], writes=[b_lo])
        k.op(k.dve, lambda e: e.tensor_copy(out=lo[R, 65:129], in_=xs[R, 3136:3200]), reads=[b_xs], writes=[b_lo])
        segs = [(0, 65), (65, 65), (130, 128), (258, 32)]
        for j, (c0, w) in enumerate(segs):
            k.op(k.pe, lambda e: e.transpose(pz[0][0:w, j * 128:j * 128 + rows], lo[R, c0:c0 + w], ident[R, R]),
                 reads=[b_lo, b_ident], writes=[b_pz[0]])
        k.op(k.dve, lambda e: e.tensor_copy(out=lT[:, :, R], in_=pz[0][:].rearrange("p (j c) -> p j c", j=4)[:, :, R]),
             reads=[b_pz[0]], writes=[b_lT])
        for hf in range(2):
            cs = slice(hf * 512, (hf + 1) * 512)
            k.op(k.pe, lambda e: e.matmul(pz[1 + hf][R, :], lhsT=lT[0:65, 0, R], rhs=w2e[0:65, cs], start=True, stop=True),
                 reads=[b_lT, b_w2e], writes=[b_pz[1 + hf]])
            k.op(k.act, lambda e: e.activation(out=t1[R, cs], in_=pz[1 + hf][R, :], func=AF.Sigmoid), reads=[b_pz[1 + hf]], writes=[b_t1])
        k.op(k.act, lambda e: e.activation(out=ops[R, 1, :], in_=t1[R, :], func=AF.Exp, scale=-0.6065306597126334), reads=[b_t1], writes=[b_ops])
        for hf in range(2):
            cs = slice(hf * 512, (hf + 1) * 512)
            k.op(k.pe, lambda e: e.matmul(pz[1 + hf][R, :], lhsT=lT[0:65, 1, R], rhs=a2e[0:65, cs], start=True, stop=True),
                 reads=[b_lT, b_a2e], writes=[b_pz[1 + hf]])
            k.op(k.act, lambda e: e.activation(out=av[R, cs], in_=pz[1 + hf][R, :], func=AF.Sigmoid), reads=[b_pz[1 + hf]], writes=[b_a])
        for hf in range(2):
            cs = slice(hf * 512, (hf + 1) * 512)
            k.op(k.pe, lambda e: e.matmul(pz[1 + hf][R, :], lhsT=lT[0:128, 2, R], rhs=g2a[:, cs], start=True, stop=False),
                 reads=[b_lT, b_g2], writes=[b_pz[1 + hf]])
            k.op(k.pe, lambda e: e.matmul(pz[1 + hf][R, :], lhsT=lT[0:32, 3, R], rhs=g2b[:, cs], start=False, stop=True),
                 reads=[b_lT, b_g2], writes=[b_pz[1 + hf]])
            k.op(k.act, lambda e: e.activation(out=gv[R, cs], in_=pz[1 + hf][R, :], func=AF.Copy), reads=[b_pz[1 + hf]], writes=[b_g])
        k.op(k.dve, lambda e: e.tensor_tensor(out=t1[R, :], in0=k_, in1=kkw[R, :], op=ALU.mult), reads=[b_xs, b_kkw], writes=[b_t1])
        k.op(k.pool, lambda e: e.tensor_tensor(out=t2[R, :], in0=t1[R, :], in1=t1[R, :], op=ALU.mult), reads=[b_t1], writes=[b_t2])
        k.op(k.dve, lambda e: e.tensor_reduce(out=sm[R, 0:16], in_=t2[R, :].rearrange("p (h c) -> p h c", h=16), axis=mybir.AxisListType.X, op=ALU.add),
             reads=[b_t2], writes=[b_sm])
        k.op(k.dve, lambda e: e.tensor_scalar(out=sm[R, 0:16], in0=sm[R, 0:16], scalar1=1e-24, scalar2=None, op0=ALU.max), writes=[b_sm])
        k.op(k.act, lambda e: e.activation(out=sm[R, 0:16], in_=sm[R, 0:16], func=AF.Sqrt), writes=[b_sm])
        k.op(k.dve, lambda e: e.reciprocal(out=sm[R, 0:16], in_=sm[R, 0:16]), writes=[b_sm])
        k.op(k.dve, lambda e: e.tensor_tensor(out=ops[R, 0, :].rearrange("p (h c) -> p h c", h=16), in0=t1[R, :].rearrange("p (h c) -> p h c", h=16),
                                              in1=sm[R, 0:16].unsqueeze(2).to_broadcast([rows, 16, 64]), op=ALU.mult),
             reads=[b_t1, b_sm], writes=[b_ops])
        k.op(k.dve, lambda e: e.scalar_tensor_tensor(out=t2[R, :], in0=av[R, :], scalar=-1.0, in1=ka[R, :], op0=ALU.add, op1=ALU.mult),
             reads=[b_a, b_ka], writes=[b_t2])
        k.op(k.dve, lambda e: e.scalar_tensor_tensor(out=ops[R, 3, :], in0=t2[R, :], scalar=1.0, in1=k_, op0=ALU.add, op1=ALU.mult),
             reads=[b_t2, b_xs], writes=[b_ops])
        k.op(k.dve, lambda e: e.tensor_tensor(out=ops[R, 2, :], in0=ops[R, 0, :], in1=av[R, :], op=ALU.mult), reads=[b_a], writes=[b_ops])
        k.op(k.pool, lambda e: e.tensor_copy(out=ops[R, 4, :], in_=r_), reads=[b_xs], writes=[b_ops])
        k.op(k.dve, lambda e: e.tensor_tensor(out=t1[R, :], in0=r_, in1=ops[R, 3, :], op=ALU.mult), reads=[b_xs, b_ops], writes=[b_t1])
        k.op(k.dve, lambda e: e.tensor_tensor(out=t1[R, :], in0=t1[R, :], in1=rk[R, :], op=ALU.mult), reads=[b_rk], writes=[b_t1])
        k.op(k.dve, lambda e: e.tensor_reduce(out=sm[R, 16:32], in_=t1[R, :].rearrange("p (h c) -> p h c", h=16), axis=mybir.AxisListType.X, op=ALU.add),
             reads=[b_t1], writes=[b_sm])
        if dbg is not None and it == 0:
            k.dma(k.sp, dbg["ops"][R], ops[R, :, :], reads=[b_ops])
            k.dma(k.sp, dbg["a"][R], av[R, :], reads=[b_a])
            k.dma(k.sp, dbg["g"][R], gv[R, :], reads=[b_g])
            k.dma(k.sp, dbg["sm"][R], sm[R, :], reads=[b_sm])
            k.dma(k.sp, dbg["xs"][R], xs[R, :], reads=[b_xs])
        for o_ in range(5):
            for b_ in range(2):
                k.dma(k.sp, opsd[r0:r0 + rows, o_, b_, :].rearrange("t (a c) -> t a c", a=8),
                      ops[R, o_, :].rearrange("p (a b c) -> p a b c", a=8, b=2)[:, :, b_, :], reads=[b_ops], writes=[b_opsd])
        for g in range(2):
            for j in range(4):
                h8 = g * 4 + j
                k.op(k.pe, lambda e: e.transpose(pz[1 + g][:, j * 128:j * 128 + rows], xs[R, 2048 + h8 * 128:2048 + (h8 + 1) * 128], ident[R, R]),
                     reads=[b_xs, b_ident], writes=[b_pz[1 + g]])
            k.op(k.act, lambda e: e.activation(out=VT[:, g * 4:(g + 1) * 4, R], in_=pz[1 + g][:].rearrange("p (j c) -> p j c", j=4)[:, :, R], func=AF.Copy),
                 reads=[b_pz[1 + g]], writes=[b_VT])
        ngrp = (rows + TS - 1) // TS
        def load_grp(gi):
            q0 = r0 + gi * TS
            nn = min(TS, rows - gi * TS)
            bt, bb = BC[gi % 2], b_BC[gi % 2]
            for h2 in range(2):
                k.dma(k.sp, bt[h2 * 64:(h2 + 1) * 64, 0:nn, :, :], opsd[q0:q0 + nn, :, h2, :].partition_broadcast(64), reads=[b_opsd], writes=[bb])
        load_grp(0)
        v3 = lambda ap: ap.rearrange("p (a c) -> p a c", a=8)
        pend = None
        for gi in range(ngrp):
            if gi + 1 < ngrp:
                load_grp(gi + 1)
            bt, bb = BC[gi % 2], b_BC[gi % 2]
            nn = min(TS, rows - gi * TS)
            for s in range(nn):
                tl = gi * TS + s
                cur, b_cur = SS[sidx[0] % 2], b_SS[sidx[0] % 2]
                nxt, b_nxt = SS[(sidx[0] + 1) % 2], b_SS[(sidx[0] + 1) % 2]
                sidx[0] += 1
                tt, btt = t3[tl % 2], b_t3[tl % 2]
                sd, bsd = Sd[tl % 2], b_Sd[tl % 2]
                ty, bty = Ty[tl % 2], b_Ty[tl % 2]
                k.op(k.pool, lambda e: e.tensor_tensor(out=sd[:], in0=cur[:], in1=bt[:, s, 1, :], op=ALU.mult), reads=[b_cur, bb], writes=[bsd])
                k.op(k.pool, lambda e: e.tensor_tensor(out=v3(tt[:]), in0=v3(bt[:, s, 3, :]), in1=VT[:, :, tl:tl + 1].to_broadcast([128, 8, 64]), op=ALU.mult),
                     reads=[bb, b_VT], writes=[btt])
                k.op(k.dve, lambda e: e.tensor_tensor(out=tmp[:], in0=cur[:], in1=bt[:, s, 0, :], op=ALU.mult), reads=[b_cur, bb], writes=[b_tmp])
                k.op(k.dve, lambda e: e.tensor_reduce(out=sk[:], in_=v3(tmp[:]), axis=mybir.AxisListType.X, op=ALU.add), reads=[b_tmp], writes=[b_sk])
                k.op(k.dve, lambda e: e.tensor_tensor(out=v3(tb[:]), in0=v3(bt[:, s, 2, :]), in1=sk[:].unsqueeze(2).to_broadcast([128, 8, 64]), op=ALU.mult),
                     reads=[bb, b_sk], writes=[b_tb])
                if pend is not None:
                    pend()
                    pend = None
                k.op(k.dve, lambda e: e.tensor_tensor(out=nxt[:], in0=sd[:], in1=tb[:], op=ALU.subtract), reads=[bsd, b_tb], writes=[b_nxt])
                k.op(k.dve, lambda e: e.tensor_tensor(out=nxt[:], in0=nxt[:], in1=tt[:], op=ALU.add), reads=[btt], writes=[b_nxt])
                k.op(k.pool, lambda e: e.tensor_tensor(out=ty[:], in0=nxt[:], in1=bt[:, s, 4, :], op=ALU.mult), reads=[b_nxt, bb], writes=[bty])
                def mk(ty=ty, bty=bty, tl=tl):
                    k.op(k.dve, lambda e: e.tensor_reduce(out=YT[:, :, tl], in_=v3(ty[:]), axis=mybir.AxisListType.X, op=ALU.add), reads=[bty], writes=[b_YT])
                pend = mk
        if pend is not None:
            pend()
            pend = None
        if dbg is not None and it == 0:
            k.dma(k.sp, dbg["YT"], YT[:], reads=[b_YT])
            k.dma(k.sp, dbg["VT"], VT[:], reads=[b_VT])
        for g in range(2):
            for j in range(4):
                h8 = g * 4 + j
                k.op(k.pe, lambda e: e.transpose(pz[1 + g][R, j * 128:(j + 1) * 128], YT[:, h8, R], ident[:, :]),
                     reads=[b_YT, b_ident], writes=[b_pz[1 + g]])
            k.op(k.act, lambda e: e.activation(out=yo[R, g * 512:(g + 1) * 512], in_=pz[1 + g][R, :], func=AF.Copy), reads=[b_pz[1 + g]], writes=[b_yo])
        y3 = yo[R, :].rearrange("p (h c) -> p h c", h=16)
        k.op(k.dve, lambda e: e.tensor_reduce(out=sm[R, 32:48], in_=y3, axis=mybir.AxisListType.X, op=ALU.add), reads=[b_yo], writes=[b_sm])
        k.op(k.dve, lambda e: e.tensor_scalar(out=sm[R, 32:48], in0=sm[R, 32:48], scalar1=1.0 / 64, scalar2=None, op0=ALU.mult), writes=[b_sm])
        k.op(k.dve, lambda e: e.tensor_tensor(out=y3, in0=y3, in1=sm[R, 32:48].unsqueeze(2).to_broadcast([rows, 16, 64]), op=ALU.subtract), reads=[b_sm], writes=[b_yo])
        k.op(k.pool, lambda e: e.tensor_tensor(out=t1[R, :], in0=yo[R, :], in1=yo[R, :], op=ALU.mult), reads=[b_yo], writes=[b_t1])
        k.op(k.dve, lambda e: e.tensor_reduce(out=sm[R, 48:64], in_=t1[R, :].rearrange("p (h c) -> p h c", h=16), axis=mybir.AxisListType.X, op=ALU.add),
             reads=[b_t1], writes=[b_sm])
        k.op(k.dve, lambda e: e.tensor_scalar(out=sm[R, 48:64], in0=sm[R, 48:64], scalar1=1.0 / 64, scalar2=64e-5, op0=ALU.mult, op1=ALU.add), writes=[b_sm])
        k.op(k.act, lambda e: e.activation(out=sm[R, 48:64], in_=sm[R, 48:64], func=AF.Sqrt), writes=[b_sm])
        k.op(k.dve, lambda e: e.reciprocal(out=sm[R, 48:64], in_=sm[R, 48:64]), writes=[b_sm])
        k.op(k.dve, lambda e: e.tensor_tensor(out=y3, in0=y3, in1=sm[R, 48:64].unsqueeze(2).to_broadcast([rows, 16, 64]), op=ALU.mult), reads=[b_sm], writes=[b_yo])
        k.op(k.dve, lambda e: e.tensor_tensor(out=yo[R, :], in0=yo[R, :], in1=gng[R, :], op=ALU.mult), reads=[b_gng], writes=[b_yo])
        k.op(k.dve, lambda e: e.tensor_tensor(out=yo[R, :], in0=yo[R, :], in1=gnb[R, :], op=ALU.add), reads=[b_gnb], writes=[b_yo])
        k.op(k.dve, lambda e: e.tensor_tensor(out=t1[R, :].rearrange("p (h c) -> p h c", h=16), in0=v_.rearrange("p (h c) -> p h c", h=16),
                                              in1=sm[R, 16:32].unsqueeze(2).to_broadcast([rows, 16, 64]), op=ALU.mult),
             reads=[b_xs, b_sm], writes=[b_t1])
        k.op(k.dve, lambda e: e.tensor_tensor(out=yo[R, :], in0=yo[R, :], in1=t1[R, :], op=ALU.add), reads=[b_t1], writes=[b_yo])
        k.op(k.dve, lambda e: e.tensor_tensor(out=yo[R, :], in0=yo[R, :], in1=gv[R, :], op=ALU.mult), reads=[b_g], writes=[b_yo])
        k.dma(k.sp, ymix[r0:r0 + rows, 0:1024], yo[R, :], reads=[b_yo], writes=[b_ymix])
    Sf, b_Sf = SS[sidx[0] % 2], b_SS[sidx[0] % 2]
    k.dma(k.sp, o_state.rearrange("(a b) v c -> (b v) a c", b=2), Sf[:].rearrange("p (a c) -> p a c", a=8), reads=[b_Sf])
    return b_ymix
NEG = -1.0e30


def nsa_host_consts():
    p = np.arange(128)
    mask4 = (p[:, None] // 32 == np.arange(4)[None, :]).astype(np.float32)
    biasC = np.zeros((16, 128, 64), np.float32)
    FM = np.zeros((16, 128, 32), np.float32)
    FA = np.zeros((16, 128, 32), np.float32)
    for i in range(16):
        qpos = i * 128 + p
        cmp_end = (np.arange(64) + 1) * 32 - 1
        biasC[i] = np.where(cmp_end[None, :] <= qpos[:, None], 0.0, NEG)
        cur = qpos // 64
        blk = np.arange(32)[None, :]
        forced = (blk == 0) | (blk == cur[:, None]) | (blk == cur[:, None] - 1)
        fut = blk > cur[:, None]
        FM[i] = np.where(fut | forced, 0.0, 1.0)
        FA[i] = np.where(fut, -1.0, np.where(forced, 1.0e4, 0.0))
    tri = np.where(p[None, :] > p[:, None], NEG, 0.0).astype(np.float32)
    bw4 = np.where(p[None, :] >= p[:, None] + 1, 0.0, NEG).astype(np.float32)
    return {"nsa_mask4": mask4, "nsa_biasC": biasC, "nsa_FM": FM, "nsa_FA": FA, "nsa_tri": tri, "nsa_bw4": bw4}


def nsa_prompt_stage(k, nc, sb, ps, ident, b_ident, proj, row0, T, cmp_w, CN, ymix, tag):
    NT = T // 128
    X = mybir.AxisListType.X
    n = lambda s: "%s_%s" % (tag, s)
    PS = [ps(n("ps%d" % i), [128, 512], F32) for i in range(8)]
    bPS = [Buf() for _ in range(8)]
    mask4 = sb(n("mask4"), [128, 4], F32); tri = sb(n("tri"), [128, 128], F32); bw4 = sb(n("bw4"), [128, 128], F32)
    b_c = Buf()
    k.dma(k.sp, mask4[:], CN["nsa_mask4"], writes=[b_c])
    k.dma(k.sp, tri[:], CN["nsa_tri"], writes=[b_c])
    k.dma(k.sp, bw4[:], CN["nsa_bw4"], writes=[b_c])
    wcol = sb(n("wcol"), [128, 2], F32); b_wcol = Buf()
    for c in range(2):
        for j in range(4):
            k.dma(k.sp, wcol[j * 32:(j + 1) * 32, c:c + 1], cmp_w[c:c + 1, :].rearrange("o l -> l o"), writes=[b_wcol])
    Ww = sb(n("Ww"), [128, 2, 124], F32); b_Ww = Buf()
    k.op(k.dve, lambda e: e.memset(Ww[:], 0.0), writes=[b_Ww])
    for c in range(2):
        k.op(k.dve, lambda e: e.tensor_scalar(out=Ww[:, c, 60:64], in0=mask4[:], scalar1=wcol[:, c:c + 1], scalar2=None, op0=ALU.mult),
             reads=[b_c, b_wcol], writes=[b_Ww])
    KTs = sb(n("KTs"), [64, 4, T], F32); KTw = sb(n("KTw"), [64, 4, T], F32)
    Vs = sb(n("Vs"), [128, NT, 256], F32); Vw = sb(n("Vw"), [128, NT, 256], F32)
    b_KT = Buf(); b_V = Buf()
    kin = [sb(n("kin%d" % i), [128, 3, 256], F32) for i in range(2)]; b_kin = [Buf(), Buf()]
    kcbT = sb(n("kcbT"), [64, 4, 64], F32); vcb = sb(n("vcb"), [64, 256], F32); b_cb = Buf()
    for i in range(NT):
        g0 = row0 + i * 128
        kt, bk = kin[i % 2], b_kin[i % 2]
        for j, off in enumerate((0, 512, 1024)):
            k.dma(k.sp, kt[:, j, :], proj[g0:g0 + 128, KV0 + off:KV0 + off + 256], writes=[bk])
        k.dma(k.sp, Vs[:, i, :], proj[g0:g0 + 128, KV0 + 768:KV0 + 1024], writes=[b_V])
        k.dma(k.sp, Vw[:, i, :], proj[g0:g0 + 128, KV0 + 1280:KV0 + 1536], writes=[b_V])
        vct = sb(n("vct%d" % i), [128, 256], F32) if False else None
        for (j, dst, pi) in ((1, KTs, 0), (2, KTw, 1)):
            for kv in range(4):
                k.op(k.pe, lambda e: e.transpose(PS[pi][0:64, kv * 128:(kv + 1) * 128], kt[:, j, kv * 64:(kv + 1) * 64], ident[:, :]),
                     reads=[bk, b_ident], writes=[bPS[pi]])
            k.op(k.act if pi else k.dve,
                 (lambda e: e.activation(out=dst[:, :, i * 128:(i + 1) * 128], in_=PS[pi][0:64, :].rearrange("p (a c) -> p a c", a=4), func=AF.Copy)) if pi else
                 (lambda e: e.tensor_copy(out=dst[:, :, i * 128:(i + 1) * 128], in_=PS[pi][0:64, :].rearrange("p (a c) -> p a c", a=4))),
                 reads=[bPS[pi]], writes=[b_KT])
        for kv in range(4):
            k.op(k.pe, lambda e: e.matmul(PS[2][0:64, kv * 64:(kv + 1) * 64], lhsT=kt[:, 0, kv * 64:(kv + 1) * 64], rhs=Ww[:, 0, 60 - 4 * i:124 - 4 * i],
                                          start=(i == 0 and kv == 0), stop=(i == NT - 1 and kv == 3), skip_group_check=True),
                 reads=[bk, b_Ww], writes=[bPS[2]])
    for i in range(NT):
        g0 = row0 + i * 128
        kt, bk = kin[i % 2], b_kin[i % 2]
        k.dma(k.sp, kt[:, 0, :], proj[g0:g0 + 128, KV0 + 256:KV0 + 512], writes=[bk])
        k.op(k.pe, lambda e: e.matmul(PS[3][0:64, 0:256], lhsT=Ww[:, 1, 60 - 4 * i:124 - 4 * i], rhs=kt[:, 0, :], start=(i == 0), stop=(i == NT - 1)),
             reads=[bk, b_Ww], writes=[bPS[3]])
    k.op(k.dve, lambda e: e.tensor_copy(out=kcbT[:], in_=PS[2][0:64, 0:256].rearrange("p (a c) -> p a c", a=4)), reads=[bPS[2]], writes=[b_cb])
    k.op(k.dve, lambda e: e.tensor_copy(out=vcb[:], in_=PS[3][0:64, 0:256]), reads=[bPS[3]], writes=[b_cb])

    qin = sb(n("qin"), [128, 1024 + 48], F32); b_qin = Buf()
    qT = sb(n("qT"), [64, 16, 128], F32); b_qT = Buf()
    cst = sb(n("cst"), [128, 64 + 32 + 32], F32); b_cst = Buf()
    sc = sb(n("sc"), [128, T], F32); b_sc = Buf()
    st = sb(n("st"), [128, 8], F32); b_st = Buf()
    pc = sb(n("pc"), [128, 64], F32); b_pc = Buf()
    imp = sb(n("imp"), [128, 4, 32], F32); b_imp = Buf()
    impw = sb(n("impw"), [128, 4, 32], F32); b_impw = Buf()
    mx = sb(n("mx"), [128, 16], F32); b_mx = Buf()
    selb = sb(n("selb"), [128, 4, 32], F32); b_selb = Buf()
    eT = [sb(n("eT%d" % i), [128, 128], F32) for i in range(2)]; b_eT = [Buf(), Buf()]
    oc = sb(n("oc"), [128, 3, 16, 64], F32); b_oc = Buf()
    gt = sb(n("gt"), [128, 48], F32); b_gt = Buf()
    yo = sb(n("yo"), [128, 1024], F32); b_yo = Buf()
    b_ymix = Buf()
    net = [0]

    def softmax_pv(h, ncols, vtiles, slot):
        k.op(k.dve, lambda e: e.tensor_reduce(out=st[:, 0:1], in_=sc[:, 0:ncols], axis=X, op=ALU.max), reads=[b_sc], writes=[b_st])
        k.op(k.dve, lambda e: e.tensor_scalar(out=st[:, 0:1], in0=st[:, 0:1], scalar1=-1.0e20, scalar2=-1.0, op0=ALU.max, op1=ALU.mult), writes=[b_st])
        k.op(k.act, lambda e: e.activation(out=sc[:, 0:ncols], in_=sc[:, 0:ncols], func=AF.Exp, bias=st[:, 0:1], scale=1.0), reads=[b_st], writes=[b_sc])
        k.op(k.dve, lambda e: e.tensor_reduce(out=st[:, 1:2], in_=sc[:, 0:ncols], axis=X, op=ALU.add), reads=[b_sc], writes=[b_st])
        k.op(k.dve, lambda e: e.tensor_scalar(out=st[:, 1:2], in0=st[:, 1:2], scalar1=1.0e-30, scalar2=None, op0=ALU.max), writes=[b_st])
        k.op(k.dve, lambda e: e.reciprocal(out=st[:, 1:2], in_=st[:, 1:2]), writes=[b_st])
        for vi, (c0, w, vap) in enumerate(vtiles):
            pt, bpt = PS[4 + net[0] % 2], bPS[4 + net[0] % 2]
            et, bet = eT[net[0] % 2], b_eT[net[0] % 2]
            net[0] += 1
            k.op(k.pe, lambda e: e.transpose(pt[0:w, 0:128], sc[:, c0:c0 + w], ident[:, :]), reads=[b_sc, b_ident], writes=[bpt])
            k.op(k.act, lambda e: e.activation(out=et[0:w, :], in_=pt[0:w, 0:128], func=AF.Copy), reads=[bpt], writes=[bet])
            k.op(k.pe, lambda e: e.matmul(PS[6][:, 0:64], lhsT=et[0:w, :], rhs=vap, start=(vi == 0), stop=(vi == len(vtiles) - 1)),
                 reads=[bet, b_V, b_cb], writes=[bPS[6]])
        k.op(k.dve, lambda e: e.tensor_scalar(out=oc[:, slot, h, :], in0=PS[6][:, 0:64], scalar1=st[:, 1:2], scalar2=None, op0=ALU.mult),
             reads=[bPS[6], b_st], writes=[b_oc])

    for i in range(NT):
        g0 = row0 + i * 128
        k.dma(k.sp, qin[:, 0:1024], proj[g0:g0 + 128, NSA0:NSA0 + 1024], writes=[b_qin])
        k.dma(k.sp, qin[:, 1024:1072], proj[g0:g0 + 128, KV0 + 1536:KV0 + 1584], writes=[b_qin])
        k.dma(k.sp, cst[:, 0:64], CN["nsa_biasC"][i], writes=[b_cst])
        k.dma(k.sp, cst[:, 64:96], CN["nsa_FM"][i], writes=[b_cst])
        k.dma(k.sp, cst[:, 96:128], CN["nsa_FA"][i], writes=[b_cst])
        k.op(k.act, lambda e: e.activation(out=gt[:], in_=qin[:, 1024:1072], func=AF.Sigmoid), reads=[b_qin], writes=[b_gt])
        for g in range(4):
            for j in range(4):
                h = g * 4 + j
                k.op(k.pe, lambda e: e.transpose(PS[g % 2][0:64, j * 128:(j + 1) * 128], qin[:, h * 64:(h + 1) * 64], ident[:, :]),
                     reads=[b_qin, b_ident], writes=[bPS[g % 2]])
            k.op(k.act, lambda e: e.activation(out=qT[:, g * 4:(g + 1) * 4, :], in_=PS[g % 2][0:64, :].rearrange("p (a c) -> p a c", a=4), func=AF.Copy, scale=0.125),
                 reads=[bPS[g % 2]], writes=[b_qT])
        for h in range(16):
            kv, g = h // 4, h % 4
            k.op(k.pe, lambda e: e.matmul(PS[2][:, 0:64], lhsT=qT[:, h, :], rhs=kcbT[:, kv, :], start=True, stop=True), reads=[b_qT, b_cb], writes=[bPS[2]])
            k.op(k.dve, lambda e: e.tensor_tensor(out=sc[:, 0:64], in0=PS[2][:, 0:64], in1=cst[:, 0:64], op=ALU.add), reads=[bPS[2], b_cst], writes=[b_sc])
            softmax_pv(h, 64, [(0, 64, vcb[:, kv * 64:(kv + 1) * 64])], 0)
            k.op(k.dve, lambda e: e.tensor_scalar(out=pc[:], in0=sc[:, 0:64], scalar1=st[:, 1:2], scalar2=None, op0=ALU.mult), reads=[b_sc, b_st], writes=[b_pc])
            if g == 0:
                k.op(k.dve, lambda e: e.tensor_reduce(out=imp[:, kv, :], in_=pc[:].rearrange("p (a c) -> p a c", c=2), axis=X, op=ALU.add), reads=[b_pc], writes=[b_imp])
            else:
                k.op(k.dve, lambda e: e.tensor_reduce(out=mx[:, 0:0 + 32] if False else impw[:, 0, :], in_=pc[:].rearrange("p (a c) -> p a c", c=2), axis=X, op=ALU.add),
                     reads=[b_pc], writes=[b_impw])
                k.op(k.dve, lambda e: e.tensor_tensor(out=imp[:, kv, :], in0=imp[:, kv, :], in1=impw[:, 0, :], op=ALU.add), reads=[b_impw], writes=[b_imp])
        k.op(k.dve, lambda e: e.tensor_tensor(out=imp[:], in0=imp[:], in1=cst[:, 64:96].unsqueeze(1).to_broadcast([128, 4, 32]), op=ALU.mult), reads=[b_cst], writes=[b_imp])
        k.op(k.dve, lambda e: e.tensor_tensor(out=imp[:], in0=imp[:], in1=cst[:, 96:128].unsqueeze(1).to_broadcast([128, 4, 32]), op=ALU.add), reads=[b_cst], writes=[b_imp])
        for kv in range(4):
            k.op(k.dve, lambda e: e.max(out=mx[:, 0:8], in_=imp[:, kv, :]), reads=[b_imp], writes=[b_mx])
            k.op(k.dve, lambda e: e.match_replace(out=impw[:, kv, :], in_to_replace=mx[:, 0:8], in_values=imp[:, kv, :], imm_value=-2.0), reads=[b_imp, b_mx], writes=[b_impw])
            k.op(k.dve, lambda e: e.max(out=mx[:, 8:16], in_=impw[:, kv, :]), reads=[b_impw], writes=[b_mx])
            k.op(k.dve, lambda e: e.tensor_scalar(out=mx[:, 15:16], in0=mx[:, 15:16], scalar1=0.0, scalar2=None, op0=ALU.max), writes=[b_mx])
            k.op(k.dve, lambda e: e.tensor_scalar(out=selb[:, kv, :], in0=imp[:, kv, :], scalar1=mx[:, 15:16], scalar2=None, op0=ALU.is_ge), reads=[b_imp, b_mx], writes=[b_selb])
        k.op(k.dve, lambda e: e.tensor_scalar(out=selb[:], in0=selb[:], scalar1=-1.0, scalar2=1.0e30, op0=ALU.add, op1=ALU.mult), writes=[b_selb])
        nk = (i + 1) * 128
        for h in range(16):
            kv = h // 4
            for c0 in range(0, nk, 512):
                w = min(512, nk - c0)
                pt, bpt = PS[(c0 // 512) % 2], bPS[(c0 // 512) % 2]
                k.op(k.pe, lambda e: e.matmul(pt[:, 0:w], lhsT=qT[:, h, :], rhs=KTs[:, kv, c0:c0 + w], start=True, stop=True), reads=[b_qT, b_KT], writes=[bpt])
                nb = w // 64
                k.op(k.dve, lambda e: e.tensor_tensor(out=sc[:, c0:c0 + w].rearrange("p (a c) -> p a c", c=64), in0=pt[:, 0:w].rearrange("p (a c) -> p a c", c=64),
                                                      in1=selb[:, kv, c0 // 64:c0 // 64 + nb].unsqueeze(2).to_broadcast([128, nb, 64]), op=ALU.add),
                     reads=[bpt, b_selb], writes=[b_sc])
            k.op(k.dve, lambda e: e.tensor_tensor(out=sc[:, nk - 128:nk], in0=sc[:, nk - 128:nk], in1=tri[:], op=ALU.add), reads=[b_c], writes=[b_sc])
            softmax_pv(h, nk, [(j * 128, 128, Vs[:, j, kv * 64:(kv + 1) * 64]) for j in range(i + 1)], 1)
            j0 = max(0, i - 4)
            wk = (i + 1 - j0) * 128
            for c0 in range(0, wk, 512):
                w = min(512, wk - c0)
                pt, bpt = PS[(c0 // 512) % 2], bPS[(c0 // 512) % 2]
                k.op(k.pe, lambda e: e.matmul(pt[:, 0:w], lhsT=qT[:, h, :], rhs=KTw[:, kv, j0 * 128 + c0:j0 * 128 + c0 + w], start=True, stop=True), reads=[b_qT, b_KT], writes=[bpt])
                k.op(k.act, lambda e: e.activation(out=sc[:, c0:c0 + w], in_=pt[:, 0:w], func=AF.Copy), reads=[bpt], writes=[b_sc])
            k.op(k.dve, lambda e: e.tensor_tensor(out=sc[:, wk - 128:wk], in0=sc[:, wk - 128:wk], in1=tri[:], op=ALU.add), reads=[b_c], writes=[b_sc])
            if i >= 4:
                k.op(k.dve, lambda e: e.tensor_tensor(out=sc[:, 0:128], in0=sc[:, 0:128], in1=bw4[:], op=ALU.add), reads=[b_c], writes=[b_sc])
            softmax_pv(h, wk, [(jj * 128, 128, Vw[:, j0 + jj, kv * 64:(kv + 1) * 64]) for jj in range(i + 1 - j0)], 2)
        g3 = gt[:].rearrange("p (h c) -> p h c", c=3)
        y3 = yo[:].rearrange("p (h c) -> p h c", h=16)
        k.op(k.dve, lambda e: e.tensor_tensor(out=y3, in0=oc[:, 0, :, :], in1=g3[:, :, 0:1].to_broadcast([128, 16, 64]), op=ALU.mult), reads=[b_oc, b_gt], writes=[b_yo])
        for s_ in (1, 2):
            k.op(k.dve, lambda e: e.tensor_tensor(out=oc[:, s_, :, :], in0=oc[:, s_, :, :], in1=g3[:, :, s_:s_ + 1].to_broadcast([128, 16, 64]), op=ALU.mult), reads=[b_gt], writes=[b_oc])
            k.op(k.dve, lambda e: e.tensor_tensor(out=y3, in0=y3, in1=oc[:, s_, :, :], op=ALU.add), reads=[b_oc], writes=[b_yo])
        k.dma(k.sp, ymix[i * 128:(i + 1) * 128, 1024:2048], yo[:], reads=[b_yo], writes=[b_ymix])
    return b_ymix
def linear_stage(k, nc, sb, ps, ident, b_ident, src, T, W, N, dst, tag, b_src=None):
    n = lambda s: "%s_%s" % (tag, s)
    TCH = 1024
    CB = 512
    xT = sb(n("xT"), [128, KC, TCH + 4], F32)
    xin = [sb(n("xin%d" % i), [128, D], F32) for i in range(2)]; b_xin = [Buf(), Buf()]
    wst = [sb(n("w%d" % i), [128, KC, CB], F32) for i in range(2)]; b_w = [Buf(), Buf()]
    po = [sb(n("po%d" % i), [128, CB], F32) for i in range(2)]; b_po = [Buf(), Buf()]
    tp = [ps(n("tp%d" % i), [128, 512], F32) for i in range(2)]; b_tp = [Buf(), Buf()]
    pp = [ps(n("pp%d" % i), [128, CB], F32) for i in range(2)]; b_pp = [Buf(), Buf()]
    b_dst = Buf()
    w_v = W.rearrange("(kc p) c -> p kc c", p=128)
    nblk = (N + CB - 1) // CB
    cnt = [0, 0, 0]
    for t0 in range(0, T, TCH):
        tn = min(TCH + 4, T - t0) if T - t0 <= TCH + 4 else TCH
        tiles = [(a, min(128, tn - a)) for a in range(0, tn, 128)]
        b_xT = [Buf() for _ in tiles]
        for ti, (a, rows) in enumerate(tiles):
            xb, bx = xin[cnt[0] % 2], b_xin[cnt[0] % 2]
            cnt[0] += 1
            k.dma(k.sp, xb[0:rows, :], src[t0 + a:t0 + a + rows, :], reads=([b_src] if b_src is not None else []), writes=[bx])
            for g in range(KC // 4):
                pt, bp = tp[cnt[1] % 2], b_tp[cnt[1] % 2]
                cnt[1] += 1
                for j in range(4):
                    kc = g * 4 + j
                    k.op(k.pe, lambda e: e.transpose(pt[:, j * 128:j * 128 + rows], xb[0:rows, kc * 128:(kc + 1) * 128], ident[0:rows, 0:rows]),
                         reads=[bx, b_ident], writes=[bp])
                dstv = xT[:, g * 4:(g + 1) * 4, a:a + rows]
                srcp = pt[:].rearrange("p (j c) -> p j c", j=4)[:, :, 0:rows]
                if g % 2 == 0:
                    k.op(k.dve, lambda e: e.tensor_copy(out=dstv, in_=srcp), reads=[bp], writes=[b_xT[ti]])
                else:
                    k.op(k.act, lambda e: e.activation(out=dstv, in_=srcp, func=AF.Copy), reads=[bp], writes=[b_xT[ti]])
        for bi in range(nblk):
            c0 = bi * CB
            cw = min(CB, N - c0)
            wb, bw = wst[bi % 2], b_w[bi % 2]
            k.dma(k.sp, wb[:, :, 0:cw], w_v[:, :, c0:c0 + cw], writes=[bw])
            for ti, (a, rows) in enumerate(tiles):
                pt, bp = pp[cnt[2] % 2], b_pp[cnt[2] % 2]
                ot, bo = po[cnt[2] % 2], b_po[cnt[2] % 2]
                cnt[2] += 1
                for kc in range(KC):
                    k.op(k.pe, lambda e: e.matmul(pt[0:rows, 0:cw], lhsT=xT[:, kc, a:a + rows], rhs=wb[:, kc, 0:cw], start=(kc == 0), stop=(kc == KC - 1)),
                         reads=[b_xT[ti], bw], writes=[bp])
                if cnt[2] % 2:
                    k.op(k.act, lambda e: e.activation(out=ot[0:rows, 0:cw], in_=pt[0:rows, 0:cw], func=AF.Copy), reads=[bp], writes=[bo])
                else:
                    k.op(k.dve, lambda e: e.tensor_copy(out=ot[0:rows, 0:cw], in_=pt[0:rows, 0:cw]), reads=[bp], writes=[bo])
                k.dma(k.sp, dst[t0 + a:t0 + a + rows, c0:c0 + cw], ot[0:rows, 0:cw], reads=[bo], writes=[b_dst])
        if tn > TCH:
            break
    return b_dst


def ln_stage(k, nc, sb, x, h, T, g_ap, b_ap, out, tag, extra_out=None, b_in=()):
    ALPHA = (2 * DEPTH) ** 0.25
    n = lambda s: "%s_%s" % (tag, s)
    X = mybir.AxisListType.X
    gb = sb(n("g"), [128, D], F32); bb = sb(n("b"), [128, D], F32); b_c = Buf()
    k.dma(k.sp, gb[:], g_ap.partition_broadcast(128), writes=[b_c])
    k.dma(k.sp, bb[:], b_ap.partition_broadcast(128), writes=[b_c])
    xt = [sb(n("x%d" % i), [128, D], F32) for i in range(2)]; b_x = [Buf(), Buf()]
    ht = [sb(n("h%d" % i), [128, D], F32) for i in range(2)]; b_h = [Buf(), Buf()]
    sq = sb(n("sq"), [128, D], F32); b_sq = Buf()
    st = sb(n("st"), [128, 4], F32); b_st = Buf()
    b_out = Buf()
    for it, r0 in enumerate(range(0, T, 128)):
        rows = min(128, T - r0)
        R = slice(0, rows)
        xx, bx = xt[it % 2], b_x[it % 2]
        hh, bh = ht[it % 2], b_h[it % 2]
        k.dma(k.sp, xx[R, :], x[r0:r0 + rows, :], reads=list(b_in), writes=[bx])
        k.dma(k.sp, hh[R, :], h[r0:r0 + rows, :], reads=list(b_in), writes=[bh])
        k.op(k.dve, lambda e: e.scalar_tensor_tensor(out=xx[R, :], in0=xx[R, :], scalar=ALPHA, in1=hh[R, :], op0=ALU.mult, op1=ALU.add), reads=[bh], writes=[bx])
        k.op(k.dve, lambda e: e.tensor_reduce(out=st[R, 0:1], in_=xx[R, :], axis=X, op=ALU.add), reads=[bx], writes=[b_st])
        k.op(k.dve, lambda e: e.tensor_scalar(out=st[R, 0:1], in0=st[R, 0:1], scalar1=-1.0 / D, scalar2=None, op0=ALU.mult), writes=[b_st])
        k.op(k.dve, lambda e: e.tensor_scalar(out=xx[R, :], in0=xx[R, :], scalar1=st[R, 0:1], scalar2=None, op0=ALU.add), reads=[b_st], writes=[bx])
        k.op(k.pool, lambda e: e.tensor_tensor(out=sq[R, :], in0=xx[R, :], in1=xx[R, :], op=ALU.mult), reads=[bx], writes=[b_sq])
        k.op(k.dve, lambda e: e.tensor_reduce(out=st[R, 1:2], in_=sq[R, :], axis=X, op=ALU.add), reads=[b_sq], writes=[b_st])
        k.op(k.dve, lambda e: e.tensor_scalar(out=st[R, 1:2], in0=st[R, 1:2], scalar1=1.0 / D, scalar2=1e-5, op0=ALU.mult, op1=ALU.add), writes=[b_st])
        k.op(k.act, lambda e: e.activation(out=st[R, 1:2], in_=st[R, 1:2], func=AF.Sqrt), writes=[b_st])
        k.op(k.dve, lambda e: e.reciprocal(out=st[R, 1:2], in_=st[R, 1:2]), writes=[b_st])
        k.op(k.dve, lambda e: e.scalar_tensor_tensor(out=xx[R, :], in0=xx[R, :], scalar=st[R, 1:2], in1=gb[R, :], op0=ALU.mult, op1=ALU.mult), reads=[b_st, b_c], writes=[bx])
        k.op(k.dve, lambda e: e.tensor_tensor(out=xx[R, :], in0=xx[R, :], in1=bb[R, :], op=ALU.add), reads=[b_c], writes=[bx])
        k.dma(k.sp, out[r0:r0 + rows, :], xx[R, :], reads=[bx], writes=[b_out])
        if extra_out is not None:
            k.dma(k.sp, extra_out[r0:r0 + rows, :], xx[R, :], reads=[bx], writes=[b_out])
    return b_out
def peer_stage(k, nc, sb, ps, ident, b_ident, x, q, T, subkeys, uv_tab, f_out, tag, iota256, b_in=(), row_base=0):
    n = lambda s: "%s_%s" % (tag, s)
    X = mybir.AxisListType.X
    U32 = mybir.dt.uint32
    PS = [ps(n("ps%d" % i), [128, 512], F32) for i in range(8)]; bPS = [Buf() for _ in range(8)]
    skT = sb(n("skT"), [128, 16, 128], F32); b_sk = Buf()
    skin = sb(n("skin"), [128, 16, 128], F32); b_skin = Buf()
    k.dma(k.sp, skin[:], subkeys.rearrange("h c key d -> key (h c) d"), writes=[b_skin])
    for g in range(4):
        for j in range(4):
            k.op(k.pe, lambda e: e.transpose(PS[g % 2][:, j * 128:(j + 1) * 128], skin[:, g * 4 + j, :], ident[:, :]), reads=[b_skin, b_ident], writes=[bPS[g % 2]])
        k.op(k.dve, lambda e: e.tensor_copy(out=skT[:, g * 4:(g + 1) * 4, :], in_=PS[g % 2][:].rearrange("p (a c) -> p a c", a=4)), reads=[bPS[g % 2]], writes=[b_sk])
    qin = sb(n("qin"), [128, D], F32); b_qin = Buf()
    qT = sb(n("qT"), [128, 16, 128], F32); b_qT = Buf()
    s = sb(n("s"), [128, 16, 128], F32); b_s = Buf()
    s2 = sb(n("s2"), [128, 128], F32); b_s2 = Buf()
    hv = sb(n("hv"), [128, 16, 16], F32); b_hv = Buf()
    hiu = sb(n("hiu"), [128, 16, 16], U32); b_hiu = Buf()
    hi = sb(n("hi"), [128, 16, 16], F32); b_hi = Buf()
    cand = sb(n("cand"), [128, 256], F32); b_cand = Buf()
    cand2 = sb(n("cand2"), [128, 256], F32); b_cand2 = Buf()
    eg = sb(n("eg"), [128, 256], F32); b_eg = Buf()
    junk = sb(n("junk"), [128, 256], F32); b_junk = Buf()
    tv = sb(n("tv"), [128, 8, 16], F32); b_tv = Buf()
    posu = sb(n("posu"), [128, 8, 16], U32); b_posu = Buf()
    posf = sb(n("posf"), [128, 8, 16], F32); b_posf = Buf()
    iot = sb(n("iot"), [128, 256], F32); b_iot = Buf()
    k.dma(k.sp, iot[:], iota256, writes=[b_iot])
    ids = sb(n("ids"), [128, 128], F32); b_ids = Buf()
    gts = sb(n("gts"), [128, 8, 16], F32); b_gts = Buf()
    st = sb(n("st"), [128, 16], F32); b_st = Buf()
    idTs = [sb(n("idT%d" % i), [128, 128], I32) for i in range(2)]; b_idTs = [Buf(), Buf()]
    gT = sb(n("gT"), [128, 128], F32); b_gT = Buf()
    hTs = [sb(n("hT%d" % i), [128, 128], F32) for i in range(2)]; b_hTs = [Buf(), Buf()]
    xb = [sb(n("xb%d" % i), [128, D], F32) for i in range(2)]; b_xb = [Buf(), Buf()]
    big = sb(n("big"), [128, D], F32); b_big = Buf()
    wv = [sb(n("wv%d" % i), [128, 1], F32) for i in range(2)]; b_wv = [Buf(), Buf()]
    orow = [sb(n("orow%d" % i), [1, D], F32) for i in range(2)]; b_orow = [Buf(), Buf()]
    b_f = Buf()
    cnt = {"u": 0}
    UV = [sb(n("UV%d" % i), [128, 2 * D], F32) for i in range(2)]; b_UV = [Buf(), Buf()]
    wT = sb(n("wT"), [128, 128], F32)
    b_hc = [Buf(), Buf()]; b_wc = [Buf(), Buf()]

    def token(it, r0, t):
        idT, b_idT = idTs[it % 2], b_idTs[it % 2]
        hT = hTs[0]
        i2 = cnt["u"] % 2
        uv, bu = UV[i2], b_UV[i2]
        xx, bx = xb[i2], b_xb[i2]
        orw, bo = orow[i2], b_orow[i2]
        bh, bw = b_hc[i2], b_wc[i2]
        cnt["u"] += 1
        k.dma(k.sp, xx[:], x[r0 + t:r0 + t + 1, :].partition_broadcast(128), reads=list(b_in), writes=[bx])
        gather(k, uv[:], uv_tab, idT[:, t:t + 1], reads=[b_idT], writes=[bu])
        k.op(k.dve, lambda e: e.scalar_tensor_tensor(out=big[:], in0=uv[:, 0:D], scalar=1.0, in1=xx[:], op0=ALU.mult, op1=ALU.mult, accum_out=hT[:, t:t + 1]),
             reads=[bu, bx], writes=[b_big, bh])
        k.op(k.act, lambda e: e.activation(out=wT[:, t:t + 1], in_=hT[:, t:t + 1], func=AF.Gelu), reads=[bh], writes=[bw])
        k.op(k.dve, lambda e: e.tensor_tensor(out=wT[:, t:t + 1], in0=wT[:, t:t + 1], in1=gT[:, t:t + 1], op=ALU.mult), reads=[b_gT], writes=[bw])
        for c in range(4):
            k.op(k.pe, lambda e: e.matmul(PSrow(PS, 0, c), lhsT=wT[:, t:t + 1], rhs=uv[:, D + c * 512:D + (c + 1) * 512], start=True, stop=True),
                 reads=[bw, bu], writes=[bPS[4 + c]])
            k.op(k.act, lambda e: e.activation(out=orw[0:1, c * 512:(c + 1) * 512], in_=PSrow(PS, 0, c), func=AF.Copy), reads=[bPS[4 + c]], writes=[bo])
        k.dma(k.sp, f_out[r0 + t:r0 + t + 1, :], orw[0:1, :], reads=[bo], writes=[b_f])

    for it, r0 in enumerate(range(0, T, 128)):
        rows = min(128, T - r0)
        R = slice(0, rows)
        idT, b_idT = idTs[it % 2], b_idTs[it % 2]
        hT, b_hT = hTs[it % 2], b_hTs[it % 2]
        k.dma(k.sp, qin[R, :], q[r0:r0 + rows, :], reads=list(b_in), writes=[b_qin])
        for g in range(4):
            for j in range(4):
                kc = g * 4 + j
                k.op(k.pe, lambda e: e.transpose(PS[g % 2][:, j * 128:j * 128 + rows], qin[R, kc * 128:(kc + 1) * 128], ident[R, R]), reads=[b_qin, b_ident], writes=[bPS[g % 2]])
            k.op(k.act, lambda e: e.activation(out=qT[:, g * 4:(g + 1) * 4, R], in_=PS[g % 2][:].rearrange("p (a c) -> p a c", a=4)[:, :, R], func=AF.Copy), reads=[bPS[g % 2]], writes=[b_qT])
        for g in range(4):
            for j in range(4):
                hc = g * 4 + j
                k.op(k.pe, lambda e: e.matmul(PS[2 + g % 2][R, j * 128:(j + 1) * 128], lhsT=qT[:, hc, R], rhs=skT[:, hc, :], start=True, stop=True), reads=[b_qT, b_sk], writes=[bPS[2 + g % 2]])
            k.op(k.dve, lambda e: e.tensor_copy(out=s[R, g * 4:(g + 1) * 4, :], in_=PS[2 + g % 2][R, :].rearrange("p (a c) -> p a c", a=4)), reads=[bPS[2 + g % 2]], writes=[b_s])
        for hc in range(16):
            k.op(k.dve, lambda e: e.max(out=hv[R, hc, 0:8], in_=s[R, hc, :]), reads=[b_s], writes=[b_hv])
            k.op(k.dve, lambda e: e.max_index(out=hiu[R, hc, 0:8], in_max=hv[R, hc, 0:8], in_values=s[R, hc, :]), reads=[b_s, b_hv], writes=[b_hiu])
            k.op(k.dve, lambda e: e.match_replace(out=s2[R, :], in_to_replace=hv[R, hc, 0:8], in_values=s[R, hc, :], imm_value=-1.0e30), reads=[b_s, b_hv], writes=[b_s2])
            k.op(k.dve, lambda e: e.max(out=hv[R, hc, 8:16], in_=s2[R, :]), reads=[b_s2], writes=[b_hv])
            k.op(k.dve, lambda e: e.max_index(out=hiu[R, hc, 8:16], in_max=hv[R, hc, 8:16], in_values=s2[R, :]), reads=[b_s2, b_hv], writes=[b_hiu])
        k.op(k.dve, lambda e: e.tensor_copy(out=hi[R], in_=hiu[R]), reads=[b_hiu], writes=[b_hi])
        for h in range(8):
            c3 = cand[R, :].rearrange("p (a b) -> p a b", a=16)
            k.op(k.dve, lambda e: e.tensor_tensor(out=c3, in0=hv[R, 2 * h, :].unsqueeze(2).to_broadcast([rows, 16, 16]),
                                                  in1=hv[R, 2 * h + 1, :].unsqueeze(1).to_broadcast([rows, 16, 16]), op=ALU.add), reads=[b_hv], writes=[b_cand])
            k.op(k.dve, lambda e: e.scalar_tensor_tensor(out=eg[R, :].rearrange("p (a b) -> p a b", a=16), in0=hi[R, 2 * h, :].unsqueeze(2).to_broadcast([rows, 16, 16]), scalar=128.0,
                                                         in1=hi[R, 2 * h + 1, :].unsqueeze(1).to_broadcast([rows, 16, 16]), op0=ALU.mult, op1=ALU.add), reads=[b_hi], writes=[b_eg])
            k.op(k.dve, lambda e: e.max(out=tv[R, h, 0:8], in_=cand[R, :]), reads=[b_cand], writes=[b_tv])
            k.op(k.dve, lambda e: e.max_index(out=posu[R, h, 0:8], in_max=tv[R, h, 0:8], in_values=cand[R, :]), reads=[b_cand, b_tv], writes=[b_posu])
            k.op(k.dve, lambda e: e.match_replace(out=cand2[R, :], in_to_replace=tv[R, h, 0:8], in_values=cand[R, :], imm_value=-1.0e30), reads=[b_cand, b_tv], writes=[b_cand2])
            k.op(k.dve, lambda e: e.max(out=tv[R, h, 8:16], in_=cand2[R, :]), reads=[b_cand2], writes=[b_tv])
            k.op(k.dve, lambda e: e.max_index(out=posu[R, h, 8:16], in_max=tv[R, h, 8:16], in_values=cand2[R, :]), reads=[b_cand2, b_tv], writes=[b_posu])
            k.op(k.dve, lambda e: e.tensor_copy(out=posf[R, h, :], in_=posu[R, h, :]), reads=[b_posu], writes=[b_posf])
            for kk_ in range(16):
                k.op(k.dve, lambda e: e.scalar_tensor_tensor(out=junk[R, :], in0=iot[R, :], scalar=posf[R, h, kk_:kk_ + 1], in1=eg[R, :], op0=ALU.is_equal, op1=ALU.mult,
                                                             accum_out=ids[R, h * 16 + kk_:h * 16 + kk_ + 1]), reads=[b_iot, b_posf, b_eg], writes=[b_junk, b_ids])
        k.op(k.dve, lambda e: e.tensor_tensor(out=gts[R], in0=tv[R], in1=tv[R, :, 0:1].to_broadcast([rows, 8, 16]), op=ALU.subtract), reads=[b_tv], writes=[b_gts])
        k.op(k.act, lambda e: e.activation(out=gts[R], in_=gts[R], func=AF.Exp), writes=[b_gts])
        k.op(k.dve, lambda e: e.tensor_reduce(out=st[R, 0:8], in_=gts[R], axis=X, op=ALU.add), reads=[b_gts], writes=[b_st])
        k.op(k.dve, lambda e: e.reciprocal(out=st[R, 0:8], in_=st[R, 0:8]), writes=[b_st])
        k.op(k.dve, lambda e: e.tensor_tensor(out=gts[R], in0=gts[R], in1=st[R, 0:8].unsqueeze(2).to_broadcast([rows, 8, 16]), op=ALU.mult), reads=[b_st], writes=[b_gts])
        k.op(k.pe, lambda e: e.transpose(PS[4][:, 0:rows], ids[R, :], ident[R, R]), reads=[b_ids, b_ident], writes=[bPS[4]])
        k.op(k.dve, lambda e: e.tensor_scalar(out=idT[:, R], in0=PS[4][:, 0:rows], scalar1=float(row_base), scalar2=None, op0=ALU.add), reads=[bPS[4]], writes=[b_idT])
        k.op(k.pe, lambda e: e.transpose(PS[5][:, 0:rows], gts[R].rearrange("p a b -> p (a b)"), ident[R, R]), reads=[b_gts, b_ident], writes=[bPS[5]])
        k.op(k.act, lambda e: e.activation(out=gT[:, R], in_=PS[5][:, 0:rows], func=AF.Copy), reads=[bPS[5]], writes=[b_gT])
        for t in range(rows):
            token(it, r0, t)
    return b_f


def PSrow(PS, tok, c):
    return PS[4 + c][0:1, 0:512]


def gather(k, out_ap, table, idx_ap, reads, writes):
    Q = k.pool
    ds = k.dsems[k.dnext]
    k.dnext = (k.dnext + 1) % len(k.dsems)
    toks = k._deps(reads, writes)
    if ds.val:
        toks.append((ds.h, ds.val, ds.key))
    last = k._waits(Q, toks)
    inst = Q.eng.indirect_dma_start(out=out_ap, out_offset=None, in_=table, in_offset=bass.IndirectOffsetOnAxis(ap=idx_ap, axis=0))
    if last is not None:
        inst.wait_op(last[0], last[1], "sem-ge")
    ds.val += 16
    inst.then_inc(ds.h, 16)
    k._mark((ds.h, ds.val, ds.key), reads, writes)
    return inst
_UID = itertools.count()


def nsas_host_consts(P):
    nsel = P // 64 + 1
    r = np.arange(16)
    tok = r % 4
    sel16 = (tok[:, None] == np.arange(4)[None, :]).astype(np.float32)
    biasN = np.where(np.arange(4)[None, :] <= tok[:, None], 0.0, NEG).astype(np.float32)
    biasW = np.zeros((16, 516), np.float32)
    biasW[:, 0:512] = np.where(np.arange(512)[None, :] >= tok[:, None] + 1, 0.0, NEG)
    biasW[:, 512:516] = biasN
    blk = np.arange(nsel)
    forced = (blk == 0) | (blk == nsel - 1) | (blk == nsel - 2)
    FM = np.tile(np.where(forced, 0.0, 1.0).astype(np.float32)[None, :], (4, 1))
    FA = np.tile(np.where(forced, 1.0e4, 0.0).astype(np.float32)[None, :], (4, 1))
    return {"ns_iota": np.arange(128, dtype=np.float32)[:, None].copy(), "ns_sel16": sel16, "ns_selT": sel16.T.copy(),
            "ns_biasN": biasN, "ns_biasW": biasW, "ns_FM": FM, "ns_FA": FA}


def nsa_sample_stage(k, nc, sb, ps, ident, b_ident, proj, row0, P, cmp_w, cache_cmp, cache_sel, page_row, swin, CN, CS,
                     past_cmp, past_sel, osc, ymix, tag, b_in=(), row_base=0):
    NP = P // 128
    NB = P // 32
    NSEL = P // 64 + 1
    X = mybir.AxisListType.X
    n = lambda s: "%s_%s" % (tag, s)
    PS = [ps(n("ps%d" % i), [128, 512], F32) for i in range(8)]; bPS = [Buf() for _ in range(8)]
    b_c = Buf()
    def cload(name, shape, src, dt=F32):
        t = sb(n(name), shape, dt)
        k.dma(k.sp, t[:], src, writes=[b_c])
        return t
    iota = cload("iota", [128, 1], CS["ns_iota"]); mask4 = cload("mask4", [128, 4], CN["nsa_mask4"])
    sel16 = cload("sel16", [16, 4], CS["ns_sel16"]); selT = cload("selT", [4, 16], CS["ns_selT"])
    biasN = cload("biasN", [16, 4], CS["ns_biasN"]); biasW = cload("biasW", [16, 516], CS["ns_biasW"])
    FM = cload("FM", [4, NSEL], CS["ns_FM"]); FA = cload("FA", [4, NSEL], CS["ns_FA"])
    wcol = sb(n("wcol"), [128, 2], F32); b_wcol = Buf()
    for c in range(2):
        for j in range(4):
            k.dma(k.sp, wcol[j * 32:(j + 1) * 32, c:c + 1], cmp_w[c:c + 1, :].rearrange("o l -> l o"), writes=[b_wcol])
    Wb = sb(n("Wb"), [128, 4], F32); Ww = sb(n("Ww"), [128, 252], F32); b_W = Buf()
    k.op(k.dve, lambda e: e.memset(Ww[:], 0.0), writes=[b_W])
    k.op(k.dve, lambda e: e.tensor_scalar(out=Wb[:], in0=mask4[:], scalar1=wcol[:, 0:1], scalar2=None, op0=ALU.mult), reads=[b_c, b_wcol], writes=[b_W])
    k.op(k.dve, lambda e: e.tensor_scalar(out=Ww[:, 124:128], in0=mask4[:], scalar1=wcol[:, 1:2], scalar2=None, op0=ALU.mult), reads=[b_c, b_wcol], writes=[b_W])
    ptb = sb(n("ptb"), [128, NP], I32); ptf = sb(n("ptf"), [128, NP], F32); idx = sb(n("idx"), [128, NP], I32); b_idx = Buf()
    k.dma(k.sp, ptb[:], page_row.partition_broadcast(128), writes=[b_idx])
    k.op(k.dve, lambda e: e.tensor_copy(out=ptf[:], in_=ptb[:]), writes=[b_idx])
    k.op(k.dve, lambda e: e.tensor_scalar(out=ptf[:], in0=ptf[:], scalar1=128.0, scalar2=iota[:, 0:1], op0=ALU.mult, op1=ALU.add), reads=[b_c], writes=[b_idx])
    k.op(k.dve, lambda e: e.tensor_scalar(out=idx[:], in0=ptf[:], scalar1=float(row_base), scalar2=None, op0=ALU.add), writes=[b_idx])
    qin = sb(n("qin"), [T_S, 1024], F32); b_qin = Buf()
    kvn = sb(n("kvn"), [T_S, 1536], F32); b_kvn = Buf()
    qT = sb(n("qT"), [64, 4, 16], F32); b_qT = Buf()
    kTn = sb(n("kTn"), [64, 2, 4, T_S], F32); b_kTn = Buf()
    st = sb(n("st"), [16, 4], F32); b_st = Buf()
    impv = sb(n("impv"), [4, NSEL], F32); impw = sb(n("impw"), [4, NSEL], F32); b_imp = Buf()
    mx = sb(n("mx"), [4, 16], F32); b_mx = Buf()
    s01 = sb(n("s01"), [4, NSEL], F32); b_s01 = Buf()
    selb = sb(n("selb"), [16, 4, NSEL], F32); b_selb = Buf()
    eT = [sb(n("eT%d" % i), [128, 16], F32) for i in range(2)]; b_eT = [Buf(), Buf()]
    ob = sb(n("ob"), [16, 64], F32); b_ob = Buf()
    oc = sb(n("oc"), [T_S, 3, 16, 64], F32); b_oc = Buf()
    gin = sb(n("gin"), [T_S, 48], F32); b_g = Buf()
    yo = sb(n("yo"), [T_S, 1024], F32); b_yo = Buf()
    phase = contextlib.ExitStack()
    sbx = lambda name, shape, dt: phase.enter_context(nc.sbuf_tensor(n(name) + "_x%d" % next(_UID), shape, dt))
    pg = [sbx("pg%d" % i, [128, 512], F32) for i in range(4)]; b_pg = [Buf() for _ in range(4)]
    b_past = Buf()
    for s_ in range(NP):
        for ci, (cache, past) in enumerate(((cache_cmp, past_cmp), (cache_sel, past_sel))):
            t_, bt = pg[(2 * s_ + ci) % 4], b_pg[(2 * s_ + ci) % 4]
            gather(k, t_[:], cache, idx[:, s_:s_ + 1], reads=[b_idx], writes=[bt])
            k.dma(k.sp, past[s_ * 128:(s_ + 1) * 128, :], t_[:], reads=[bt], writes=[b_past])
    k.barrier()
    phase.close()
    phase = contextlib.ExitStack()
    k.dma(k.sp, qin[:], proj[row0:row0 + T_S, NSA0:NSA0 + 1024], reads=list(b_in), writes=[b_qin])
    k.dma(k.sp, kvn[:], proj[row0:row0 + T_S, KV0:KV0 + 1536], reads=list(b_in), writes=[b_kvn])
    for h in range(16):
        k.op(k.pe, lambda e: e.transpose(PS[0][0:64, h * 4:h * 4 + 4], qin[:, h * 64:(h + 1) * 64], ident[0:T_S, 0:T_S]), reads=[b_qin, b_ident], writes=[bPS[0]])
    k.op(k.act, lambda e: e.activation(out=qT[:].rearrange("p a b -> p (a b)"), in_=PS[0][0:64, 0:64], func=AF.Copy, scale=0.125), reads=[bPS[0]], writes=[b_qT])
    for bi, off in enumerate((512, 1024)):
        for kv in range(4):
            k.op(k.pe, lambda e: e.transpose(PS[1][0:64, (bi * 4 + kv) * 4:(bi * 4 + kv) * 4 + 4], kvn[:, off + kv * 64:off + (kv + 1) * 64], ident[0:T_S, 0:T_S]),
                 reads=[b_kvn, b_ident], writes=[bPS[1]])
    k.op(k.dve, lambda e: e.tensor_copy(out=kTn[:].rearrange("p a b c -> p (a b c)"), in_=PS[1][0:64, 0:32]), reads=[bPS[1]], writes=[b_kTn])
    kcbT = sbx("kcbT", [64, 4, NB], F32); b_kcb = Buf()
    nvt = (NB + 127) // 128
    vcb = sbx("vcb", [128, nvt, 256], F32); b_vcb = Buf()
    pin = [sbx("pin%d" % i, [128, 512], F32) for i in range(2)]; b_pin = [Buf(), Buf()]
    for s_ in range(NP):
        pt_, bp = pin[s_ % 2], b_pin[s_ % 2]
        k.dma(k.sp, pt_[:], past_cmp[s_ * 128:(s_ + 1) * 128, :], reads=[b_past], writes=[bp])
        for kv in range(4):
            k.op(k.pe, lambda e: e.matmul(PS[kv][0:64, 4 * s_:4 * s_ + 4], lhsT=pt_[:, kv * 64:(kv + 1) * 64], rhs=Wb[:, :], start=True, stop=True, skip_group_check=True),
                 reads=[bp, b_W], writes=[bPS[kv]])
        vt_, off = s_ // 32, 4 * (s_ % 32)
        k.op(k.pe, lambda e: e.matmul(PS[4 + vt_ % 4][:, 0:256], lhsT=Ww[:, 124 - off:252 - off], rhs=pt_[:, 256:512], start=(s_ % 32 == 0), stop=(s_ % 32 == 31 or s_ == NP - 1)),
             reads=[bp, b_W], writes=[bPS[4 + vt_ % 4]])
    for kv in range(4):
        k.op(k.dve, lambda e: e.tensor_copy(out=kcbT[:, kv, :], in_=PS[kv][0:64, 0:NB]), reads=[bPS[kv]], writes=[b_kcb])
    for vt_ in range(nvt):
        k.op(k.act, lambda e: e.activation(out=vcb[:, vt_, :], in_=PS[4 + vt_ % 4][:, 0:256], func=AF.Copy), reads=[bPS[4 + vt_ % 4]], writes=[b_vcb])
    wst = sbx("wst", [128, 4, 512], F32); b_wst = Buf()
    k.dma(k.sp, wst[:], swin.rearrange("(a p) c -> p a c", p=128), writes=[b_wst])
    KTw = sbx("KTw", [64, 4, 516], F32); b_KTw = Buf()
    for a in range(4):
        for kv in range(4):
            k.op(k.pe, lambda e: e.transpose(PS[a % 2][0:64, kv * 128:(kv + 1) * 128], wst[:, a, kv * 64:(kv + 1) * 64], ident[:, :]), reads=[b_wst, b_ident], writes=[bPS[a % 2]])
        k.op(k.dve, lambda e: e.tensor_copy(out=KTw[:, :, a * 128:(a + 1) * 128], in_=PS[a % 2][0:64, :].rearrange("p (a c) -> p a c", a=4)), reads=[bPS[a % 2]], writes=[b_KTw])
    k.op(k.dve, lambda e: e.tensor_copy(out=KTw[:, :, 512:516], in_=kTn[:, 1, :, :]), reads=[b_kTn], writes=[b_KTw])
    scB = sbx("scB", [16, max(NB, 516)], F32)
    sc = scB; b_sc = Buf()
    pc = sbx("pc", [16, NB], F32); b_pc = Buf()
    b_osc = Buf()
    net = [0]

    def softmax_pv(ncols, vtiles, slot, kv):
        k.op(k.dve, lambda e: e.tensor_reduce(out=st[:, 0:1], in_=sc[:, 0:ncols], axis=X, op=ALU.max), reads=[b_sc], writes=[b_st])
        k.op(k.dve, lambda e: e.tensor_scalar(out=st[:, 0:1], in0=st[:, 0:1], scalar1=-1.0e20, scalar2=-1.0, op0=ALU.max, op1=ALU.mult), writes=[b_st])
        k.op(k.act, lambda e: e.activation(out=sc[:, 0:ncols], in_=sc[:, 0:ncols], func=AF.Exp, bias=st[:, 0:1], scale=1.0), reads=[b_st], writes=[b_sc])
        k.op(k.dve, lambda e: e.tensor_reduce(out=st[:, 1:2], in_=sc[:, 0:ncols], axis=X, op=ALU.add), reads=[b_sc], writes=[b_st])
        k.op(k.dve, lambda e: e.tensor_scalar(out=st[:, 1:2], in0=st[:, 1:2], scalar1=1.0e-30, scalar2=None, op0=ALU.max), writes=[b_st])
        k.op(k.dve, lambda e: e.reciprocal(out=st[:, 1:2], in_=st[:, 1:2]), writes=[b_st])
        for vi, (c0, w, vap, bv) in enumerate(vtiles):
            pt, bpt = PS[4 + net[0] % 2], bPS[4 + net[0] % 2]
            et, bet = eT[net[0] % 2], b_eT[net[0] % 2]
            net[0] += 1
            k.op(k.pe, lambda e: e.transpose(pt[0:w, 0:16], sc[:, c0:c0 + w], ident[0:16, 0:16]), reads=[b_sc, b_ident], writes=[bpt])
            k.op(k.act, lambda e: e.activation(out=et[0:w, :], in_=pt[0:w, 0:16], func=AF.Copy), reads=[bpt], writes=[bet])
            k.op(k.pe, lambda e: e.matmul(PS[6][0:16, 0:64], lhsT=et[0:w, :], rhs=vap, start=(vi == 0), stop=(vi == len(vtiles) - 1)),
                 reads=[bet, bv], writes=[bPS[6]])
        k.op(k.dve, lambda e: e.tensor_scalar(out=ob[:], in0=PS[6][0:16, 0:64], scalar1=st[:, 1:2], scalar2=None, op0=ALU.mult), reads=[bPS[6], b_st], writes=[b_ob])
        for g_ in range(4):
            k.dma(k.sp, osc[slot, :, kv * 4 + g_, :], ob[g_ * 4:(g_ + 1) * 4, :], reads=[b_ob], writes=[b_osc])

    for kv in range(4):
        q_kv = qT[:, kv, :]
        for c0 in range(0, NB, 512):
            w = min(512, NB - c0)
            k.op(k.pe, lambda e: e.matmul(PS[0][0:16, 0:w], lhsT=q_kv, rhs=kcbT[:, kv, c0:c0 + w], start=True, stop=True), reads=[b_qT, b_kcb], writes=[bPS[0]])
            k.op(k.act, lambda e: e.activation(out=sc[:, c0:c0 + w], in_=PS[0][0:16, 0:w], func=AF.Copy), reads=[bPS[0]], writes=[b_sc])
        softmax_pv(NB, [(j * 128, min(128, NB - j * 128), vcb[0:min(128, NB - j * 128), j, kv * 64:(kv + 1) * 64], b_vcb) for j in range(nvt)], 0, kv)
        k.op(k.dve, lambda e: e.tensor_scalar(out=pc[:], in0=sc[:, 0:NB], scalar1=st[:, 1:2], scalar2=None, op0=ALU.mult), reads=[b_sc, b_st], writes=[b_pc])
        for c0 in range(0, NB, 512):
            w = min(512, NB - c0)
            k.op(k.pe, lambda e: e.matmul(PS[1][0:4, 0:w], lhsT=sel16[:, :], rhs=pc[:, c0:c0 + w], start=True, stop=True), reads=[b_pc, b_c], writes=[bPS[1]])
            k.op(k.dve, lambda e: e.tensor_reduce(out=impv[:, c0 // 2:(c0 + w) // 2], in_=PS[1][0:4, 0:w].rearrange("p (a c) -> p a c", c=2), axis=X, op=ALU.add), reads=[bPS[1]], writes=[b_imp])
        k.op(k.dve, lambda e: e.memset(impv[:, NSEL - 1:NSEL], 0.0), writes=[b_imp])
        k.op(k.dve, lambda e: e.tensor_tensor(out=impv[:], in0=impv[:], in1=FM[:], op=ALU.mult), reads=[b_c], writes=[b_imp])
        k.op(k.dve, lambda e: e.tensor_tensor(out=impv[:], in0=impv[:], in1=FA[:], op=ALU.add), reads=[b_c], writes=[b_imp])
        k.op(k.dve, lambda e: e.max(out=mx[:, 0:8], in_=impv[:]), reads=[b_imp], writes=[b_mx])
        k.op(k.dve, lambda e: e.match_replace(out=impw[:], in_to_replace=mx[:, 0:8], in_values=impv[:], imm_value=-2.0), reads=[b_mx], writes=[b_imp])
        k.op(k.dve, lambda e: e.max(out=mx[:, 8:16], in_=impw[:]), reads=[b_imp], writes=[b_mx])
        k.op(k.dve, lambda e: e.tensor_scalar(out=mx[:, 15:16], in0=mx[:, 15:16], scalar1=0.0, scalar2=None, op0=ALU.max), writes=[b_mx])
        k.op(k.dve, lambda e: e.tensor_scalar(out=s01[:], in0=impv[:], scalar1=mx[:, 15:16], scalar2=None, op0=ALU.is_ge), reads=[b_imp, b_mx], writes=[b_s01])
        k.op(k.pe, lambda e: e.matmul(PS[1][0:16, 0:NSEL], lhsT=selT[:, :], rhs=s01[:, :], start=True, stop=True), reads=[b_s01, b_c], writes=[bPS[1]])
        k.op(k.dve, lambda e: e.tensor_scalar(out=selb[:, kv, :], in0=PS[1][0:16, 0:NSEL], scalar1=-1.0, scalar2=1.0e30, op0=ALU.add, op1=ALU.mult), reads=[bPS[1]], writes=[b_selb])
        for c0 in range(0, 516, 512):
            w = min(512, 516 - c0)
            k.op(k.pe, lambda e: e.matmul(PS[0][0:16, 0:w], lhsT=q_kv, rhs=KTw[:, kv, c0:c0 + w], start=True, stop=True), reads=[b_qT, b_KTw], writes=[bPS[0]])
            k.op(k.dve, lambda e: e.tensor_tensor(out=sc[:, c0:c0 + w], in0=PS[0][0:16, 0:w], in1=biasW[:, c0:c0 + w], op=ALU.add), reads=[bPS[0], b_c], writes=[b_sc])
        vt = [(a * 128, 128, wst[:, a, 256 + kv * 64:256 + (kv + 1) * 64], b_wst) for a in range(4)] + [(512, T_S, kvn[:, 1280 + kv * 64:1280 + (kv + 1) * 64], b_kvn)]
        softmax_pv(516, vt, 2, kv)
    k.barrier()
    phase.close()
    phase = contextlib.ExitStack()
    KTs = sbx("KTs", [64, P + T_S], F32); b_KTs = Buf()
    Vsp = sbx("Vsp", [128, NP, 64], F32); b_Vsp = Buf()
    scC = sbx("scC", [16, P + 64], F32)
    sc = scC; b_sc = Buf()
    pin = [sbx("pinC%d" % i, [128, 64], F32) for i in range(2)]; b_pin = [Buf(), Buf()]
    for kv in range(4):
        q_kv = qT[:, kv, :]
        for s_ in range(NP):
            pt_, bp = pin[s_ % 2], b_pin[s_ % 2]
            k.dma(k.sp, pt_[:, 0:64], past_sel[s_ * 128:(s_ + 1) * 128, kv * 64:(kv + 1) * 64], reads=[b_past], writes=[bp])
            k.dma(k.sp, Vsp[:, s_, :], past_sel[s_ * 128:(s_ + 1) * 128, 256 + kv * 64:256 + (kv + 1) * 64], reads=[b_past], writes=[b_Vsp])
            pp_, bpp = PS[2 + (s_ // 4) % 2], bPS[2 + (s_ // 4) % 2]
            k.op(k.pe, lambda e: e.transpose(pp_[0:64, (s_ % 4) * 128:(s_ % 4 + 1) * 128], pt_[:, 0:64], ident[:, :]), reads=[bp, b_ident], writes=[bpp])
            if s_ % 4 == 3 or s_ == NP - 1:
                a0 = (s_ // 4) * 4
                wd = (s_ - a0 + 1) * 128
                k.op(k.dve, lambda e: e.tensor_copy(out=KTs[:, a0 * 128:a0 * 128 + wd], in_=pp_[0:64, 0:wd]), reads=[bpp], writes=[b_KTs])
        k.op(k.dve, lambda e: e.tensor_copy(out=KTs[:, P:P + T_S], in_=kTn[:, 0, kv, :]), reads=[b_kTn], writes=[b_KTs])
        for c0 in range(0, P, 512):
            w = min(512, P - c0)
            pt, bpt = PS[(c0 // 512) % 2], bPS[(c0 // 512) % 2]
            k.op(k.pe, lambda e: e.matmul(pt[0:16, 0:w], lhsT=q_kv, rhs=KTs[:, c0:c0 + w], start=True, stop=True), reads=[b_qT, b_KTs], writes=[bpt])
            nb = w // 64
            k.op(k.dve, lambda e: e.tensor_tensor(out=sc[:, c0:c0 + w].rearrange("p (a c) -> p a c", c=64), in0=pt[0:16, 0:w].rearrange("p (a c) -> p a c", c=64),
                                                  in1=selb[:, kv, c0 // 64:c0 // 64 + nb].unsqueeze(2).to_broadcast([16, nb, 64]), op=ALU.add), reads=[bpt, b_selb], writes=[b_sc])
        k.op(k.pe, lambda e: e.matmul(PS[0][0:16, 0:T_S], lhsT=q_kv, rhs=KTs[:, P:P + T_S], start=True, stop=True), reads=[b_qT, b_KTs], writes=[bPS[0]])
        k.op(k.dve, lambda e: e.tensor_tensor(out=sc[:, P:P + T_S], in0=PS[0][0:16, 0:T_S], in1=biasN[:, :], op=ALU.add), reads=[bPS[0], b_c], writes=[b_sc])
        vt = [(s_ * 128, 128, Vsp[:, s_, :], b_Vsp) for s_ in range(NP)] + [(P, T_S, kvn[:, 768 + kv * 64:768 + (kv + 1) * 64], b_kvn)]
        softmax_pv(P + T_S, vt, 1, kv)
    k.barrier()
    phase.close()
    k.dma(k.sp, oc[:], osc.rearrange("s t h c -> t s h c"), reads=[b_osc], writes=[b_oc])
    k.dma(k.sp, gin[:], proj[row0:row0 + T_S, KV0 + 1536:KV0 + 1584], reads=list(b_in), writes=[b_g])
    k.op(k.act, lambda e: e.activation(out=gin[:], in_=gin[:], func=AF.Sigmoid), writes=[b_g])
    g3 = gin[:].rearrange("p (h c) -> p h c", c=3)
    y3 = yo[:].rearrange("p (h c) -> p h c", h=16)
    k.op(k.dve, lambda e: e.tensor_tensor(out=y3, in0=oc[:, 0, :, :], in1=g3[:, :, 0:1].to_broadcast([T_S, 16, 64]), op=ALU.mult), reads=[b_oc, b_g], writes=[b_yo])
    for s_ in (1, 2):
        k.op(k.dve, lambda e: e.tensor_tensor(out=oc[:, s_, :, :], in0=oc[:, s_, :, :], in1=g3[:, :, s_:s_ + 1].to_broadcast([T_S, 16, 64]), op=ALU.mult), reads=[b_g], writes=[b_oc])
        k.op(k.dve, lambda e: e.tensor_tensor(out=y3, in0=y3, in1=oc[:, s_, :, :], op=ALU.add), reads=[b_oc], writes=[b_yo])
    b_y = Buf()
    k.dma(k.sp, ymix[:, 1024:2048], yo[:], reads=[b_yo], writes=[b_y])
    return b_y
NCU = 4
RW_NAMES = ["rwkv_mu", "rwkv_w0", "rwkv_w2", "rwkv_a0", "rwkv_a2", "rwkv_g2", "rwkv_k_k", "rwkv_k_a", "rwkv_r_k", "rwkv_gn_g", "rwkv_gn_b"]


def build_nc(T_Pc=T_P, P=16384, n_pool=1280, NS=2):
    nc = bass.Bass("TRN2", target_bir_lowering=False)
    NP = P // 128
    NTOK = T_Pc + NS * T_S
    CNH = nsa_host_consts()
    CSH = nsas_host_consts(P)
    din = lambda name, shape, d=F32: nc.dram_tensor(name, list(shape), d, kind="ExternalInput").ap()
    dout = lambda name, shape: nc.dram_tensor(name, list(shape), F32, kind="ExternalOutput").ap()
    dscr = lambda name, shape: nc.dram_tensor(name, list(shape), F32, kind="Internal").ap()
    xp = din("xp", [T_Pc, D]); xs = din("xs", [NS * T_S, D])
    w_in = din("w_in", [DEPTH, D, IN_COLS]); w_out = din("w_out", [DEPTH, D, D]); wq = din("wq", [DEPTH, D, D])
    skeys = din("skeys", [DEPTH, 8, 2, 128, 128]); puv = din("puv", [DEPTH, 16384, 2, D])
    lnp = {nm: din(nm, [DEPTH, 1, D]) for nm in ("ln1_g", "ln1_b", "ln2_g", "ln2_b")}
    rwp = {}
    for nm in RW_NAMES:
        shp = {"rwkv_mu": [1, RW], "rwkv_w2": [64, 1024], "rwkv_a2": [64, 1024], "rwkv_g2": [160, 1024]}.get(nm, [1, 1024])
        rwp[nm] = din(nm, [DEPTH] + shp)
    cmpw = din("cmp_w", [DEPTH, 2, 32])
    cc = din("cc", [DEPTH, n_pool * 128, 512]); cs = din("cs", [DEPTH, n_pool * 128, 512])
    ptab = din("ptab", [NS, 1, NP], I32)
    swin = din("swin", [DEPTH, NS, 512, 512]); srw = din("srw", [DEPTH, NS, 16, 64, 64]); ssh = din("ssh", [DEPTH, NS, 1, RW])
    zrow = din("zrow", [1, RW]); ident_d = din("ident", [128, 128]); iota256 = din("iota256", [128, 256])
    CN = {n_: din(n_, v.shape) for n_, v in CNH.items()}
    CS = {n_: din(n_, v.shape) for n_, v in CSH.items()}
    y_p = dout("y_p", [T_Pc, D]); y_s = dout("y_s", [NS * T_S, D])
    cmp_p = dout("cmp_p", [DEPTH, T_Pc, 512]); sel_p = dout("sel_p", [DEPTH, T_Pc, 512]); win_p = dout("win_p", [DEPTH, 512, 512])
    rw_p = dout("rw_p", [DEPTH, 16, 64, 64]); sh_p = dout("sh_p", [DEPTH, 1, RW])
    cmp_s = dout("cmp_s", [DEPTH, NS, T_S, 512]); sel_s = dout("sel_s", [DEPTH, NS, T_S, 512]); win_s = dout("win_s", [DEPTH, NS, 512, 512])
    rw_s = dout("rw_s", [DEPTH, NS, 16, 64, 64]); sh_s = dout("sh_s", [DEPTH, NS, 1, RW])
    Xa = dscr("Xa", [NTOK, D]); X1 = dscr("X1", [NTOK, D]); Hs = dscr("Hs", [NTOK, D]); Qs = dscr("Qs", [NTOK, D]); Fs = dscr("Fs", [NTOK, D])
    proj = dscr("proj", [NTOK, IN_COLS]); ymix = dscr("ymix", [NTOK, D]); opsd = dscr("opsd", [max(T_Pc, T_S), 5, 2, 512])
    pcmp = dscr("pcmp", [P, 512]); psel = dscr("psel", [P, 512]); osc = dscr("osc", [3, T_S, 16, 64])

    with contextlib.ExitStack() as ctx:
        k = K(ctx, nc)
        ident = ctx.enter_context(nc.sbuf_tensor("ident_sb", [128, 128], F32)); b_ident = Buf()
        k.dma(k.sp, ident[:], ident_d[:, :], writes=[b_ident])

        uid = [0]

        class Scope:
            def __enter__(self):
                self.st = contextlib.ExitStack()
                def uniq(name):
                    uid[0] += 1
                    return "%s_u%d" % (name, uid[0])
                self.sb = lambda name, shape, d: self.st.enter_context(nc.sbuf_tensor(uniq(name), shape, d))
                self.ps = lambda name, shape, d: self.st.enter_context(nc.psum_tensor(uniq(name), shape, d))
                return self

            def __exit__(self, *a):
                if a[0] is not None:
                    return False
                k.barrier()
                self.st.close()

        for r0 in range(0, T_Pc, 512):
            k.dma(k.sp, Xa[r0:r0 + 512, :], xp[r0:r0 + 512, :])
        k.dma(k.sp, Xa[T_Pc:NTOK, :], xs[:, :])
        k.barrier()
        for l in range(DEPTH):
            L = "l%d" % l
            with Scope() as s:
                linear_stage(k, nc, s.sb, s.ps, ident, b_ident, Xa, NTOK, w_in[l], IN_COLS, proj, L + "li")
            for r0 in range(0, T_Pc, 512):
                k.dma(k.sp, cmp_p[l, r0:r0 + 512, :], proj[r0:r0 + 512, KV0:KV0 + 512])
                k.dma(k.sp, sel_p[l, r0:r0 + 512, :], proj[r0:r0 + 512, KV0 + 512:KV0 + 1024])
            k.dma(k.sp, win_p[l, :, :], proj[T_Pc - 512:T_Pc, KV0 + 1024:KV0 + 1536])
            k.dma(k.sp, sh_p[l, :, :], proj[T_Pc - 1:T_Pc, 0:RW])
            for j in range(NS):
                s0 = T_Pc + j * T_S
                k.dma(k.sp, cmp_s[l, j, :, :], proj[s0:s0 + T_S, KV0:KV0 + 512])
                k.dma(k.sp, sel_s[l, j, :, :], proj[s0:s0 + T_S, KV0 + 512:KV0 + 1024])
                k.dma(k.sp, win_s[l, j, 512 - T_S:512, :], proj[s0:s0 + T_S, KV0 + 1024:KV0 + 1536])
                k.dma(k.sp, win_s[l, j, 0:512 - T_S, :], swin[l, j, T_S:512, :])
                k.dma(k.sp, sh_s[l, j, :, :], proj[s0 + T_S - 1:s0 + T_S, 0:RW])
            W = {nm: rwp[nm][l] for nm in RW_NAMES}
            with Scope() as s:
                C = load_rwkv_consts(k, nc, s.sb, W)
                with Scope() as s2:
                    rwkv_stage(k, nc, s2.sb, s2.ps, C, ident, b_ident, proj, 0, T_Pc, zrow, None, opsd, ymix[0:T_Pc], rw_p[l], L + "rp")
                for j in range(NS):
                    s0 = T_Pc + j * T_S
                    with Scope() as s2:
                        rwkv_stage(k, nc, s2.sb, s2.ps, C, ident, b_ident, proj, s0, T_S, ssh[l, j], srw[l, j], opsd, ymix[s0:s0 + T_S], rw_s[l, j], L + "r%d" % j)
            with Scope() as s:
                nsa_prompt_stage(k, nc, s.sb, s.ps, ident, b_ident, proj, 0, T_Pc, cmpw[l], CN, ymix[0:T_Pc], L + "np")
            for j in range(NS):
                s0 = T_Pc + j * T_S
                with Scope() as s:
                    nsa_sample_stage(k, nc, s.sb, s.ps, ident, b_ident, proj, s0, P, cmpw[l], cc.rearrange("l r c -> (l r) c"), cs.rearrange("l r c -> (l r) c"),
                                     ptab[j], swin[l, j], CN, CS, pcmp, psel, osc, ymix[s0:s0 + T_S], L + "q%d" % j, row_base=l * n_pool * 128)
            with Scope() as s:
                linear_stage(k, nc, s.sb, s.ps, ident, b_ident, ymix, NTOK, w_out[l], D, Hs, L + "lo")
            with Scope() as s:
                ln_stage(k, nc, s.sb, Xa, Hs, NTOK, lnp["ln1_g"][l], lnp["ln1_b"][l], X1, L + "n1")
            with Scope() as s:
                linear_stage(k, nc, s.sb, s.ps, ident, b_ident, X1, NTOK, wq[l], D, Qs, L + "lq")
            with Scope() as s:
                peer_stage(k, nc, s.sb, s.ps, ident, b_ident, X1, Qs, NTOK, skeys[l], puv.rearrange("l e t d -> (l e) (t d)"), Fs, L + "pe",
                           iota256, row_base=l * 16384)
            with Scope() as s:
                ln_stage(k, nc, s.sb, X1, Fs, NTOK, lnp["ln2_g"][l], lnp["ln2_b"][l], Xa, L + "n2")
        for r0 in range(0, T_Pc, 512):
            k.dma(k.sp, y_p[r0:r0 + 512, :], Xa[r0:r0 + 512, :])
        k.dma(k.sp, y_s[:, :], Xa[T_Pc:NTOK, :])
        k.finish()
    return nc


def make_in_maps(inputs, ncu, T_Pc, P, n_pool, NS):
    f = lambda nm: np.ascontiguousarray(np.asarray(inputs[nm], np.float32))
    xp, xs = f("x_prompt"), f("x_sample")
    shared = {
        "w_in": f("w_in"), "w_out": f("w_out"), "wq": f("peer_wq"), "skeys": f("peer_subkeys"), "puv": np.ascontiguousarray(np.stack([np.asarray(inputs["peer_u"], np.float32), np.asarray(inputs["peer_v"], np.float32)], axis=2)),
        "cmp_w": f("nsa_cmp_w"), "cc": f("cache_cmp_kv").reshape(DEPTH, n_pool * 128, 512), "cs": f("cache_sel_kv").reshape(DEPTH, n_pool * 128, 512),
        "zrow": np.zeros((1, RW), np.float32), "ident": np.eye(128, dtype=np.float32),
        "iota256": np.tile(np.arange(256, dtype=np.float32)[None, :], (128, 1)),
    }
    for nm in ("ln1_g", "ln1_b", "ln2_g", "ln2_b"):
        shared[nm] = f(nm).reshape(DEPTH, 1, D)
    for nm in RW_NAMES:
        a = f(nm)
        shared[nm] = a if a.ndim == 3 and nm in ("rwkv_w2", "rwkv_a2", "rwkv_g2") else a.reshape(DEPTH, 1, -1)
    shared.update(nsa_host_consts())
    shared.update(nsas_host_consts(P))
    pt = np.asarray(inputs["page_table"], np.int32)
    swin, srw, ssh = f("state_win_kv"), f("state_rwkv"), f("state_shift")
    maps = []
    for c in range(ncu):
        sidx = [c + j * ncu for j in range(NS)]
        m = dict(shared)
        m["xp"] = np.ascontiguousarray(xp[c])
        m["xs"] = np.ascontiguousarray(xs[sidx].reshape(NS * T_S, D))
        m["ptab"] = np.ascontiguousarray(pt[sidx].reshape(NS, 1, -1))
        m["swin"] = np.ascontiguousarray(swin[:, sidx].reshape(DEPTH, NS, 512, 512))
        m["srw"] = np.ascontiguousarray(srw[:, sidx])
        m["ssh"] = np.ascontiguousarray(ssh[:, sidx])
        maps.append(m)
    return maps


def assemble(res, ncu, T_Pc, NS):
    kv = (2, 4, 64)
    B, BS = ncu, ncu * NS
    st = lambda key, shp: np.stack([res[c][key].reshape(shp) for c in range(ncu)], axis=0)
    y_p = st("y_p", (T_Pc, D))
    y_s = np.zeros((BS, T_S, D), np.float32)
    mv = lambda a: np.moveaxis(a, 0, 1)
    cmp_p = mv(st("cmp_p", (DEPTH, T_Pc) + kv)); sel_p = mv(st("sel_p", (DEPTH, T_Pc) + kv)); win_p = mv(st("win_p", (DEPTH, 512) + kv))
    rw_p = mv(st("rw_p", (DEPTH, 16, 64, 64))); sh_p = mv(st("sh_p", (DEPTH, 1, RW)))
    cmp_s = np.zeros((DEPTH, BS, T_S) + kv, np.float32); sel_s = np.zeros_like(cmp_s)
    win_s = np.zeros((DEPTH, BS, 512) + kv, np.float32); rw_s = np.zeros((DEPTH, BS, 16, 64, 64), np.float32); sh_s = np.zeros((DEPTH, BS, 1, RW), np.float32)
    for c in range(ncu):
        r = res[c]
        for j in range(NS):
            b = c + j * ncu
            y_s[b] = r["y_s"].reshape(NS, T_S, D)[j]
            cmp_s[:, b] = r["cmp_s"].reshape((DEPTH, NS, T_S) + kv)[:, j]
            sel_s[:, b] = r["sel_s"].reshape((DEPTH, NS, T_S) + kv)[:, j]
            win_s[:, b] = r["win_s"].reshape((DEPTH, NS, 512) + kv)[:, j]
            rw_s[:, b] = r["rw_s"].reshape(DEPTH, NS, 16, 64, 64)[:, j]
            sh_s[:, b] = r["sh_s"].reshape(DEPTH, NS, 1, RW)[:, j]
    return (y_p, y_s, cmp_p, sel_p, win_p, rw_p, sh_p, cmp_s, sel_s, win_s, rw_s, sh_s)


def kernel(**inputs):
    nc = build_nc()
    maps = make_in_maps(inputs, NCU, T_P, 16384, 1280, 2)
    res = run_bass_kernel_spmd(nc, maps, core_ids=list(range(NCU))).results
    return assemble(res, NCU, T_P, 2)
```

```python
import contextlib
import itertools
import numpy as np
import concourse.bass as bass
import concourse.mybir as mybir
from concourse.bass_utils import run_bass_kernel_spmd

F32 = mybir.dt.float32
BF16 = mybir.dt.bfloat16
I32 = mybir.dt.int32
AF = mybir.ActivationFunctionType
ALU = mybir.AluOpType

NCORES = 8
D = 2048
KC = D // 128
T_P = 2048
T_S = 4
RW = 3360
IN_COLS = 5968
NSA0 = RW
KV0 = RW + 1024
DEPTH = 2


class Buf:
    __slots__ = ("w", "r")

    def __init__(self):
        self.w = None
        self.r = {}


class Eng:
    def __init__(self, ctx, nc, eng, name):
        self.nc = nc
        self.eng = eng
        self.key = name
        self.sem = ctx.enter_context(nc.semaphore("sem_" + name))
        self.cnt = 0
        self.seen = {}

    def need(self, tok):
        sem, val, key = tok
        if key == self.key and key == "pe":
            return None
        if self.seen.get(id(sem), 0) < val:
            self.seen[id(sem)] = val
            return (sem, val)
        return None

    def wait(self, tok):
        sem, val, key = tok
        if key == self.key and key == "pe":
            return
        if self.seen.get(id(sem), 0) < val:
            self.eng.wait_ge(sem, val)
            self.seen[id(sem)] = val


class DmaSem:
    def __init__(self, h, i):
        self.h = h
        self.val = 0
        self.key = "dma%d" % i


class K:
    def __init__(self, ctx, nc):
        self.ctx = ctx
        self.nc = nc
        self.pe = Eng(ctx, nc, nc.tensor, "pe")
        self.dve = Eng(ctx, nc, nc.vector, "dve")
        self.act = Eng(ctx, nc, nc.scalar, "act")
        self.pool = Eng(ctx, nc, nc.gpsimd, "pool")
        self.sp = Eng(ctx, nc, nc.sync, "sp")
        self.dsems = [DmaSem(ctx.enter_context(nc.semaphore("dsem%d" % i)), i) for i in range(32)]
        self.dnext = 0

    def _deps(self, reads, writes):
        deps = []
        for b in reads:
            if b.w is not None:
                deps.append(b.w)
        for b in writes:
            if b.w is not None:
                deps.append(b.w)
            deps.extend(b.r.values())
        return deps

    def _mark(self, tok, reads, writes):
        for b in writes:
            b.w = tok
            b.r = {}
        for b in reads:
            b.r[tok[2]] = tok

    def _waits(self, E, toks):
        best = {}
        for tok in toks:
            sem, val, key = tok
            if key == E.key and key == "pe":
                continue
            if E.seen.get(id(sem), 0) < val and best.get(id(sem), (None, 0))[1] < val:
                best[id(sem)] = (sem, val)
        needs = list(best.values())
        for sem, val in needs:
            E.seen[id(sem)] = val
        for sem, val in needs[:-1]:
            E.eng.wait_ge(sem, val)
        return needs[-1] if needs else None

    def op(self, E, fn, reads=(), writes=()):
        last = self._waits(E, self._deps(reads, writes))
        inst = fn(E.eng)
        if last is not None:
            inst.wait_op(last[0], last[1], "sem-ge")
        E.cnt += 1
        inst.then_inc(E.sem, 1)
        self._mark((E.sem, E.cnt, E.key), reads, writes)
        return inst

    def dma(self, Q, out, in_, reads=(), writes=(), **kw):
        ds = self.dsems[self.dnext]
        self.dnext = (self.dnext + 1) % len(self.dsems)
        toks = self._deps(reads, writes)
        if ds.val:
            toks.append((ds.h, ds.val, ds.key))
        last = self._waits(Q, toks)
        inst = Q.eng.dma_start(out=out, in_=in_, **kw)
        if last is not None:
            inst.wait_op(last[0], last[1], "sem-ge")
        ds.val += 16
        inst.then_inc(ds.h, 16)
        self._mark((ds.h, ds.val, ds.key), reads, writes)
        return inst

    def barrier(self):
        engs = (self.pe, self.dve, self.act, self.pool, self.sp)
        toks = [(E.sem, E.cnt, E.key) for E in engs if E.cnt]
        toks += [(ds.h, ds.val, ds.key) for ds in self.dsems if ds.val]
        for E in engs:
            for t in toks:
                E.wait(t)

    def finish(self):
        for ds in self.dsems:
            if ds.val:
                self.sp.wait((ds.h, ds.val, ds.key))
        for E in (self.pe, self.dve, self.act, self.pool):
            if E.cnt:
                self.sp.wait((E.sem, E.cnt, E.key))


def load_rwkv_consts(k, nc, sb, W):
    C = {}
    def bc(name, ap_row, n):
        t = sb("c_" + name, [128, n], F32)
        b = Buf()
        k.dma(k.sp, t[:], ap_row.partition_broadcast(128), writes=[b])
        C[name] = (t, b)
    bc("mu", W["rwkv_mu"], RW)
    bc("kkw", W["rwkv_k_k"], 1024)
    bc("ka", W["rwkv_k_a"], 1024)
    bc("rk", W["rwkv_r_k"], 1024)
    bc("gng", W["rwkv_gn_g"], 1024)
    bc("gnb", W["rwkv_gn_b"], 1024)
    w2e = sb("c_w2e", [65, 1024], F32); b1 = Buf()
    k.dma(k.sp, w2e[0:64, :], W["rwkv_w2"], writes=[b1])
    k.dma(k.sp, w2e[64:65, :], W["rwkv_w0"], writes=[b1])
    a2e = sb("c_a2e", [65, 1024], F32); b2 = Buf()
    k.dma(k.sp, a2e[0:64, :], W["rwkv_a2"], writes=[b2])
    k.dma(k.sp, a2e[64:65, :], W["rwkv_a0"], writes=[b2])
    g2a = sb("c_g2a", [128, 1024], F32); g2b = sb("c_g2b", [32, 1024], F32); b3 = Buf()
    k.dma(k.sp, g2a[:], W["rwkv_g2"][0:128, :], writes=[b3])
    k.dma(k.sp, g2b[:], W["rwkv_g2"][128:160, :], writes=[b3])
    C["w2e"] = (w2e, b1); C["a2e"] = (a2e, b2); C["g2"] = ((g2a, g2b), b3)
    return C


def rwkv_stage(k, nc, sb, ps, C, ident, b_ident, proj, row0, T, shift_prev, s0, opsd, ymix, o_state, tag, dbg=None):
    TS = 2
    mu, b_mu = C["mu"]; kkw, b_kkw = C["kkw"]; ka, b_ka = C["ka"]; rk, b_rk = C["rk"]
    gng, b_gng = C["gng"]; gnb, b_gnb = C["gnb"]
    w2e, b_w2e = C["w2e"]; a2e, b_a2e = C["a2e"]; (g2a, g2b), b_g2 = C["g2"]
    n = lambda s: "%s_%s" % (tag, s)
    xr = sb(n("xr"), [128, RW], F32); b_xr = Buf()
    xs = sb(n("xs"), [128, RW], F32); b_xs = Buf()
    lo = sb(n("lo"), [128, 2 * 65 + 160], F32); b_lo = Buf()
    lT = sb(n("lT"), [128, 4, 128], F32); b_lT = Buf()
    ops = sb(n("ops"), [128, 5, 1024], F32); b_ops = Buf()
    av = sb(n("a"), [128, 1024], F32); b_a = Buf()
    gv = sb(n("g"), [128, 1024], F32); b_g = Buf()
    t1 = sb(n("t1"), [128, 1024], F32); b_t1 = Buf()
    t2 = sb(n("t2"), [128, 1024], F32); b_t2 = Buf()
    sm = sb(n("sm"), [128, 64], F32); b_sm = Buf()
    VT = sb(n("VT"), [128, 8, 128], F32); b_VT = Buf()
    YT = sb(n("YT"), [128, 8, 128], F32); b_YT = Buf()
    SS = [sb(n("S%d" % i), [128, 512], F32) for i in range(2)]; b_SS = [Buf(), Buf()]
    S, b_S = SS[0], b_SS[0]
    sidx = [0]
    Sd = [sb(n("Sd%d" % i), [128, 512], F32) for i in range(2)]; b_Sd = [Buf(), Buf()]
    Ty = [sb(n("Ty%d" % i), [128, 512], F32) for i in range(2)]; b_Ty = [Buf(), Buf()]
    tmp = sb(n("tmp"), [128, 512], F32); b_tmp = Buf()
    tb = sb(n("tb"), [128, 512], F32); b_tb = Buf()
    t3 = [sb(n("t3%d" % i), [128, 512], F32) for i in range(2)]; b_t3 = [Buf(), Buf()]
    sk = sb(n("sk"), [128, 8], F32); b_sk = Buf()
    BC = [sb(n("BC%d" % i), [128, TS, 5, 512], F32) for i in range(2)]; b_BC = [Buf(), Buf()]
    yo = sb(n("yo"), [128, 1024], F32); b_yo = Buf()
    pz = [ps(n("pz%d" % i), [128, 512], F32) for i in range(4)]; b_pz = [Buf() for _ in range(4)]
    b_opsd = Buf()
    b_ymix = Buf()

    k.op(k.dve, lambda e: e.memset(lo[:, 64:65], 1.0), writes=[b_lo])
    k.op(k.dve, lambda e: e.memset(lo[:, 129:130], 1.0), writes=[b_lo])
    if s0 is None:
        k.op(k.dve, lambda e: e.memset(S[:], 0.0), writes=[b_S])
    else:
        k.dma(k.sp, S[:].rearrange("p (a c) -> p a c", a=8), s0.rearrange("(a b) v c -> (b v) a c", b=2), writes=[b_S])

    ntile = (T + 127) // 128
    for it in range(ntile):
        r0 = it * 128
        rows = min(128, T - r0)
        R = slice(0, rows)
        g0 = row0 + r0
        k.dma(k.sp, xr[R, :], proj[g0:g0 + rows, 0:RW], writes=[b_xr])
        if it == 0:
            k.dma(k.sp, xs[0:1, :], shift_prev, writes=[b_xs])
            if rows > 1:
                k.dma(k.sp, xs[1:rows, :], proj[g0:g0 + rows - 1, 0:RW], writes=[b_xs])
        else:
            k.dma(k.sp, xs[R, :], proj[g0 - 1:g0 + rows - 1, 0:RW], writes=[b_xs])
        k.op(k.dve, lambda e: e.tensor_tensor(out=xs[R, :], in0=xs[R, :], in1=xr[R, :], op=ALU.subtract), reads=[b_xr], writes=[b_xs])
        k.op(k.dve, lambda e: e.tensor_tensor(out=xs[R, :], in0=xs[R, :], in1=mu[R, :], op=ALU.mult), reads=[b_mu], writes=[b_xs])
        k.op(k.dve, lambda e: e.tensor_tensor(out=xs[R, :], in0=xs[R, :], in1=xr[R, :], op=ALU.add), reads=[b_xr], writes=[b_xs])
        r_, k_, v_ = xs[R, 0:1024], xs[R, 1024:2048], xs[R, 2048:3072]
        k.op(k.act, lambda e: e.activation(out=lo[R, 0:64], in_=xs[R, 3072:3136], func=AF.Tanh), reads=[b_xs], writes=[b_lo])
        k.op(k.act, lambda e: e.activation(out=lo[R, 130:290], in_=xs[R, 3200:3360], func=AF.Sigmoid), reads=[b_xs], writes=[b_lo])
        k.op(k.dve, lambda e: e.tensor_copy(out=lo[R, 65:129], in_=xs[R, 3136:3200]), reads=[b_xs], writes=[b_lo])
        segs = [(0, 65), (65, 65), (130, 128), (258, 32)]
        for j, (c0, w) in enumerate(segs):
            k.op(k.pe, lambda e: e.transpose(pz[0][0:w, j * 128:j * 128 + rows], lo[R, c0:c0 + w], ident[R, R]),
                 reads=[b_lo, b_ident], writes=[b_pz[0]])
        k.op(k.dve, lambda e: e.tensor_copy(out=lT[:, :, R], in_=pz[0][:].rearrange("p (j c) -> p j c", j=4)[:, :, R]),
             reads=[b_pz[0]], writes=[b_lT])
        for hf in range(2):
            cs = slice(hf * 512, (hf + 1) * 512)
            k.op(k.pe, lambda e: e.matmul(pz[1 + hf][R, :], lhsT=lT[0:65, 0, R], rhs=w2e[0:65, cs], start=True, stop=True),
                 reads=[b_lT, b_w2e], writes=[b_pz[1 + hf]])
            k.op(k.act, lambda e: e.activation(out=t1[R, cs], in_=pz[1 + hf][R, :], func=AF.Sigmoid), reads=[b_pz[1 + hf]], writes=[b_t1])
        k.op(k.act, lambda e: e.activation(out=ops[R, 1, :], in_=t1[R, :], func=AF.Exp, scale=-0.6065306597126334), reads=[b_t1], writes=[b_ops])
        for hf in range(2):
            cs = slice(hf * 512, (hf + 1) * 512)
            k.op(k.pe, lambda e: e.matmul(pz[1 + hf][R, :], lhsT=lT[0:65, 1, R], rhs=a2e[0:65, cs], start=True, stop=True),
                 reads=[b_lT, b_a2e], writes=[b_pz[1 + hf]])
            k.op(k.act, lambda e: e.activation(out=av[R, cs], in_=pz[1 + hf][R, :], func=AF.Sigmoid), reads=[b_pz[1 + hf]], writes=[b_a])
        for hf in range(2):
            cs = slice(hf * 512, (hf + 1) * 512)
            k.op(k.pe, lambda e: e.matmul(pz[1 + hf][R, :], lhsT=lT[0:128, 2, R], rhs=g2a[:, cs], start=True, stop=False),
                 reads=[b_lT, b_g2], writes=[b_pz[1 + hf]])
            k.op(k.pe, lambda e: e.matmul(pz[1 + hf][R, :], lhsT=lT[0:32, 3, R], rhs=g2b[:, cs], start=False, stop=True),
                 reads=[b_lT, b_g2], writes=[b_pz[1 + hf]])
            k.op(k.act, lambda e: e.activation(out=gv[R, cs], in_=pz[1 + hf][R, :], func=AF.Copy), reads=[b_pz[1 + hf]], writes=[b_g])
        k.op(k.dve, lambda e: e.tensor_tensor(out=t1[R, :], in0=k_, in1=kkw[R, :], op=ALU.mult), reads=[b_xs, b_kkw], writes=[b_t1])
        k.op(k.pool, lambda e: e.tensor_tensor(out=t2[R, :], in0=t1[R, :], in1=t1[R, :], op=ALU.mult), reads=[b_t1], writes=[b_t2])
        k.op(k.dve, lambda e: e.tensor_reduce(out=sm[R, 0:16], in_=t2[R, :].rearrange("p (h c) -> p h c", h=16), axis=mybir.AxisListType.X, op=ALU.add),
             reads=[b_t2], writes=[b_sm])
        k.op(k.dve, lambda e: e.tensor_scalar(out=sm[R, 0:16], in0=sm[R, 0:16], scalar1=1e-24, scalar2=None, op0=ALU.max), writes=[b_sm])
        k.op(k.act, lambda e: e.activation(out=sm[R, 0:16], in_=sm[R, 0:16], func=AF.Sqrt), writes=[b_sm])
        k.op(k.dve, lambda e: e.reciprocal(out=sm[R, 0:16], in_=sm[R, 0:16]), writes=[b_sm])
        k.op(k.dve, lambda e: e.tensor_tensor(out=ops[R, 0, :].rearrange("p (h c) -> p h c", h=16), in0=t1[R, :].rearrange("p (h c) -> p h c", h=16),
                                              in1=sm[R, 0:16].unsqueeze(2).to_broadcast([rows, 16, 64]), op=ALU.mult),
             reads=[b_t1, b_sm], writes=[b_ops])
        k.op(k.dve, lambda e: e.scalar_tensor_tensor(out=t2[R, :], in0=av[R, :], scalar=-1.0, in1=ka[R, :], op0=ALU.add, op1=ALU.mult),
             reads=[b_a, b_ka], writes=[b_t2])
        k.op(k.dve, lambda e: e.scalar_tensor_tensor(out=ops[R, 3, :], in0=t2[R, :], scalar=1.0, in1=k_, op0=ALU.add, op1=ALU.mult),
             reads=[b_t2, b_xs], writes=[b_ops])
        k.op(k.dve, lambda e: e.tensor_tensor(out=ops[R, 2, :], in0=ops[R, 0, :], in1=av[R, :], op=ALU.mult), reads=[b_a], writes=[b_ops])
        k.op(k.pool, lambda e: e.tensor_copy(out=ops[R, 4, :], in_=r_), reads=[b_xs], writes=[b_ops])
        k.op(k.dve, lambda e: e.tensor_tensor(out=t1[R, :], in0=r_, in1=ops[R, 3, :], op=ALU.mult), reads=[b_xs, b_ops], writes=[b_t1])
        k.op(k.dve, lambda e: e.tensor_tensor(out=t1[R, :], in0=t1[R, :], in1=rk[R, :], op=ALU.mult), reads=[b_rk], writes=[b_t1])
        k.op(k.dve, lambda e: e.tensor_reduce(out=sm[R, 16:32], in_=t1[R, :].rearrange("p (h c) -> p h c", h=16), axis=mybir.AxisListType.X, op=ALU.add),
             reads=[b_t1], writes=[b_sm])
        if dbg is not None and it == 0:
            k.dma(k.sp, dbg["ops"][R], ops[R, :, :], reads=[b_ops])
            k.dma(k.sp, dbg["a"][R], av[R, :], reads=[b_a])
            k.dma(k.sp, dbg["g"][R], gv[R, :], reads=[b_g])
            k.dma(k.sp, dbg["sm"][R], sm[R, :], reads=[b_sm])
            k.dma(k.sp, dbg["xs"][R], xs[R, :], reads=[b_xs])
        for o_ in range(5):
            for b_ in range(2):
                k.dma(k.sp, opsd[r0:r0 + rows, b_, o_, :].rearrange("t (a c) -> t a c", a=8),
                      ops[R, o_, :].rearrange("p (a b c) -> p a b c", a=8, b=2)[:, :, b_, :], reads=[b_ops], writes=[b_opsd])
        for g in range(2):
            for j in range(4):
                h8 = g * 4 + j
                k.op(k.pe, lambda e: e.transpose(pz[1 + g][:, j * 128:j * 128 + rows], xs[R, 2048 + h8 * 128:2048 + (h8 + 1) * 128], ident[R, R]),
                     reads=[b_xs, b_ident], writes=[b_pz[1 + g]])
            k.op(k.act, lambda e: e.activation(out=VT[:, g * 4:(g + 1) * 4, R], in_=pz[1 + g][:].rearrange("p (j c) -> p j c", j=4)[:, :, R], func=AF.Copy),
                 reads=[b_pz[1 + g]], writes=[b_VT])
        ngrp = (rows + TS - 1) // TS
        def load_grp(gi):
            q0 = r0 + gi * TS
            nn = min(TS, rows - gi * TS)
            bt, bb = BC[gi % 2], b_BC[gi % 2]
            for h2 in range(2):
                k.dma(k.sp, bt[h2 * 64:(h2 + 1) * 64, 0:nn, :, :], opsd[q0:q0 + nn, h2, :, :].partition_broadcast(64), reads=[b_opsd], writes=[bb])
        load_grp(0)
        v3 = lambda ap: ap.rearrange("p (a c) -> p a c", a=8)
        pend = None
        for gi in range(ngrp):
            if gi + 1 < ngrp:
                load_grp(gi + 1)
            bt, bb = BC[gi % 2], b_BC[gi % 2]
            nn = min(TS, rows - gi * TS)
            for s in range(nn):
                tl = gi * TS + s
                cur, b_cur = SS[sidx[0] % 2], b_SS[sidx[0] % 2]
                nxt, b_nxt = SS[(sidx[0] + 1) % 2], b_SS[(sidx[0] + 1) % 2]
                sidx[0] += 1
                tt, btt = t3[tl % 2], b_t3[tl % 2]
                sd, bsd = Sd[tl % 2], b_Sd[tl % 2]
                ty, bty = Ty[tl % 2], b_Ty[tl % 2]
                k.op(k.pool, lambda e: e.tensor_tensor(out=sd[:], in0=cur[:], in1=bt[:, s, 1, :], op=ALU.mult), reads=[b_cur, bb], writes=[bsd])
                k.op(k.pool, lambda e: e.tensor_tensor(out=v3(tt[:]), in0=v3(bt[:, s, 3, :]), in1=VT[:, :, tl:tl + 1].to_broadcast([128, 8, 64]), op=ALU.mult),
                     reads=[bb, b_VT], writes=[btt])
                k.op(k.dve, lambda e: e.tensor_tensor(out=tmp[:], in0=cur[:], in1=bt[:, s, 0, :], op=ALU.mult), reads=[b_cur, bb], writes=[b_tmp])
                k.op(k.dve, lambda e: e.tensor_reduce(out=sk[:], in_=v3(tmp[:]), axis=mybir.AxisListType.X, op=ALU.add), reads=[b_tmp], writes=[b_sk])
                k.op(k.dve, lambda e: e.tensor_tensor(out=v3(tb[:]), in0=v3(bt[:, s, 2, :]), in1=sk[:].unsqueeze(2).to_broadcast([128, 8, 64]), op=ALU.mult),
                     reads=[bb, b_sk], writes=[b_tb])
                if pend is not None:
                    pend()
                    pend = None
                k.op(k.dve, lambda e: e.tensor_tensor(out=nxt[:], in0=sd[:], in1=tb[:], op=ALU.subtract), reads=[bsd, b_tb], writes=[b_nxt])
                k.op(k.dve, lambda e: e.tensor_tensor(out=nxt[:], in0=nxt[:], in1=tt[:], op=ALU.add), reads=[btt], writes=[b_nxt])
                k.op(k.pool, lambda e: e.tensor_tensor(out=ty[:], in0=nxt[:], in1=bt[:, s, 4, :], op=ALU.mult), reads=[b_nxt, bb], writes=[bty])
                def mk(ty=ty, bty=bty, tl=tl):
                    k.op(k.dve, lambda e: e.tensor_reduce(out=YT[:, :, tl], in_=v3(ty[:]), axis=mybir.AxisListType.X, op=ALU.add), reads=[bty], writes=[b_YT])
                pend = mk
        if pend is not None:
            pend()
            pend = None
        if dbg is not None and it == 0:
            k.dma(k.sp, dbg["YT"], YT[:], reads=[b_YT])
            k.dma(k.sp, dbg["VT"], VT[:], reads=[b_VT])
        for g in range(2):
            for j in range(4):
                h8 = g * 4 + j
                k.op(k.pe, lambda e: e.transpose(pz[1 + g][R, j * 128:(j + 1) * 128], YT[:, h8, R], ident[:, :]),
                     reads=[b_YT, b_ident], writes=[b_pz[1 + g]])
            k.op(k.act, lambda e: e.activation(out=yo[R, g * 512:(g + 1) * 512], in_=pz[1 + g][R, :], func=AF.Copy), reads=[b_pz[1 + g]], writes=[b_yo])
        y3 = yo[R, :].rearrange("p (h c) -> p h c", h=16)
        k.op(k.dve, lambda e: e.tensor_reduce(out=sm[R, 32:48], in_=y3, axis=mybir.AxisListType.X, op=ALU.add), reads=[b_yo], writes=[b_sm])
        k.op(k.dve, lambda e: e.tensor_scalar(out=sm[R, 32:48], in0=sm[R, 32:48], scalar1=1.0 / 64, scalar2=None, op0=ALU.mult), writes=[b_sm])
        k.op(k.dve, lambda e: e.tensor_tensor(out=y3, in0=y3, in1=sm[R, 32:48].unsqueeze(2).to_broadcast([rows, 16, 64]), op=ALU.subtract), reads=[b_sm], writes=[b_yo])
        k.op(k.pool, lambda e: e.tensor_tensor(out=t1[R, :], in0=yo[R, :], in1=yo[R, :], op=ALU.mult), reads=[b_yo], writes=[b_t1])
        k.op(k.dve, lambda e: e.tensor_reduce(out=sm[R, 48:64], in_=t1[R, :].rearrange("p (h c) -> p h c", h=16), axis=mybir.AxisListType.X, op=ALU.add),
             reads=[b_t1], writes=[b_sm])
        k.op(k.dve, lambda e: e.tensor_scalar(out=sm[R, 48:64], in0=sm[R, 48:64], scalar1=1.0 / 64, scalar2=64e-5, op0=ALU.mult, op1=ALU.add), writes=[b_sm])
        k.op(k.act, lambda e: e.activation(out=sm[R, 48:64], in_=sm[R, 48:64], func=AF.Sqrt), writes=[b_sm])
        k.op(k.dve, lambda e: e.reciprocal(out=sm[R, 48:64], in_=sm[R, 48:64]), writes=[b_sm])
        k.op(k.dve, lambda e: e.tensor_tensor(out=y3, in0=y3, in1=sm[R, 48:64].unsqueeze(2).to_broadcast([rows, 16, 64]), op=ALU.mult), reads=[b_sm], writes=[b_yo])
        k.op(k.dve, lambda e: e.tensor_tensor(out=yo[R, :], in0=yo[R, :], in1=gng[R, :], op=ALU.mult), reads=[b_gng], writes=[b_yo])
        k.op(k.dve, lambda e: e.tensor_tensor(out=yo[R, :], in0=yo[R, :], in1=gnb[R, :], op=ALU.add), reads=[b_gnb], writes=[b_yo])
        k.op(k.dve, lambda e: e.tensor_tensor(out=t1[R, :].rearrange("p (h c) -> p h c", h=16), in0=v_.rearrange("p (h c) -> p h c", h=16),
                                              in1=sm[R, 16:32].unsqueeze(2).to_broadcast([rows, 16, 64]), op=ALU.mult),
             reads=[b_xs, b_sm], writes=[b_t1])
        k.op(k.dve, lambda e: e.tensor_tensor(out=yo[R, :], in0=yo[R, :], in1=t1[R, :], op=ALU.add), reads=[b_t1], writes=[b_yo])
        k.op(k.dve, lambda e: e.tensor_tensor(out=yo[R, :], in0=yo[R, :], in1=gv[R, :], op=ALU.mult), reads=[b_g], writes=[b_yo])
        k.dma(k.sp, ymix[r0:r0 + rows, 0:1024], yo[R, :], reads=[b_yo], writes=[b_ymix])
    Sf, b_Sf = SS[sidx[0] % 2], b_SS[sidx[0] % 2]
    k.dma(k.sp, o_state.rearrange("(a b) v c -> (b v) a c", b=2), Sf[:].rearrange("p (a c) -> p a c", a=8), reads=[b_Sf])
    return b_ymix
NEG = -1.0e30


def nsa_host_consts():
    p = np.arange(128)
    mask4 = (p[:, None] // 32 == np.arange(4)[None, :]).astype(np.float32)
    biasC = np.zeros((16, 128, 64), np.float32)
    FM = np.zeros((16, 128, 32), np.float32)
    FA = np.zeros((16, 128, 32), np.float32)
    for i in range(16):
        qpos = i * 128 + p
        cmp_end = (np.arange(64) + 1) * 32 - 1
        biasC[i] = np.where(cmp_end[None, :] <= qpos[:, None], 0.0, NEG)
        cur = qpos // 64
        blk = np.arange(32)[None, :]
        forced = (blk == 0) | (blk == cur[:, None]) | (blk == cur[:, None] - 1)
        fut = blk > cur[:, None]
        FM[i] = np.where(fut | forced, 0.0, 1.0)
        FA[i] = np.where(fut, -1.0, np.where(forced, 1.0e4, 0.0))
    tri = np.where(p[None, :] > p[:, None], NEG, 0.0).astype(np.float32)
    bw4 = np.where(p[None, :] >= p[:, None] + 1, 0.0, NEG).astype(np.float32)
    return {"nsa_mask4": mask4, "nsa_biasC": biasC, "nsa_FM": FM, "nsa_FA": FA, "nsa_tri": tri, "nsa_bw4": bw4}


def nsa_prompt_stage(k, nc, sb, ps, ident, b_ident, proj, row0, T, cmp_w, CN, ymix, tag):
    NT = T // 128
    X = mybir.AxisListType.X
    n = lambda s: "%s_%s" % (tag, s)
    PS = [ps(n("ps%d" % i), [128, 512], F32) for i in range(8)]
    bPS = [Buf() for _ in range(8)]
    mask4 = sb(n("mask4"), [128, 4], F32); tri = sb(n("tri"), [128, 128], F32); bw4 = sb(n("bw4"), [128, 128], F32)
    b_c = Buf()
    k.dma(k.sp, mask4[:], CN["nsa_mask4"], writes=[b_c])
    k.dma(k.sp, tri[:], CN["nsa_tri"], writes=[b_c])
    k.dma(k.sp, bw4[:], CN["nsa_bw4"], writes=[b_c])
    wcol = sb(n("wcol"), [128, 2], F32); b_wcol = Buf()
    for c in range(2):
        for j in range(4):
            k.dma(k.sp, wcol[j * 32:(j + 1) * 32, c:c + 1], cmp_w[c:c + 1, :].rearrange("o l -> l o"), writes=[b_wcol])
    Ww = sb(n("Ww"), [128, 2, 124], F32); b_Ww = Buf()
    k.op(k.dve, lambda e: e.memset(Ww[:], 0.0), writes=[b_Ww])
    for c in range(2):
        k.op(k.dve, lambda e: e.tensor_scalar(out=Ww[:, c, 60:64], in0=mask4[:], scalar1=wcol[:, c:c + 1], scalar2=None, op0=ALU.mult),
             reads=[b_c, b_wcol], writes=[b_Ww])
    KTs = sb(n("KTs"), [64, 4, T], F32); KTw = sb(n("KTw"), [64, 4, T], F32)
    Vs = sb(n("Vs"), [128, NT, 256], F32); Vw = sb(n("Vw"), [128, NT, 256], F32)
    b_KT = Buf(); b_V = Buf()
    kin = [sb(n("kin%d" % i), [128, 3, 256], F32) for i in range(2)]; b_kin = [Buf(), Buf()]
    kcbT = sb(n("kcbT"), [64, 4, 64], F32); vcb = sb(n("vcb"), [64, 256], F32); b_cb = Buf()
    for i in range(NT):
        g0 = row0 + i * 128
        kt, bk = kin[i % 2], b_kin[i % 2]
        for j, off in enumerate((0, 512, 1024)):
            k.dma(k.sp, kt[:, j, :], proj[g0:g0 + 128, KV0 + off:KV0 + off + 256], writes=[bk])
        k.dma(k.sp, Vs[:, i, :], proj[g0:g0 + 128, KV0 + 768:KV0 + 1024], writes=[b_V])
        k.dma(k.sp, Vw[:, i, :], proj[g0:g0 + 128, KV0 + 1280:KV0 + 1536], writes=[b_V])
        vct = sb(n("vct%d" % i), [128, 256], F32) if False else None
        for (j, dst, pi) in ((1, KTs, 0), (2, KTw, 1)):
            for kv in range(4):
                k.op(k.pe, lambda e: e.transpose(PS[pi][0:64, kv * 128:(kv + 1) * 128], kt[:, j, kv * 64:(kv + 1) * 64], ident[:, :]),
                     reads=[bk, b_ident], writes=[bPS[pi]])
            k.op(k.act if pi else k.dve,
                 (lambda e: e.activation(out=dst[:, :, i * 128:(i + 1) * 128], in_=PS[pi][0:64, :].rearrange("p (a c) -> p a c", a=4), func=AF.Copy)) if pi else
                 (lambda e: e.tensor_copy(out=dst[:, :, i * 128:(i + 1) * 128], in_=PS[pi][0:64, :].rearrange("p (a c) -> p a c", a=4))),
                 reads=[bPS[pi]], writes=[b_KT])
        for kv in range(4):
            k.op(k.pe, lambda e: e.matmul(PS[2][0:64, kv * 64:(kv + 1) * 64], lhsT=kt[:, 0, kv * 64:(kv + 1) * 64], rhs=Ww[:, 0, 60 - 4 * i:124 - 4 * i],
                                          start=(i == 0 and kv == 0), stop=(i == NT - 1 and kv == 3), skip_group_check=True),
                 reads=[bk, b_Ww], writes=[bPS[2]])
    for i in range(NT):
        g0 = row0 + i * 128
        kt, bk = kin[i % 2], b_kin[i % 2]
        k.dma(k.sp, kt[:, 0, :], proj[g0:g0 + 128, KV0 + 256:KV0 + 512], writes=[bk])
        k.op(k.pe, lambda e: e.matmul(PS[3][0:64, 0:256], lhsT=Ww[:, 1, 60 - 4 * i:124 - 4 * i], rhs=kt[:, 0, :], start=(i == 0), stop=(i == NT - 1)),
             reads=[bk, b_Ww], writes=[bPS[3]])
    k.op(k.dve, lambda e: e.tensor_copy(out=kcbT[:], in_=PS[2][0:64, 0:256].rearrange("p (a c) -> p a c", a=4)), reads=[bPS[2]], writes=[b_cb])
    k.op(k.dve, lambda e: e.tensor_copy(out=vcb[:], in_=PS[3][0:64, 0:256]), reads=[bPS[3]], writes=[b_cb])

    qin = sb(n("qin"), [128, 1024 + 48], F32); b_qin = Buf()
    qT = sb(n("qT"), [64, 16, 128], F32); b_qT = Buf()
    cst = sb(n("cst"), [128, 64 + 32 + 32], F32); b_cst = Buf()
    sc = sb(n("sc"), [128, T], F32); b_sc = Buf()
    st = sb(n("st"), [128, 8], F32); b_st = Buf()
    pc = sb(n("pc"), [128, 64], F32); b_pc = Buf()
    imp = sb(n("imp"), [128, 4, 32], F32); b_imp = Buf()
    impw = sb(n("impw"), [128, 4, 32], F32); b_impw = Buf()
    mx = sb(n("mx"), [128, 16], F32); b_mx = Buf()
    selb = sb(n("selb"), [128, 4, 32], F32); b_selb = Buf()
    eT = [sb(n("eT%d" % i), [128, 128], F32) for i in range(2)]; b_eT = [Buf(), Buf()]
    oc = sb(n("oc"), [128, 3, 16, 64], F32); b_oc = Buf()
    gt = sb(n("gt"), [128, 48], F32); b_gt = Buf()
    yo = sb(n("yo"), [128, 1024], F32); b_yo = Buf()
    b_ymix = Buf()
    net = [0]

    def softmax_pv(h, ncols, vtiles, slot):
        k.op(k.dve, lambda e: e.tensor_reduce(out=st[:, 0:1], in_=sc[:, 0:ncols], axis=X, op=ALU.max), reads=[b_sc], writes=[b_st])
        k.op(k.dve, lambda e: e.tensor_scalar(out=st[:, 0:1], in0=st[:, 0:1], scalar1=-1.0e20, scalar2=-1.0, op0=ALU.max, op1=ALU.mult), writes=[b_st])
        k.op(k.act, lambda e: e.activation(out=sc[:, 0:ncols], in_=sc[:, 0:ncols], func=AF.Exp, bias=st[:, 0:1], scale=1.0), reads=[b_st], writes=[b_sc])
        k.op(k.dve, lambda e: e.tensor_reduce(out=st[:, 1:2], in_=sc[:, 0:ncols], axis=X, op=ALU.add), reads=[b_sc], writes=[b_st])
        k.op(k.dve, lambda e: e.tensor_scalar(out=st[:, 1:2], in0=st[:, 1:2], scalar1=1.0e-30, scalar2=None, op0=ALU.max), writes=[b_st])
        k.op(k.dve, lambda e: e.reciprocal(out=st[:, 1:2], in_=st[:, 1:2]), writes=[b_st])
        for vi, (c0, w, vap) in enumerate(vtiles):
            pt, bpt = PS[4 + net[0] % 2], bPS[4 + net[0] % 2]
            et, bet = eT[net[0] % 2], b_eT[net[0] % 2]
            net[0] += 1
            k.op(k.pe, lambda e: e.transpose(pt[0:w, 0:128], sc[:, c0:c0 + w], ident[:, :]), reads=[b_sc, b_ident], writes=[bpt])
            k.op(k.act, lambda e: e.activation(out=et[0:w, :], in_=pt[0:w, 0:128], func=AF.Copy), reads=[bpt], writes=[bet])
            k.op(k.pe, lambda e: e.matmul(PS[6][:, 0:64], lhsT=et[0:w, :], rhs=vap, start=(vi == 0), stop=(vi == len(vtiles) - 1)),
                 reads=[bet, b_V, b_cb], writes=[bPS[6]])
        k.op(k.dve, lambda e: e.tensor_scalar(out=oc[:, slot, h, :], in0=PS[6][:, 0:64], scalar1=st[:, 1:2], scalar2=None, op0=ALU.mult),
             reads=[bPS[6], b_st], writes=[b_oc])

    for i in range(NT):
        g0 = row0 + i * 128
        k.dma(k.sp, qin[:, 0:1024], proj[g0:g0 + 128, NSA0:NSA0 + 1024], writes=[b_qin])
        k.dma(k.sp, qin[:, 1024:1072], proj[g0:g0 + 128, KV0 + 1536:KV0 + 1584], writes=[b_qin])
        k.dma(k.sp, cst[:, 0:64], CN["nsa_biasC"][i], writes=[b_cst])
        k.dma(k.sp, cst[:, 64:96], CN["nsa_FM"][i], writes=[b_cst])
        k.dma(k.sp, cst[:, 96:128], CN["nsa_FA"][i], writes=[b_cst])
        k.op(k.act, lambda e: e.activation(out=gt[:], in_=qin[:, 1024:1072], func=AF.Sigmoid), reads=[b_qin], writes=[b_gt])
        for g in range(4):
            for j in range(4):
                h = g * 4 + j
                k.op(k.pe, lambda e: e.transpose(PS[g % 2][0:64, j * 128:(j + 1) * 128], qin[:, h * 64:(h + 1) * 64], ident[:, :]),
                     reads=[b_qin, b_ident], writes=[bPS[g % 2]])
            k.op(k.act, lambda e: e.activation(out=qT[:, g * 4:(g + 1) * 4, :], in_=PS[g % 2][0:64, :].rearrange("p (a c) -> p a c", a=4), func=AF.Copy, scale=0.125),
                 reads=[bPS[g % 2]], writes=[b_qT])
        for h in range(16):
            kv, g = h // 4, h % 4
            k.op(k.pe, lambda e: e.matmul(PS[2][:, 0:64], lhsT=qT[:, h, :], rhs=kcbT[:, kv, :], start=True, stop=True), reads=[b_qT, b_cb], writes=[bPS[2]])
            k.op(k.dve, lambda e: e.tensor_tensor(out=sc[:, 0:64], in0=PS[2][:, 0:64], in1=cst[:, 0:64], op=ALU.add), reads=[bPS[2], b_cst], writes=[b_sc])
            softmax_pv(h, 64, [(0, 64, vcb[:, kv * 64:(kv + 1) * 64])], 0)
            k.op(k.dve, lambda e: e.tensor_scalar(out=pc[:], in0=sc[:, 0:64], scalar1=st[:, 1:2], scalar2=None, op0=ALU.mult), reads=[b_sc, b_st], writes=[b_pc])
            if g == 0:
                k.op(k.dve, lambda e: e.tensor_reduce(out=imp[:, kv, :], in_=pc[:].rearrange("p (a c) -> p a c", c=2), axis=X, op=ALU.add), reads=[b_pc], writes=[b_imp])
            else:
                k.op(k.dve, lambda e: e.tensor_reduce(out=mx[:, 0:0 + 32] if False else impw[:, 0, :], in_=pc[:].rearrange("p (a c) -> p a c", c=2), axis=X, op=ALU.add),
                     reads=[b_pc], writes=[b_impw])
                k.op(k.dve, lambda e: e.tensor_tensor(out=imp[:, kv, :], in0=imp[:, kv, :], in1=impw[:, 0, :], op=ALU.add), reads=[b_impw], writes=[b_imp])
        k.op(k.dve, lambda e: e.tensor_tensor(out=imp[:], in0=imp[:], in1=cst[:, 64:96].unsqueeze(1).to_broadcast([128, 4, 32]), op=ALU.mult), reads=[b_cst], writes=[b_imp])
        k.op(k.dve, lambda e: e.tensor_tensor(out=imp[:], in0=imp[:], in1=cst[:, 96:128].unsqueeze(1).to_broadcast([128, 4, 32]), op=ALU.add), reads=[b_cst], writes=[b_imp])
        for kv in range(4):
            k.op(k.dve, lambda e: e.max(out=mx[:, 0:8], in_=imp[:, kv, :]), reads=[b_imp], writes=[b_mx])
            k.op(k.dve, lambda e: e.match_replace(out=impw[:, kv, :], in_to_replace=mx[:, 0:8], in_values=imp[:, kv, :], imm_value=-2.0), reads=[b_imp, b_mx], writes=[b_impw])
            k.op(k.dve, lambda e: e.max(out=mx[:, 8:16], in_=impw[:, kv, :]), reads=[b_impw], writes=[b_mx])
            k.op(k.dve, lambda e: e.tensor_scalar(out=mx[:, 15:16], in0=mx[:, 15:16], scalar1=0.0, scalar2=None, op0=ALU.max), writes=[b_mx])
            k.op(k.dve, lambda e: e.tensor_scalar(out=selb[:, kv, :], in0=imp[:, kv, :], scalar1=mx[:, 15:16], scalar2=None, op0=ALU.is_ge), reads=[b_imp, b_mx], writes=[b_selb])
        k.op(k.dve, lambda e: e.tensor_scalar(out=selb[:], in0=selb[:], scalar1=-1.0, scalar2=1.0e30, op0=ALU.add, op1=ALU.mult), writes=[b_selb])
        nk = (i + 1) * 128
        for h in range(16):
            kv = h // 4
            for c0 in range(0, nk, 512):
                w = min(512, nk - c0)
                pt, bpt = PS[(c0 // 512) % 2], bPS[(c0 // 512) % 2]
                k.op(k.pe, lambda e: e.matmul(pt[:, 0:w], lhsT=qT[:, h, :], rhs=KTs[:, kv, c0:c0 + w], start=True, stop=True), reads=[b_qT, b_KT], writes=[bpt])
                nb = w // 64
                k.op(k.dve, lambda e: e.tensor_tensor(out=sc[:, c0:c0 + w].rearrange("p (a c) -> p a c", c=64), in0=pt[:, 0:w].rearrange("p (a c) -> p a c", c=64),
                                                      in1=selb[:, kv, c0 // 64:c0 // 64 + nb].unsqueeze(2).to_broadcast([128, nb, 64]), op=ALU.add),
                     reads=[bpt, b_selb], writes=[b_sc])
            k.op(k.dve, lambda e: e.tensor_tensor(out=sc[:, nk - 128:nk], in0=sc[:, nk - 128:nk], in1=tri[:], op=ALU.add), reads=[b_c], writes=[b_sc])
            softmax_pv(h, nk, [(j * 128, 128, Vs[:, j, kv * 64:(kv + 1) * 64]) for j in range(i + 1)], 1)
            j0 = max(0, i - 4)
            wk = (i + 1 - j0) * 128
            for c0 in range(0, wk, 512):
                w = min(512, wk - c0)
                pt, bpt = PS[(c0 // 512) % 2], bPS[(c0 // 512) % 2]
                k.op(k.pe, lambda e: e.matmul(pt[:, 0:w], lhsT=qT[:, h, :], rhs=KTw[:, kv, j0 * 128 + c0:j0 * 128 + c0 + w], start=True, stop=True), reads=[b_qT, b_KT], writes=[bpt])
                k.op(k.act, lambda e: e.activation(out=sc[:, c0:c0 + w], in_=pt[:, 0:w], func=AF.Copy), reads=[bpt], writes=[b_sc])
            k.op(k.dve, lambda e: e.tensor_tensor(out=sc[:, wk - 128:wk], in0=sc[:, wk - 128:wk], in1=tri[:], op=ALU.add), reads=[b_c], writes=[b_sc])
            if i >= 4:
                k.op(k.dve, lambda e: e.tensor_tensor(out=sc[:, 0:128], in0=sc[:, 0:128], in1=bw4[:], op=ALU.add), reads=[b_c], writes=[b_sc])
            softmax_pv(h, wk, [(jj * 128, 128, Vw[:, j0 + jj, kv * 64:(kv + 1) * 64]) for jj in range(i + 1 - j0)], 2)
        g3 = gt[:].rearrange("p (h c) -> p h c", c=3)
        y3 = yo[:].rearrange("p (h c) -> p h c", h=16)
        k.op(k.dve, lambda e: e.tensor_tensor(out=y3, in0=oc[:, 0, :, :], in1=g3[:, :, 0:1].to_broadcast([128, 16, 64]), op=ALU.mult), reads=[b_oc, b_gt], writes=[b_yo])
        for s_ in (1, 2):
            k.op(k.dve, lambda e: e.tensor_tensor(out=oc[:, s_, :, :], in0=oc[:, s_, :, :], in1=g3[:, :, s_:s_ + 1].to_broadcast([128, 16, 64]), op=ALU.mult), reads=[b_gt], writes=[b_oc])
            k.op(k.dve, lambda e: e.tensor_tensor(out=y3, in0=y3, in1=oc[:, s_, :, :], op=ALU.add), reads=[b_oc], writes=[b_yo])
        k.dma(k.sp, ymix[i * 128:(i + 1) * 128, 1024:2048], yo[:], reads=[b_yo], writes=[b_ymix])
    return b_ymix
def linear_stage(k, nc, sb, ps, ident, b_ident, src, T, W, N, dst, tag, b_src=None):
    n = lambda s: "%s_%s" % (tag, s)
    TCH = 1024
    CB = 512
    xT = sb(n("xT"), [128, KC, TCH + 4], F32)
    xin = [sb(n("xin%d" % i), [128, D], F32) for i in range(2)]; b_xin = [Buf(), Buf()]
    wst = [sb(n("w%d" % i), [128, KC, CB], F32) for i in range(2)]; b_w = [Buf(), Buf()]
    po = [sb(n("po%d" % i), [128, CB], F32) for i in range(2)]; b_po = [Buf(), Buf()]
    tp = [ps(n("tp%d" % i), [128, 512], F32) for i in range(2)]; b_tp = [Buf(), Buf()]
    pp = [ps(n("pp%d" % i), [128, CB], F32) for i in range(2)]; b_pp = [Buf(), Buf()]
    b_dst = Buf()
    w_v = W.rearrange("(kc p) c -> p kc c", p=128)
    nblk = (N + CB - 1) // CB
    cnt = [0, 0, 0]
    for t0 in range(0, T, TCH):
        tn = min(TCH + 4, T - t0) if T - t0 <= TCH + 4 else TCH
        tiles = [(a, min(128, tn - a)) for a in range(0, tn, 128)]
        b_xT = [Buf() for _ in tiles]
        for ti, (a, rows) in enumerate(tiles):
            xb, bx = xin[cnt[0] % 2], b_xin[cnt[0] % 2]
            cnt[0] += 1
            k.dma(k.sp, xb[0:rows, :], src[t0 + a:t0 + a + rows, :], reads=([b_src] if b_src is not None else []), writes=[bx])
            for g in range(KC // 4):
                pt, bp = tp[cnt[1] % 2], b_tp[cnt[1] % 2]
                cnt[1] += 1
                for j in range(4):
                    kc = g * 4 + j
                    k.op(k.pe, lambda e: e.transpose(pt[:, j * 128:j * 128 + rows], xb[0:rows, kc * 128:(kc + 1) * 128], ident[0:rows, 0:rows]),
                         reads=[bx, b_ident], writes=[bp])
                dstv = xT[:, g * 4:(g + 1) * 4, a:a + rows]
                srcp = pt[:].rearrange("p (j c) -> p j c", j=4)[:, :, 0:rows]
                if g % 2 == 0:
                    k.op(k.dve, lambda e: e.tensor_copy(out=dstv, in_=srcp), reads=[bp], writes=[b_xT[ti]])
                else:
                    k.op(k.act, lambda e: e.activation(out=dstv, in_=srcp, func=AF.Copy), reads=[bp], writes=[b_xT[ti]])
        for bi in range(nblk):
            c0 = bi * CB
            cw = min(CB, N - c0)
            wb, bw = wst[bi % 2], b_w[bi % 2]
            k.dma(k.sp, wb[:, :, 0:cw], w_v[:, :, c0:c0 + cw], writes=[bw])
            for ti, (a, rows) in enumerate(tiles):
                pt, bp = pp[cnt[2] % 2], b_pp[cnt[2] % 2]
                ot, bo = po[cnt[2] % 2], b_po[cnt[2] % 2]
                cnt[2] += 1
                for kc in range(KC):
                    k.op(k.pe, lambda e: e.matmul(pt[0:rows, 0:cw], lhsT=xT[:, kc, a:a + rows], rhs=wb[:, kc, 0:cw], start=(kc == 0), stop=(kc == KC - 1)),
                         reads=[b_xT[ti], bw], writes=[bp])
                if cnt[2] % 2:
                    k.op(k.act, lambda e: e.activation(out=ot[0:rows, 0:cw], in_=pt[0:rows, 0:cw], func=AF.Copy), reads=[bp], writes=[bo])
                else:
                    k.op(k.dve, lambda e: e.tensor_copy(out=ot[0:rows, 0:cw], in_=pt[0:rows, 0:cw]), reads=[bp], writes=[bo])
                k.dma(k.sp, dst[t0 + a:t0 + a + rows, c0:c0 + cw], ot[0:rows, 0:cw], reads=[bo], writes=[b_dst])
        if tn > TCH:
            break
    return b_dst


def ln_stage(k, nc, sb, x, h, T, g_ap, b_ap, out, tag, extra_out=None, b_in=()):
    ALPHA = (2 * DEPTH) ** 0.25
    n = lambda s: "%s_%s" % (tag, s)
    X = mybir.AxisListType.X
    gb = sb(n("g"), [128, D], F32); bb = sb(n("b"), [128, D], F32); b_c = Buf()
    k.dma(k.sp, gb[:], g_ap.partition_broadcast(128), writes=[b_c])
    k.dma(k.sp, bb[:], b_ap.partition_broadcast(128), writes=[b_c])
    xt = [sb(n("x%d" % i), [128, D], F32) for i in range(2)]; b_x = [Buf(), Buf()]
    ht = [sb(n("h%d" % i), [128, D], F32) for i in range(2)]; b_h = [Buf(), Buf()]
    sq = sb(n("sq"), [128, D], F32); b_sq = Buf()
    st = sb(n("st"), [128, 4], F32); b_st = Buf()
    b_out = Buf()
    for it, r0 in enumerate(range(0, T, 128)):
        rows = min(128, T - r0)
        R = slice(0, rows)
        xx, bx = xt[it % 2], b_x[it % 2]
        hh, bh = ht[it % 2], b_h[it % 2]
        k.dma(k.sp, xx[R, :], x[r0:r0 + rows, :], reads=list(b_in), writes=[bx])
        k.dma(k.sp, hh[R, :], h[r0:r0 + rows, :], reads=list(b_in), writes=[bh])
        k.op(k.dve, lambda e: e.scalar_tensor_tensor(out=xx[R, :], in0=xx[R, :], scalar=ALPHA, in1=hh[R, :], op0=ALU.mult, op1=ALU.add), reads=[bh], writes=[bx])
        k.op(k.dve, lambda e: e.tensor_reduce(out=st[R, 0:1], in_=xx[R, :], axis=X, op=ALU.add), reads=[bx], writes=[b_st])
        k.op(k.dve, lambda e: e.tensor_scalar(out=st[R, 0:1], in0=st[R, 0:1], scalar1=-1.0 / D, scalar2=None, op0=ALU.mult), writes=[b_st])
        k.op(k.dve, lambda e: e.tensor_scalar(out=xx[R, :], in0=xx[R, :], scalar1=st[R, 0:1], scalar2=None, op0=ALU.add), reads=[b_st], writes=[bx])
        k.op(k.pool, lambda e: e.tensor_tensor(out=sq[R, :], in0=xx[R, :], in1=xx[R, :], op=ALU.mult), reads=[bx], writes=[b_sq])
        k.op(k.dve, lambda e: e.tensor_reduce(out=st[R, 1:2], in_=sq[R, :], axis=X, op=ALU.add), reads=[b_sq], writes=[b_st])
        k.op(k.dve, lambda e: e.tensor_scalar(out=st[R, 1:2], in0=st[R, 1:2], scalar1=1.0 / D, scalar2=1e-5, op0=ALU.mult, op1=ALU.add), writes=[b_st])
        k.op(k.act, lambda e: e.activation(out=st[R, 1:2], in_=st[R, 1:2], func=AF.Sqrt), writes=[b_st])
        k.op(k.dve, lambda e: e.reciprocal(out=st[R, 1:2], in_=st[R, 1:2]), writes=[b_st])
        k.op(k.dve, lambda e: e.scalar_tensor_tensor(out=xx[R, :], in0=xx[R, :], scalar=st[R, 1:2], in1=gb[R, :], op0=ALU.mult, op1=ALU.mult), reads=[b_st, b_c], writes=[bx])
        k.op(k.dve, lambda e: e.tensor_tensor(out=xx[R, :], in0=xx[R, :], in1=bb[R, :], op=ALU.add), reads=[b_c], writes=[bx])
        k.dma(k.sp, out[r0:r0 + rows, :], xx[R, :], reads=[bx], writes=[b_out])
        if extra_out is not None:
            k.dma(k.sp, extra_out[r0:r0 + rows, :], xx[R, :], reads=[bx], writes=[b_out])
    return b_out
def peer_stage(k, nc, sb, ps, ident, b_ident, x, q, T, subkeys, u_tab, v_tab, f_out, tag, iota256, b_in=(), row_base=0):
    n = lambda s: "%s_%s" % (tag, s)
    X = mybir.AxisListType.X
    U32 = mybir.dt.uint32
    PS = [ps(n("ps%d" % i), [128, 512], F32) for i in range(8)]; bPS = [Buf() for _ in range(8)]
    skT = sb(n("skT"), [128, 16, 128], F32); b_sk = Buf()
    skin = sb(n("skin"), [128, 16, 128], F32); b_skin = Buf()
    k.dma(k.sp, skin[:], subkeys.rearrange("h c key d -> key (h c) d"), writes=[b_skin])
    for g in range(4):
        for j in range(4):
            k.op(k.pe, lambda e: e.transpose(PS[g % 2][:, j * 128:(j + 1) * 128], skin[:, g * 4 + j, :], ident[:, :]), reads=[b_skin, b_ident], writes=[bPS[g % 2]])
        k.op(k.dve, lambda e: e.tensor_copy(out=skT[:, g * 4:(g + 1) * 4, :], in_=PS[g % 2][:].rearrange("p (a c) -> p a c", a=4)), reads=[bPS[g % 2]], writes=[b_sk])
    qin = sb(n("qin"), [128, D], F32); b_qin = Buf()
    qT = sb(n("qT"), [128, 16, 128], F32); b_qT = Buf()
    s = sb(n("s"), [128, 16, 128], F32); b_s = Buf()
    s2 = sb(n("s2"), [128, 128], F32); b_s2 = Buf()
    hv = sb(n("hv"), [128, 16, 16], F32); b_hv = Buf()
    hiu = sb(n("hiu"), [128, 16, 16], U32); b_hiu = Buf()
    hi = sb(n("hi"), [128, 16, 16], F32); b_hi = Buf()
    cand = sb(n("cand"), [128, 256], F32); b_cand = Buf()
    cand2 = sb(n("cand2"), [128, 256], F32); b_cand2 = Buf()
    eg = sb(n("eg"), [128, 256], F32); b_eg = Buf()
    junk = sb(n("junk"), [128, 256], F32); b_junk = Buf()
    tv = sb(n("tv"), [128, 8, 16], F32); b_tv = Buf()
    posu = sb(n("posu"), [128, 8, 16], U32); b_posu = Buf()
    posf = sb(n("posf"), [128, 8, 16], F32); b_posf = Buf()
    iot = sb(n("iot"), [128, 256], F32); b_iot = Buf()
    k.dma(k.sp, iot[:], iota256, writes=[b_iot])
    ids = sb(n("ids"), [128, 128], F32); b_ids = Buf()
    gts = sb(n("gts"), [128, 8, 16], F32); b_gts = Buf()
    st = sb(n("st"), [128, 16], F32); b_st = Buf()
    idTs = [sb(n("idT%d" % i), [128, 128], I32) for i in range(2)]; b_idTs = [Buf(), Buf()]
    gT = sb(n("gT"), [128, 128], F32); b_gT = Buf()
    hTs = [sb(n("hT%d" % i), [128, 128], F32) for i in range(2)]; b_hTs = [Buf(), Buf()]
    Ug = [sb(n("Ug%d" % i), [128, D], F32) for i in range(2)]; b_Ug = [Buf(), Buf()]
    Vg = [sb(n("Vg%d" % i), [128, D], F32) for i in range(2)]; b_Vg = [Buf(), Buf()]
    xb = [sb(n("xb%d" % i), [128, D], F32) for i in range(2)]; b_xb = [Buf(), Buf()]
    big = sb(n("big"), [128, D], F32); b_big = Buf()
    wv = [sb(n("wv%d" % i), [128, 1], F32) for i in range(2)]; b_wv = [Buf(), Buf()]
    orow = [sb(n("orow%d" % i), [1, D], F32) for i in range(2)]; b_orow = [Buf(), Buf()]
    b_f = Buf()
    cnt = {"u": 0, "v": 0}

    def pass1_tok(it, r0, t):
        idT, b_idT = idTs[it % 2], b_idTs[it % 2]
        hT, b_hT = hTs[it % 2], b_hTs[it % 2]
        ug, bu = Ug[cnt["u"] % 2], b_Ug[cnt["u"] % 2]
        xx, bx = xb[cnt["u"] % 2], b_xb[cnt["u"] % 2]
        cnt["u"] += 1
        k.dma(k.sp, xx[:], x[r0 + t:r0 + t + 1, :].partition_broadcast(128), reads=list(b_in), writes=[bx])
        gather(k, ug[:], u_tab, idT[:, t:t + 1], reads=[b_idT], writes=[bu])
        k.op(k.dve, lambda e: e.scalar_tensor_tensor(out=big[:], in0=ug[:], scalar=1.0, in1=xx[:], op0=ALU.mult, op1=ALU.mult, accum_out=hT[:, t:t + 1]),
             reads=[bu, bx], writes=[b_big, b_hT])

    def pass2_tok(it, r0, t):
        idT, b_idT = idTs[it % 2], b_idTs[it % 2]
        hT, b_hT = hTs[it % 2], b_hTs[it % 2]
        vg, bv = Vg[cnt["v"] % 2], b_Vg[cnt["v"] % 2]
        orw, bo = orow[cnt["v"] % 2], b_orow[cnt["v"] % 2]
        cnt["v"] += 1
        gather(k, vg[:], v_tab, idT[:, t:t + 1], reads=[b_idT], writes=[bv])
        for c in range(4):
            k.op(k.pe, lambda e: e.matmul(PSrow(PS, 0, c), lhsT=hT[:, t:t + 1], rhs=vg[:, c * 512:(c + 1) * 512], start=True, stop=True),
                 reads=[b_hT, bv], writes=[bPS[4 + c]])
            k.op(k.act, lambda e: e.activation(out=orw[0:1, c * 512:(c + 1) * 512], in_=PSrow(PS, 0, c), func=AF.Copy), reads=[bPS[4 + c]], writes=[bo])
        k.dma(k.sp, f_out[r0 + t:r0 + t + 1, :], orw[0:1, :], reads=[bo], writes=[b_f])

    prev = None
    for it, r0 in enumerate(range(0, T, 128)):
        rows = min(128, T - r0)
        R = slice(0, rows)
        idT, b_idT = idTs[it % 2], b_idTs[it % 2]
        hT, b_hT = hTs[it % 2], b_hTs[it % 2]
        k.dma(k.sp, qin[R, :], q[r0:r0 + rows, :], reads=list(b_in), writes=[b_qin])
        for g in range(4):
            for j in range(4):
                kc = g * 4 + j
                k.op(k.pe, lambda e: e.transpose(PS[g % 2][:, j * 128:j * 128 + rows], qin[R, kc * 128:(kc + 1) * 128], ident[R, R]), reads=[b_qin, b_ident], writes=[bPS[g % 2]])
            k.op(k.act, lambda e: e.activation(out=qT[:, g * 4:(g + 1) * 4, R], in_=PS[g % 2][:].rearrange("p (a c) -> p a c", a=4)[:, :, R], func=AF.Copy), reads=[bPS[g % 2]], writes=[b_qT])
        for g in range(4):
            for j in range(4):
                hc = g * 4 + j
                k.op(k.pe, lambda e: e.matmul(PS[2 + g % 2][R, j * 128:(j + 1) * 128], lhsT=qT[:, hc, R], rhs=skT[:, hc, :], start=True, stop=True), reads=[b_qT, b_sk], writes=[bPS[2 + g % 2]])
            k.op(k.dve, lambda e: e.tensor_copy(out=s[R, g * 4:(g + 1) * 4, :], in_=PS[2 + g % 2][R, :].rearrange("p (a c) -> p a c", a=4)), reads=[bPS[2 + g % 2]], writes=[b_s])
        for hc in range(16):
            k.op(k.dve, lambda e: e.max(out=hv[R, hc, 0:8], in_=s[R, hc, :]), reads=[b_s], writes=[b_hv])
            k.op(k.dve, lambda e: e.max_index(out=hiu[R, hc, 0:8], in_max=hv[R, hc, 0:8], in_values=s[R, hc, :]), reads=[b_s, b_hv], writes=[b_hiu])
            k.op(k.dve, lambda e: e.match_replace(out=s2[R, :], in_to_replace=hv[R, hc, 0:8], in_values=s[R, hc, :], imm_value=-1.0e30), reads=[b_s, b_hv], writes=[b_s2])
            k.op(k.dve, lambda e: e.max(out=hv[R, hc, 8:16], in_=s2[R, :]), reads=[b_s2], writes=[b_hv])
            k.op(k.dve, lambda e: e.max_index(out=hiu[R, hc, 8:16], in_max=hv[R, hc, 8:16], in_values=s2[R, :]), reads=[b_s2, b_hv], writes=[b_hiu])
        k.op(k.dve, lambda e: e.tensor_copy(out=hi[R], in_=hiu[R]), reads=[b_hiu], writes=[b_hi])
        for h in range(8):
            c3 = cand[R, :].rearrange("p (a b) -> p a b", a=16)
            k.op(k.dve, lambda e: e.tensor_tensor(out=c3, in0=hv[R, 2 * h, :].unsqueeze(2).to_broadcast([rows, 16, 16]),
                                                  in1=hv[R, 2 * h + 1, :].unsqueeze(1).to_broadcast([rows, 16, 16]), op=ALU.add), reads=[b_hv], writes=[b_cand])
            k.op(k.dve, lambda e: e.scalar_tensor_tensor(out=eg[R, :].rearrange("p (a b) -> p a b", a=16), in0=hi[R, 2 * h, :].unsqueeze(2).to_broadcast([rows, 16, 16]), scalar=128.0,
                                                         in1=hi[R, 2 * h + 1, :].unsqueeze(1).to_broadcast([rows, 16, 16]), op0=ALU.mult, op1=ALU.add), reads=[b_hi], writes=[b_eg])
            k.op(k.dve, lambda e: e.max(out=tv[R, h, 0:8], in_=cand[R, :]), reads=[b_cand], writes=[b_tv])
            k.op(k.dve, lambda e: e.max_index(out=posu[R, h, 0:8], in_max=tv[R, h, 0:8], in_values=cand[R, :]), reads=[b_cand, b_tv], writes=[b_posu])
            k.op(k.dve, lambda e: e.match_replace(out=cand2[R, :], in_to_replace=tv[R, h, 0:8], in_values=cand[R, :], imm_value=-1.0e30), reads=[b_cand, b_tv], writes=[b_cand2])
            k.op(k.dve, lambda e: e.max(out=tv[R, h, 8:16], in_=cand2[R, :]), reads=[b_cand2], writes=[b_tv])
            k.op(k.dve, lambda e: e.max_index(out=posu[R, h, 8:16], in_max=tv[R, h, 8:16], in_values=cand2[R, :]), reads=[b_cand2, b_tv], writes=[b_posu])
            k.op(k.dve, lambda e: e.tensor_copy(out=posf[R, h, :], in_=posu[R, h, :]), reads=[b_posu], writes=[b_posf])
            for kk_ in range(16):
                k.op(k.dve, lambda e: e.scalar_tensor_tensor(out=junk[R, :], in0=iot[R, :], scalar=posf[R, h, kk_:kk_ + 1], in1=eg[R, :], op0=ALU.is_equal, op1=ALU.mult,
                                                             accum_out=ids[R, h * 16 + kk_:h * 16 + kk_ + 1]), reads=[b_iot, b_posf, b_eg], writes=[b_junk, b_ids])
        k.op(k.dve, lambda e: e.tensor_tensor(out=gts[R], in0=tv[R], in1=tv[R, :, 0:1].to_broadcast([rows, 8, 16]), op=ALU.subtract), reads=[b_tv], writes=[b_gts])
        k.op(k.act, lambda e: e.activation(out=gts[R], in_=gts[R], func=AF.Exp), writes=[b_gts])
        k.op(k.dve, lambda e: e.tensor_reduce(out=st[R, 0:8], in_=gts[R], axis=X, op=ALU.add), reads=[b_gts], writes=[b_st])
        k.op(k.dve, lambda e: e.reciprocal(out=st[R, 0:8], in_=st[R, 0:8]), writes=[b_st])
        k.op(k.dve, lambda e: e.tensor_tensor(out=gts[R], in0=gts[R], in1=st[R, 0:8].unsqueeze(2).to_broadcast([rows, 8, 16]), op=ALU.mult), reads=[b_st], writes=[b_gts])
        k.op(k.pe, lambda e: e.transpose(PS[4][:, 0:rows], ids[R, :], ident[R, R]), reads=[b_ids, b_ident], writes=[bPS[4]])
        k.op(k.dve, lambda e: e.tensor_scalar(out=idT[:, R], in0=PS[4][:, 0:rows], scalar1=float(row_base), scalar2=None, op0=ALU.add), reads=[bPS[4]], writes=[b_idT])
        k.op(k.pe, lambda e: e.transpose(PS[5][:, 0:rows], gts[R].rearrange("p a b -> p (a b)"), ident[R, R]), reads=[b_gts, b_ident], writes=[bPS[5]])
        k.op(k.act, lambda e: e.activation(out=gT[:, R], in_=PS[5][:, 0:rows], func=AF.Copy), reads=[bPS[5]], writes=[b_gT])
        n2 = prev[2] if prev is not None else 0
        for t in range(max(rows, n2)):
            if t < rows:
                pass1_tok(it, r0, t)
            if t < n2:
                pass2_tok(prev[0], prev[1], t)
        k.op(k.act, lambda e: e.activation(out=hT[:, R], in_=hT[:, R], func=AF.Gelu), writes=[b_hT])
        k.op(k.dve, lambda e: e.tensor_tensor(out=hT[:, R], in0=hT[:, R], in1=gT[:, R], op=ALU.mult), reads=[b_gT], writes=[b_hT])
        prev = (it, r0, rows)
    for t in range(prev[2]):
        pass2_tok(prev[0], prev[1], t)
    return b_f


def PSrow(PS, tok, c):
    return PS[4 + c][0:1, 0:512]


def gather(k, out_ap, table, idx_ap, reads, writes):
    Q = k.pool
    ds = k.dsems[k.dnext]
    k.dnext = (k.dnext + 1) % len(k.dsems)
    toks = k._deps(reads, writes)
    if ds.val:
        toks.append((ds.h, ds.val, ds.key))
    last = k._waits(Q, toks)
    inst = Q.eng.indirect_dma_start(out=out_ap, out_offset=None, in_=table, in_offset=bass.IndirectOffsetOnAxis(ap=idx_ap, axis=0))
    if last is not None:
        inst.wait_op(last[0], last[1], "sem-ge")
    ds.val += 16
    inst.then_inc(ds.h, 16)
    k._mark((ds.h, ds.val, ds.key), reads, writes)
    return inst
_UID = itertools.count()


def nsas_host_consts(P):
    nsel = P // 64 + 1
    r = np.arange(16)
    tok = r % 4
    sel16 = (tok[:, None] == np.arange(4)[None, :]).astype(np.float32)
    biasN = np.where(np.arange(4)[None, :] <= tok[:, None], 0.0, NEG).astype(np.float32)
    biasW = np.zeros((16, 516), np.float32)
    biasW[:, 0:512] = np.where(np.arange(512)[None, :] >= tok[:, None] + 1, 0.0, NEG)
    biasW[:, 512:516] = biasN
    blk = np.arange(nsel)
    forced = (blk == 0) | (blk == nsel - 1) | (blk == nsel - 2)
    FM = np.tile(np.where(forced, 0.0, 1.0).astype(np.float32)[None, :], (4, 1))
    FA = np.tile(np.where(forced, 1.0e4, 0.0).astype(np.float32)[None, :], (4, 1))
    return {"ns_iota": np.arange(128, dtype=np.float32)[:, None].copy(), "ns_sel16": sel16, "ns_selT": sel16.T.copy(),
            "ns_biasN": biasN, "ns_biasW": biasW, "ns_FM": FM, "ns_FA": FA}


def nsa_sample_stage(k, nc, sb, ps, ident, b_ident, proj, row0, P, cmp_w, cache_cmp, cache_sel, page_row, swin, CN, CS,
                     past_cmp, past_sel, osc, ymix, tag, b_in=(), row_base=0):
    NP = P // 128
    NB = P // 32
    NSEL = P // 64 + 1
    X = mybir.AxisListType.X
    n = lambda s: "%s_%s" % (tag, s)
    PS = [ps(n("ps%d" % i), [128, 512], F32) for i in range(8)]; bPS = [Buf() for _ in range(8)]
    b_c = Buf()
    def cload(name, shape, src, dt=F32):
        t = sb(n(name), shape, dt)
        k.dma(k.sp, t[:], src, writes=[b_c])
        return t
    iota = cload("iota", [128, 1], CS["ns_iota"]); mask4 = cload("mask4", [128, 4], CN["nsa_mask4"])
    sel16 = cload("sel16", [16, 4], CS["ns_sel16"]); selT = cload("selT", [4, 16], CS["ns_selT"])
    biasN = cload("biasN", [16, 4], CS["ns_biasN"]); biasW = cload("biasW", [16, 516], CS["ns_biasW"])
    FM = cload("FM", [4, NSEL], CS["ns_FM"]); FA = cload("FA", [4, NSEL], CS["ns_FA"])
    wcol = sb(n("wcol"), [128, 2], F32); b_wcol = Buf()
    for c in range(2):
        for j in range(4):
            k.dma(k.sp, wcol[j * 32:(j + 1) * 32, c:c + 1], cmp_w[c:c + 1, :].rearrange("o l -> l o"), writes=[b_wcol])
    Wb = sb(n("Wb"), [128, 4], F32); Ww = sb(n("Ww"), [128, 252], F32); b_W = Buf()
    k.op(k.dve, lambda e: e.memset(Ww[:], 0.0), writes=[b_W])
    k.op(k.dve, lambda e: e.tensor_scalar(out=Wb[:], in0=mask4[:], scalar1=wcol[:, 0:1], scalar2=None, op0=ALU.mult), reads=[b_c, b_wcol], writes=[b_W])
    k.op(k.dve, lambda e: e.tensor_scalar(out=Ww[:, 124:128], in0=mask4[:], scalar1=wcol[:, 1:2], scalar2=None, op0=ALU.mult), reads=[b_c, b_wcol], writes=[b_W])
    ptb = sb(n("ptb"), [128, NP], I32); ptf = sb(n("ptf"), [128, NP], F32); idx = sb(n("idx"), [128, NP], I32); b_idx = Buf()
    k.dma(k.sp, ptb[:], page_row.partition_broadcast(128), writes=[b_idx])
    k.op(k.dve, lambda e: e.tensor_copy(out=ptf[:], in_=ptb[:]), writes=[b_idx])
    k.op(k.dve, lambda e: e.tensor_scalar(out=ptf[:], in0=ptf[:], scalar1=128.0, scalar2=iota[:, 0:1], op0=ALU.mult, op1=ALU.add), reads=[b_c], writes=[b_idx])
    k.op(k.dve, lambda e: e.tensor_scalar(out=idx[:], in0=ptf[:], scalar1=float(row_base), scalar2=None, op0=ALU.add), writes=[b_idx])
    qin = sb(n("qin"), [T_S, 1024], F32); b_qin = Buf()
    kvn = sb(n("kvn"), [T_S, 1536], F32); b_kvn = Buf()
    qT = sb(n("qT"), [64, 4, 16], F32); b_qT = Buf()
    kTn = sb(n("kTn"), [64, 2, 4, T_S], F32); b_kTn = Buf()
    st = sb(n("st"), [16, 4], F32); b_st = Buf()
    impv = sb(n("impv"), [4, NSEL], F32); impw = sb(n("impw"), [4, NSEL], F32); b_imp = Buf()
    mx = sb(n("mx"), [4, 16], F32); b_mx = Buf()
    s01 = sb(n("s01"), [4, NSEL], F32); b_s01 = Buf()
    selb = sb(n("selb"), [16, 4, NSEL], F32); b_selb = Buf()
    eT = [sb(n("eT%d" % i), [128, 16], F32) for i in range(2)]; b_eT = [Buf(), Buf()]
    ob = sb(n("ob"), [16, 64], F32); b_ob = Buf()
    oc = sb(n("oc"), [T_S, 3, 16, 64], F32); b_oc = Buf()
    gin = sb(n("gin"), [T_S, 48], F32); b_g = Buf()
    yo = sb(n("yo"), [T_S, 1024], F32); b_yo = Buf()
    phase = contextlib.ExitStack()
    sbx = lambda name, shape, dt: phase.enter_context(nc.sbuf_tensor(n(name) + "_x%d" % next(_UID), shape, dt))
    pg = [sbx("pg%d" % i, [128, 512], F32) for i in range(4)]; b_pg = [Buf() for _ in range(4)]
    b_past = Buf()
    for s_ in range(NP):
        for ci, (cache, past) in enumerate(((cache_cmp, past_cmp), (cache_sel, past_sel))):
            t_, bt = pg[(2 * s_ + ci) % 4], b_pg[(2 * s_ + ci) % 4]
            gather(k, t_[:], cache, idx[:, s_:s_ + 1], reads=[b_idx], writes=[bt])
            k.dma(k.sp, past[s_ * 128:(s_ + 1) * 128, :], t_[:], reads=[bt], writes=[b_past])
    k.barrier()
    phase.close()
    phase = contextlib.ExitStack()
    k.dma(k.sp, qin[:], proj[row0:row0 + T_S, NSA0:NSA0 + 1024], reads=list(b_in), writes=[b_qin])
    k.dma(k.sp, kvn[:], proj[row0:row0 + T_S, KV0:KV0 + 1536], reads=list(b_in), writes=[b_kvn])
    for h in range(16):
        k.op(k.pe, lambda e: e.transpose(PS[0][0:64, h * 4:h * 4 + 4], qin[:, h * 64:(h + 1) * 64], ident[0:T_S, 0:T_S]), reads=[b_qin, b_ident], writes=[bPS[0]])
    k.op(k.act, lambda e: e.activation(out=qT[:].rearrange("p a b -> p (a b)"), in_=PS[0][0:64, 0:64], func=AF.Copy, scale=0.125), reads=[bPS[0]], writes=[b_qT])
    for bi, off in enumerate((512, 1024)):
        for kv in range(4):
            k.op(k.pe, lambda e: e.transpose(PS[1][0:64, (bi * 4 + kv) * 4:(bi * 4 + kv) * 4 + 4], kvn[:, off + kv * 64:off + (kv + 1) * 64], ident[0:T_S, 0:T_S]),
                 reads=[b_kvn, b_ident], writes=[bPS[1]])
    k.op(k.dve, lambda e: e.tensor_copy(out=kTn[:].rearrange("p a b c -> p (a b c)"), in_=PS[1][0:64, 0:32]), reads=[bPS[1]], writes=[b_kTn])
    kcbT = sbx("kcbT", [64, 4, NB], F32); b_kcb = Buf()
    nvt = (NB + 127) // 128
    vcb = sbx("vcb", [128, nvt, 256], F32); b_vcb = Buf()
    pin = [sbx("pin%d" % i, [128, 512], F32) for i in range(2)]; b_pin = [Buf(), Buf()]
    for s_ in range(NP):
        pt_, bp = pin[s_ % 2], b_pin[s_ % 2]
        k.dma(k.sp, pt_[:], past_cmp[s_ * 128:(s_ + 1) * 128, :], reads=[b_past], writes=[bp])
        for kv in range(4):
            k.op(k.pe, lambda e: e.matmul(PS[kv][0:64, 4 * s_:4 * s_ + 4], lhsT=pt_[:, kv * 64:(kv + 1) * 64], rhs=Wb[:, :], start=True, stop=True, skip_group_check=True),
                 reads=[bp, b_W], writes=[bPS[kv]])
        vt_, off = s_ // 32, 4 * (s_ % 32)
        k.op(k.pe, lambda e: e.matmul(PS[4 + vt_ % 4][:, 0:256], lhsT=Ww[:, 124 - off:252 - off], rhs=pt_[:, 256:512], start=(s_ % 32 == 0), stop=(s_ % 32 == 31 or s_ == NP - 1)),
             reads=[bp, b_W], writes=[bPS[4 + vt_ % 4]])
    for kv in range(4):
        k.op(k.dve, lambda e: e.tensor_copy(out=kcbT[:, kv, :], in_=PS[kv][0:64, 0:NB]), reads=[bPS[kv]], writes=[b_kcb])
    for vt_ in range(nvt):
        k.op(k.act, lambda e: e.activation(out=vcb[:, vt_, :], in_=PS[4 + vt_ % 4][:, 0:256], func=AF.Copy), reads=[bPS[4 + vt_ % 4]], writes=[b_vcb])
    wst = sbx("wst", [128, 4, 512], F32); b_wst = Buf()
    k.dma(k.sp, wst[:], swin.rearrange("(a p) c -> p a c", p=128), writes=[b_wst])
    KTw = sbx("KTw", [64, 4, 516], F32); b_KTw = Buf()
    for a in range(4):
        for kv in range(4):
            k.op(k.pe, lambda e: e.transpose(PS[a % 2][0:64, kv * 128:(kv + 1) * 128], wst[:, a, kv * 64:(kv + 1) * 64], ident[:, :]), reads=[b_wst, b_ident], writes=[bPS[a % 2]])
        k.op(k.dve, lambda e: e.tensor_copy(out=KTw[:, :, a * 128:(a + 1) * 128], in_=PS[a % 2][0:64, :].rearrange("p (a c) -> p a c", a=4)), reads=[bPS[a % 2]], writes=[b_KTw])
    k.op(k.dve, lambda e: e.tensor_copy(out=KTw[:, :, 512:516], in_=kTn[:, 1, :, :]), reads=[b_kTn], writes=[b_KTw])
    scB = sbx("scB", [16, max(NB, 516)], F32)
    sc = scB; b_sc = Buf()
    pc = sbx("pc", [16, NB], F32); b_pc = Buf()
    b_osc = Buf()
    net = [0]

    def softmax_pv(ncols, vtiles, slot, kv):
        k.op(k.dve, lambda e: e.tensor_reduce(out=st[:, 0:1], in_=sc[:, 0:ncols], axis=X, op=ALU.max), reads=[b_sc], writes=[b_st])
        k.op(k.dve, lambda e: e.tensor_scalar(out=st[:, 0:1], in0=st[:, 0:1], scalar1=-1.0e20, scalar2=-1.0, op0=ALU.max, op1=ALU.mult), writes=[b_st])
        k.op(k.act, lambda e: e.activation(out=sc[:, 0:ncols], in_=sc[:, 0:ncols], func=AF.Exp, bias=st[:, 0:1], scale=1.0), reads=[b_st], writes=[b_sc])
        k.op(k.dve, lambda e: e.tensor_reduce(out=st[:, 1:2], in_=sc[:, 0:ncols], axis=X, op=ALU.add), reads=[b_sc], writes=[b_st])
        k.op(k.dve, lambda e: e.tensor_scalar(out=st[:, 1:2], in0=st[:, 1:2], scalar1=1.0e-30, scalar2=None, op0=ALU.max), writes=[b_st])
        k.op(k.dve, lambda e: e.reciprocal(out=st[:, 1:2], in_=st[:, 1:2]), writes=[b_st])
        for vi, (c0, w, vap, bv) in enumerate(vtiles):
            pt, bpt = PS[4 + net[0] % 2], bPS[4 + net[0] % 2]
            et, bet = eT[net[0] % 2], b_eT[net[0] % 2]
            net[0] += 1
            k.op(k.pe, lambda e: e.transpose(pt[0:w, 0:16], sc[:, c0:c0 + w], ident[0:16, 0:16]), reads=[b_sc, b_ident], writes=[bpt])
            k.op(k.act, lambda e: e.activation(out=et[0:w, :], in_=pt[0:w, 0:16], func=AF.Copy), reads=[bpt], writes=[bet])
            k.op(k.pe, lambda e: e.matmul(PS[6][0:16, 0:64], lhsT=et[0:w, :], rhs=vap, start=(vi == 0), stop=(vi == len(vtiles) - 1)),
                 reads=[bet, bv], writes=[bPS[6]])
        k.op(k.dve, lambda e: e.tensor_scalar(out=ob[:], in0=PS[6][0:16, 0:64], scalar1=st[:, 1:2], scalar2=None, op0=ALU.mult), reads=[bPS[6], b_st], writes=[b_ob])
        for g_ in range(4):
            k.dma(k.sp, osc[slot, :, kv * 4 + g_, :], ob[g_ * 4:(g_ + 1) * 4, :], reads=[b_ob], writes=[b_osc])

    for kv in range(4):
        q_kv = qT[:, kv, :]
        for c0 in range(0, NB, 512):
            w = min(512, NB - c0)
            k.op(k.pe, lambda e: e.matmul(PS[0][0:16, 0:w], lhsT=q_kv, rhs=kcbT[:, kv, c0:c0 + w], start=True, stop=True), reads=[b_qT, b_kcb], writes=[bPS[0]])
            k.op(k.act, lambda e: e.activation(out=sc[:, c0:c0 + w], in_=PS[0][0:16, 0:w], func=AF.Copy), reads=[bPS[0]], writes=[b_sc])
        softmax_pv(NB, [(j * 128, min(128, NB - j * 128), vcb[0:min(128, NB - j * 128), j, kv * 64:(kv + 1) * 64], b_vcb) for j in range(nvt)], 0, kv)
        k.op(k.dve, lambda e: e.tensor_scalar(out=pc[:], in0=sc[:, 0:NB], scalar1=st[:, 1:2], scalar2=None, op0=ALU.mult), reads=[b_sc, b_st], writes=[b_pc])
        for c0 in range(0, NB, 512):
            w = min(512, NB - c0)
            k.op(k.pe, lambda e: e.matmul(PS[1][0:4, 0:w], lhsT=sel16[:, :], rhs=pc[:, c0:c0 + w], start=True, stop=True), reads=[b_pc, b_c], writes=[bPS[1]])
            k.op(k.dve, lambda e: e.tensor_reduce(out=impv[:, c0 // 2:(c0 + w) // 2], in_=PS[1][0:4, 0:w].rearrange("p (a c) -> p a c", c=2), axis=X, op=ALU.add), reads=[bPS[1]], writes=[b_imp])
        k.op(k.dve, lambda e: e.memset(impv[:, NSEL - 1:NSEL], 0.0), writes=[b_imp])
        k.op(k.dve, lambda e: e.tensor_tensor(out=impv[:], in0=impv[:], in1=FM[:], op=ALU.mult), reads=[b_c], writes=[b_imp])
        k.op(k.dve, lambda e: e.tensor_tensor(out=impv[:], in0=impv[:], in1=FA[:], op=ALU.add), reads=[b_c], writes=[b_imp])
        k.op(k.dve, lambda e: e.max(out=mx[:, 0:8], in_=impv[:]), reads=[b_imp], writes=[b_mx])
        k.op(k.dve, lambda e: e.match_replace(out=impw[:], in_to_replace=mx[:, 0:8], in_values=impv[:], imm_value=-2.0), reads=[b_mx], writes=[b_imp])
        k.op(k.dve, lambda e: e.max(out=mx[:, 8:16], in_=impw[:]), reads=[b_imp], writes=[b_mx])
        k.op(k.dve, lambda e: e.tensor_scalar(out=mx[:, 15:16], in0=mx[:, 15:16], scalar1=0.0, scalar2=None, op0=ALU.max), writes=[b_mx])
        k.op(k.dve, lambda e: e.tensor_scalar(out=s01[:], in0=impv[:], scalar1=mx[:, 15:16], scalar2=None, op0=ALU.is_ge), reads=[b_imp, b_mx], writes=[b_s01])
        k.op(k.pe, lambda e: e.matmul(PS[1][0:16, 0:NSEL], lhsT=selT[:, :], rhs=s01[:, :], start=True, stop=True), reads=[b_s01, b_c], writes=[bPS[1]])
        k.op(k.dve, lambda e: e.tensor_scalar(out=selb[:, kv, :], in0=PS[1][0:16, 0:NSEL], scalar1=-1.0, scalar2=1.0e30, op0=ALU.add, op1=ALU.mult), reads=[bPS[1]], writes=[b_selb])
        for c0 in range(0, 516, 512):
            w = min(512, 516 - c0)
            k.op(k.pe, lambda e: e.matmul(PS[0][0:16, 0:w], lhsT=q_kv, rhs=KTw[:, kv, c0:c0 + w], start=True, stop=True), reads=[b_qT, b_KTw], writes=[bPS[0]])
            k.op(k.dve, lambda e: e.tensor_tensor(out=sc[:, c0:c0 + w], in0=PS[0][0:16, 0:w], in1=biasW[:, c0:c0 + w], op=ALU.add), reads=[bPS[0], b_c], writes=[b_sc])
        vt = [(a * 128, 128, wst[:, a, 256 + kv * 64:256 + (kv + 1) * 64], b_wst) for a in range(4)] + [(512, T_S, kvn[:, 1280 + kv * 64:1280 + (kv + 1) * 64], b_kvn)]
        softmax_pv(516, vt, 2, kv)
    k.barrier()
    phase.close()
    phase = contextlib.ExitStack()
    KTs = sbx("KTs", [64, P + T_S], F32); b_KTs = Buf()
    Vsp = sbx("Vsp", [128, NP, 64], F32); b_Vsp = Buf()
    scC = sbx("scC", [16, P + 64], F32)
    sc = scC; b_sc = Buf()
    pin = [sbx("pinC%d" % i, [128, 64], F32) for i in range(2)]; b_pin = [Buf(), Buf()]
    for kv in range(4):
        q_kv = qT[:, kv, :]
        for s_ in range(NP):
            pt_, bp = pin[s_ % 2], b_pin[s_ % 2]
            k.dma(k.sp, pt_[:, 0:64], past_sel[s_ * 128:(s_ + 1) * 128, kv * 64:(kv + 1) * 64], reads=[b_past], writes=[bp])
            k.dma(k.sp, Vsp[:, s_, :], past_sel[s_ * 128:(s_ + 1) * 128, 256 + kv * 64:256 + (kv + 1) * 64], reads=[b_past], writes=[b_Vsp])
            pp_, bpp = PS[2 + (s_ // 4) % 2], bPS[2 + (s_ // 4) % 2]
            k.op(k.pe, lambda e: e.transpose(pp_[0:64, (s_ % 4) * 128:(s_ % 4 + 1) * 128], pt_[:, 0:64], ident[:, :]), reads=[bp, b_ident], writes=[bpp])
            if s_ % 4 == 3 or s_ == NP - 1:
                a0 = (s_ // 4) * 4
                wd = (s_ - a0 + 1) * 128
                k.op(k.dve, lambda e: e.tensor_copy(out=KTs[:, a0 * 128:a0 * 128 + wd], in_=pp_[0:64, 0:wd]), reads=[bpp], writes=[b_KTs])
        k.op(k.dve, lambda e: e.tensor_copy(out=KTs[:, P:P + T_S], in_=kTn[:, 0, kv, :]), reads=[b_kTn], writes=[b_KTs])
        for c0 in range(0, P, 512):
            w = min(512, P - c0)
            pt, bpt = PS[(c0 // 512) % 2], bPS[(c0 // 512) % 2]
            k.op(k.pe, lambda e: e.matmul(pt[0:16, 0:w], lhsT=q_kv, rhs=KTs[:, c0:c0 + w], start=True, stop=True), reads=[b_qT, b_KTs], writes=[bpt])
            nb = w // 64
            k.op(k.dve, lambda e: e.tensor_tensor(out=sc[:, c0:c0 + w].rearrange("p (a c) -> p a c", c=64), in0=pt[0:16, 0:w].rearrange("p (a c) -> p a c", c=64),
                                                  in1=selb[:, kv, c0 // 64:c0 // 64 + nb].unsqueeze(2).to_broadcast([16, nb, 64]), op=ALU.add), reads=[bpt, b_selb], writes=[b_sc])
        k.op(k.pe, lambda e: e.matmul(PS[0][0:16, 0:T_S], lhsT=q_kv, rhs=KTs[:, P:P + T_S], start=True, stop=True), reads=[b_qT, b_KTs], writes=[bPS[0]])
        k.op(k.dve, lambda e: e.tensor_tensor(out=sc[:, P:P + T_S], in0=PS[0][0:16, 0:T_S], in1=biasN[:, :], op=ALU.add), reads=[bPS[0], b_c], writes=[b_sc])
        vt = [(s_ * 128, 128, Vsp[:, s_, :], b_Vsp) for s_ in range(NP)] + [(P, T_S, kvn[:, 768 + kv * 64:768 + (kv + 1) * 64], b_kvn)]
        softmax_pv(P + T_S, vt, 1, kv)
    k.barrier()
    phase.close()
    k.dma(k.sp, oc[:], osc.rearrange("s t h c -> t s h c"), reads=[b_osc], writes=[b_oc])
    k.dma(k.sp, gin[:], proj[row0:row0 + T_S, KV0 + 1536:KV0 + 1584], reads=list(b_in), writes=[b_g])
    k.op(k.act, lambda e: e.activation(out=gin[:], in_=gin[:], func=AF.Sigmoid), writes=[b_g])
    g3 = gin[:].rearrange("p (h c) -> p h c", c=3)
    y3 = yo[:].rearrange("p (h c) -> p h c", h=16)
    k.op(k.dve, lambda e: e.tensor_tensor(out=y3, in0=oc[:, 0, :, :], in1=g3[:, :, 0:1].to_broadcast([T_S, 16, 64]), op=ALU.mult), reads=[b_oc, b_g], writes=[b_yo])
    for s_ in (1, 2):
        k.op(k.dve, lambda e: e.tensor_tensor(out=oc[:, s_, :, :], in0=oc[:, s_, :, :], in1=g3[:, :, s_:s_ + 1].to_broadcast([T_S, 16, 64]), op=ALU.mult), reads=[b_g], writes=[b_oc])
        k.op(k.dve, lambda e: e.tensor_tensor(out=y3, in0=y3, in1=oc[:, s_, :, :], op=ALU.add), reads=[b_oc], writes=[b_yo])
    b_y = Buf()
    k.dma(k.sp, ymix[:, 1024:2048], yo[:], reads=[b_yo], writes=[b_y])
    return b_y
NCU = 4
RW_NAMES = ["rwkv_mu", "rwkv_w0", "rwkv_w2", "rwkv_a0", "rwkv_a2", "rwkv_g2", "rwkv_k_k", "rwkv_k_a", "rwkv_r_k", "rwkv_gn_g", "rwkv_gn_b"]


def build_nc(T_Pc=T_P, P=16384, n_pool=1280, NS=2):
    nc = bass.Bass("TRN2", target_bir_lowering=False)
    NP = P // 128
    NTOK = T_Pc + NS * T_S
    CNH = nsa_host_consts()
    CSH = nsas_host_consts(P)
    din = lambda name, shape, d=F32: nc.dram_tensor(name, list(shape), d, kind="ExternalInput").ap()
    dout = lambda name, shape: nc.dram_tensor(name, list(shape), F32, kind="ExternalOutput").ap()
    dscr = lambda name, shape: nc.dram_tensor(name, list(shape), F32, kind="Internal").ap()
    xp = din("xp", [T_Pc, D]); xs = din("xs", [NS * T_S, D])
    w_in = din("w_in", [DEPTH, D, IN_COLS]); w_out = din("w_out", [DEPTH, D, D]); wq = din("wq", [DEPTH, D, D])
    skeys = din("skeys", [DEPTH, 8, 2, 128, 128]); pu = din("pu", [DEPTH, 16384, D]); pv = din("pv", [DEPTH, 16384, D])
    lnp = {nm: din(nm, [DEPTH, 1, D]) for nm in ("ln1_g", "ln1_b", "ln2_g", "ln2_b")}
    rwp = {}
    for nm in RW_NAMES:
        shp = {"rwkv_mu": [1, RW], "rwkv_w2": [64, 1024], "rwkv_a2": [64, 1024], "rwkv_g2": [160, 1024]}.get(nm, [1, 1024])
        rwp[nm] = din(nm, [DEPTH] + shp)
    cmpw = din("cmp_w", [DEPTH, 2, 32])
    cc = din("cc", [DEPTH, n_pool * 128, 512]); cs = din("cs", [DEPTH, n_pool * 128, 512])
    ptab = din("ptab", [NS, 1, NP], I32)
    swin = din("swin", [DEPTH, NS, 512, 512]); srw = din("srw", [DEPTH, NS, 16, 64, 64]); ssh = din("ssh", [DEPTH, NS, 1, RW])
    zrow = din("zrow", [1, RW]); ident_d = din("ident", [128, 128]); iota256 = din("iota256", [128, 256])
    CN = {n_: din(n_, v.shape) for n_, v in CNH.items()}
    CS = {n_: din(n_, v.shape) for n_, v in CSH.items()}
    y_p = dout("y_p", [T_Pc, D]); y_s = dout("y_s", [NS * T_S, D])
    cmp_p = dout("cmp_p", [DEPTH, T_Pc, 512]); sel_p = dout("sel_p", [DEPTH, T_Pc, 512]); win_p = dout("win_p", [DEPTH, 512, 512])
    rw_p = dout("rw_p", [DEPTH, 16, 64, 64]); sh_p = dout("sh_p", [DEPTH, 1, RW])
    cmp_s = dout("cmp_s", [DEPTH, NS, T_S, 512]); sel_s = dout("sel_s", [DEPTH, NS, T_S, 512]); win_s = dout("win_s", [DEPTH, NS, 512, 512])
    rw_s = dout("rw_s", [DEPTH, NS, 16, 64, 64]); sh_s = dout("sh_s", [DEPTH, NS, 1, RW])
    Xa = dscr("Xa", [NTOK, D]); X1 = dscr("X1", [NTOK, D]); Hs = dscr("Hs", [NTOK, D]); Qs = dscr("Qs", [NTOK, D]); Fs = dscr("Fs", [NTOK, D])
    proj = dscr("proj", [NTOK, IN_COLS]); ymix = dscr("ymix", [NTOK, D]); opsd = dscr("opsd", [max(T_Pc, T_S), 2, 5, 512])
    pcmp = dscr("pcmp", [P, 512]); psel = dscr("psel", [P, 512]); osc = dscr("osc", [3, T_S, 16, 64])

    with contextlib.ExitStack() as ctx:
        k = K(ctx, nc)
        ident = ctx.enter_context(nc.sbuf_tensor("ident_sb", [128, 128], F32)); b_ident = Buf()
        k.dma(k.sp, ident[:], ident_d[:, :], writes=[b_ident])

        uid = [0]

        class Scope:
            def __enter__(self):
                self.st = contextlib.ExitStack()
                def uniq(name):
                    uid[0] += 1
                    return "%s_u%d" % (name, uid[0])
                self.sb = lambda name, shape, d: self.st.enter_context(nc.sbuf_tensor(uniq(name), shape, d))
                self.ps = lambda name, shape, d: self.st.enter_context(nc.psum_tensor(uniq(name), shape, d))
                return self

            def __exit__(self, *a):
                if a[0] is not None:
                    return False
                k.barrier()
                self.st.close()

        for r0 in range(0, T_Pc, 512):
            k.dma(k.sp, Xa[r0:r0 + 512, :], xp[r0:r0 + 512, :])
        k.dma(k.sp, Xa[T_Pc:NTOK, :], xs[:, :])
        k.barrier()
        for l in range(DEPTH):
            L = "l%d" % l
            with Scope() as s:
                linear_stage(k, nc, s.sb, s.ps, ident, b_ident, Xa, NTOK, w_in[l], IN_COLS, proj, L + "li")
            for r0 in range(0, T_Pc, 512):
                k.dma(k.sp, cmp_p[l, r0:r0 + 512, :], proj[r0:r0 + 512, KV0:KV0 + 512])
                k.dma(k.sp, sel_p[l, r0:r0 + 512, :], proj[r0:r0 + 512, KV0 + 512:KV0 + 1024])
            k.dma(k.sp, win_p[l, :, :], proj[T_Pc - 512:T_Pc, KV0 + 1024:KV0 + 1536])
            k.dma(k.sp, sh_p[l, :, :], proj[T_Pc - 1:T_Pc, 0:RW])
            for j in range(NS):
                s0 = T_Pc + j * T_S
                k.dma(k.sp, cmp_s[l, j, :, :], proj[s0:s0 + T_S, KV0:KV0 + 512])
                k.dma(k.sp, sel_s[l, j, :, :], proj[s0:s0 + T_S, KV0 + 512:KV0 + 1024])
                k.dma(k.sp, win_s[l, j, 512 - T_S:512, :], proj[s0:s0 + T_S, KV0 + 1024:KV0 + 1536])
                k.dma(k.sp, win_s[l, j, 0:512 - T_S, :], swin[l, j, T_S:512, :])
                k.dma(k.sp, sh_s[l, j, :, :], proj[s0 + T_S - 1:s0 + T_S, 0:RW])
            W = {nm: rwp[nm][l] for nm in RW_NAMES}
            with Scope() as s:
                C = load_rwkv_consts(k, nc, s.sb, W)
                with Scope() as s2:
                    rwkv_stage(k, nc, s2.sb, s2.ps, C, ident, b_ident, proj, 0, T_Pc, zrow, None, opsd, ymix[0:T_Pc], rw_p[l], L + "rp")
                for j in range(NS):
                    s0 = T_Pc + j * T_S
                    with Scope() as s2:
                        rwkv_stage(k, nc, s2.sb, s2.ps, C, ident, b_ident, proj, s0, T_S, ssh[l, j], srw[l, j], opsd, ymix[s0:s0 + T_S], rw_s[l, j], L + "r%d" % j)
            with Scope() as s:
                nsa_prompt_stage(k, nc, s.sb, s.ps, ident, b_ident, proj, 0, T_Pc, cmpw[l], CN, ymix[0:T_Pc], L + "np")
            for j in range(NS):
                s0 = T_Pc + j * T_S
                with Scope() as s:
                    nsa_sample_stage(k, nc, s.sb, s.ps, ident, b_ident, proj, s0, P, cmpw[l], cc.rearrange("l r c -> (l r) c"), cs.rearrange("l r c -> (l r) c"),
                                     ptab[j], swin[l, j], CN, CS, pcmp, psel, osc, ymix[s0:s0 + T_S], L + "q%d" % j, row_base=l * n_pool * 128)
            with Scope() as s:
                linear_stage(k, nc, s.sb, s.ps, ident, b_ident, ymix, NTOK, w_out[l], D, Hs, L + "lo")
            with Scope() as s:
                ln_stage(k, nc, s.sb, Xa, Hs, NTOK, lnp["ln1_g"][l], lnp["ln1_b"][l], X1, L + "n1")
            with Scope() as s:
                linear_stage(k, nc, s.sb, s.ps, ident, b_ident, X1, NTOK, wq[l], D, Qs, L + "lq")
            with Scope() as s:
                peer_stage(k, nc, s.sb, s.ps, ident, b_ident, X1, Qs, NTOK, skeys[l], pu.rearrange("l e d -> (l e) d"), pv.rearrange("l e d -> (l e) d"), Fs, L + "pe",
                           iota256, row_base=l * 16384)
            with Scope() as s:
                ln_stage(k, nc, s.sb, X1, Fs, NTOK, lnp["ln2_g"][l], lnp["ln2_b"][l], Xa, L + "n2")
        for r0 in range(0, T_Pc, 512):
            k.dma(k.sp, y_p[r0:r0 + 512, :], Xa[r0:r0 + 512, :])
        k.dma(k.sp, y_s[:, :], Xa[T_Pc:NTOK, :])
        k.finish()
    return nc


def make_in_maps(inputs, ncu, T_Pc, P, n_pool, NS):
    f = lambda nm: np.ascontiguousarray(np.asarray(inputs[nm], np.float32))
    xp, xs = f("x_prompt"), f("x_sample")
    shared = {
        "w_in": f("w_in"), "w_out": f("w_out"), "wq": f("peer_wq"), "skeys": f("peer_subkeys"), "pu": f("peer_u"), "pv": f("peer_v"),
        "cmp_w": f("nsa_cmp_w"), "cc": f("cache_cmp_kv").reshape(DEPTH, n_pool * 128, 512), "cs": f("cache_sel_kv").reshape(DEPTH, n_pool * 128, 512),
        "zrow": np.zeros((1, RW), np.float32), "ident": np.eye(128, dtype=np.float32),
        "iota256": np.tile(np.arange(256, dtype=np.float32)[None, :], (128, 1)),
    }
    for nm in ("ln1_g", "ln1_b", "ln2_g", "ln2_b"):
        shared[nm] = f(nm).reshape(DEPTH, 1, D)
    for nm in RW_NAMES:
        a = f(nm)
        shared[nm] = a if a.ndim == 3 and nm in ("rwkv_w2", "rwkv_a2", "rwkv_g2") else a.reshape(DEPTH, 1, -1)
    shared.update(nsa_host_consts())
    shared.update(nsas_host_consts(P))
    pt = np.asarray(inputs["page_table"], np.int32)
    swin, srw, ssh = f("state_win_kv"), f("state_rwkv"), f("state_shift")
    maps = []
    for c in range(ncu):
        sidx = [c + j * ncu for j in range(NS)]
        m = dict(shared)
        m["xp"] = np.ascontiguousarray(xp[c])
        m["xs"] = np.ascontiguousarray(xs[sidx].reshape(NS * T_S, D))
        m["ptab"] = np.ascontiguousarray(pt[sidx].reshape(NS, 1, -1))
        m["swin"] = np.ascontiguousarray(swin[:, sidx].reshape(DEPTH, NS, 512, 512))
        m["srw"] = np.ascontiguousarray(srw[:, sidx])
        m["ssh"] = np.ascontiguousarray(ssh[:, sidx])
        maps.append(m)
    return maps


def assemble(res, ncu, T_Pc, NS):
    kv = (2, 4, 64)
    B, BS = ncu, ncu * NS
    st = lambda key, shp: np.stack([res[c][key].reshape(shp) for c in range(ncu)], axis=0)
    y_p = st("y_p", (T_Pc, D))
    y_s = np.zeros((BS, T_S, D), np.float32)
    mv = lambda a: np.moveaxis(a, 0, 1)
    cmp_p = mv(st("cmp_p", (DEPTH, T_Pc) + kv)); sel_p = mv(st("sel_p", (DEPTH, T_Pc) + kv)); win_p = mv(st("win_p", (DEPTH, 512) + kv))
    rw_p = mv(st("rw_p", (DEPTH, 16, 64, 64))); sh_p = mv(st("sh_p", (DEPTH, 1, RW)))
    cmp_s = np.zeros((DEPTH, BS, T_S) + kv, np.float32); sel_s = np.zeros_like(cmp_s)
    win_s = np.zeros((DEPTH, BS, 512) + kv, np.float32); rw_s = np.zeros((DEPTH, BS, 16, 64, 64), np.float32); sh_s = np.zeros((DEPTH, BS, 1, RW), np.float32)
    for c in range(ncu):
        r = res[c]
        for j in range(NS):
            b = c + j * ncu
            y_s[b] = r["y_s"].reshape(NS, T_S, D)[j]
            cmp_s[:, b] = r["cmp_s"].reshape((DEPTH, NS, T_S) + kv)[:, j]
            sel_s[:, b] = r["sel_s"].reshape((DEPTH, NS, T_S) + kv)[:, j]
            win_s[:, b] = r["win_s"].reshape((DEPTH, NS, 512) + kv)[:, j]
            rw_s[:, b] = r["rw_s"].reshape(DEPTH, NS, 16, 64, 64)[:, j]
            sh_s[:, b] = r["sh_s"].reshape(DEPTH, NS, 1, RW)[:, j]
    return (y_p, y_s, cmp_p, sel_p, win_p, rw_p, sh_p, cmp_s, sel_s, win_s, rw_s, sh_s)


def kernel(**inputs):
    nc = build_nc()
    maps = make_in_maps(inputs, NCU, T_P, 16384, 1280, 2)
    res = run_bass_kernel_spmd(nc, maps, core_ids=list(range(NCU))).results
    return assemble(res, NCU, T_P, 2)
```

```python
import contextlib
import itertools
import numpy as np
import concourse.bass as bass
import concourse.mybir as mybir
from concourse.bass_utils import run_bass_kernel_spmd

F32 = mybir.dt.float32
BF16 = mybir.dt.bfloat16
I32 = mybir.dt.int32
AF = mybir.ActivationFunctionType
ALU = mybir.AluOpType

NCORES = 8
D = 2048
KC = D // 128
T_P = 2048
T_S = 4
RW = 3360
IN_COLS = 5968
NSA0 = RW
KV0 = RW + 1024
DEPTH = 2


class Buf:
    __slots__ = ("w", "r")

    def __init__(self):
        self.w = None
        self.r = {}


class Eng:
    def __init__(self, ctx, nc, eng, name):
        self.nc = nc
        self.eng = eng
        self.key = name
        self.sem = ctx.enter_context(nc.semaphore("sem_" + name))
        self.cnt = 0
        self.seen = {}

    def need(self, tok):
        sem, val, key = tok
        if key == self.key and key == "pe":
            return None
        if self.seen.get(id(sem), 0) < val:
            self.seen[id(sem)] = val
            return (sem, val)
        return None

    def wait(self, tok):
        sem, val, key = tok
        if key == self.key and key == "pe":
            return
        if self.seen.get(id(sem), 0) < val:
            self.eng.wait_ge(sem, val)
            self.seen[id(sem)] = val


class DmaSem:
    def __init__(self, h, i):
        self.h = h
        self.val = 0
        self.key = "dma%d" % i


class K:
    def __init__(self, ctx, nc):
        self.ctx = ctx
        self.nc = nc
        self.pe = Eng(ctx, nc, nc.tensor, "pe")
        self.dve = Eng(ctx, nc, nc.vector, "dve")
        self.act = Eng(ctx, nc, nc.scalar, "act")
        self.pool = Eng(ctx, nc, nc.gpsimd, "pool")
        self.sp = Eng(ctx, nc, nc.sync, "sp")
        self.dsems = [DmaSem(ctx.enter_context(nc.semaphore("dsem%d" % i)), i) for i in range(32)]
        self.dnext = 0

    def _deps(self, reads, writes):
        deps = []
        for b in reads:
            if b.w is not None:
                deps.append(b.w)
        for b in writes:
            if b.w is not None:
                deps.append(b.w)
            deps.extend(b.r.values())
        return deps

    def _mark(self, tok, reads, writes):
        for b in writes:
            b.w = tok
            b.r = {}
        for b in reads:
            b.r[tok[2]] = tok

    def _waits(self, E, toks):
        best = {}
        for tok in toks:
            sem, val, key = tok
            if key == E.key and key == "pe":
                continue
            if E.seen.get(id(sem), 0) < val and best.get(id(sem), (None, 0))[1] < val:
                best[id(sem)] = (sem, val)
        needs = list(best.values())
        for sem, val in needs:
            E.seen[id(sem)] = val
        for sem, val in needs[:-1]:
            E.eng.wait_ge(sem, val)
        return needs[-1] if needs else None

    def op(self, E, fn, reads=(), writes=()):
        last = self._waits(E, self._deps(reads, writes))
        inst = fn(E.eng)
        if last is not None:
            inst.wait_op(last[0], last[1], "sem-ge")
        E.cnt += 1
        inst.then_inc(E.sem, 1)
        self._mark((E.sem, E.cnt, E.key), reads, writes)
        return inst

    def dma(self, Q, out, in_, reads=(), writes=(), **kw):
        ds = self.dsems[self.dnext]
        self.dnext = (self.dnext + 1) % len(self.dsems)
        toks = self._deps(reads, writes)
        if ds.val:
            toks.append((ds.h, ds.val, ds.key))
        last = self._waits(Q, toks)
        inst = Q.eng.dma_start(out=out, in_=in_, **kw)
        if last is not None:
            inst.wait_op(last[0], last[1], "sem-ge")
        ds.val += 16
        inst.then_inc(ds.h, 16)
        self._mark((ds.h, ds.val, ds.key), reads, writes)
        return inst

    def barrier(self):
        engs = (self.pe, self.dve, self.act, self.pool, self.sp)
        toks = [(E.sem, E.cnt, E.key) for E in engs if E.cnt]
        toks += [(ds.h, ds.val, ds.key) for ds in self.dsems if ds.val]
        for E in engs:
            for t in toks:
                E.wait(t)

    def finish(self):
        for ds in self.dsems:
            if ds.val:
                self.sp.wait((ds.h, ds.val, ds.key))
        for E in (self.pe, self.dve, self.act, self.pool):
            if E.cnt:
                self.sp.wait((E.sem, E.cnt, E.key))


def load_rwkv_consts(k, nc, sb, W):
    C = {}
    def bc(name, ap_row, n):
        t = sb("c_" + name, [128, n], F32)
        b = Buf()
        k.dma(k.sp, t[:], ap_row.partition_broadcast(128), writes=[b])
        C[name] = (t, b)
    bc("mu", W["rwkv_mu"], RW)
    bc("kkw", W["rwkv_k_k"], 1024)
    bc("ka", W["rwkv_k_a"], 1024)
    bc("rk", W["rwkv_r_k"], 1024)
    bc("gng", W["rwkv_gn_g"], 1024)
    bc("gnb", W["rwkv_gn_b"], 1024)
    w2e = sb("c_w2e", [65, 1024], F32); b1 = Buf()
    k.dma(k.sp, w2e[0:64, :], W["rwkv_w2"], writes=[b1])
    k.dma(k.sp, w2e[64:65, :], W["rwkv_w0"], writes=[b1])
    a2e = sb("c_a2e", [65, 1024], F32); b2 = Buf()
    k.dma(k.sp, a2e[0:64, :], W["rwkv_a2"], writes=[b2])
    k.dma(k.sp, a2e[64:65, :], W["rwkv_a0"], writes=[b2])
    g2a = sb("c_g2a", [128, 1024], F32); g2b = sb("c_g2b", [32, 1024], F32); b3 = Buf()
    k.dma(k.sp, g2a[:], W["rwkv_g2"][0:128, :], writes=[b3])
    k.dma(k.sp, g2b[:], W["rwkv_g2"][128:160, :], writes=[b3])
    C["w2e"] = (w2e, b1); C["a2e"] = (a2e, b2); C["g2"] = ((g2a, g2b), b3)
    return C


def rwkv_stage(k, nc, sb, ps, C, ident, b_ident, proj, row0, T, shift_prev, s0, opsd, ymix, o_state, tag, dbg=None):
    TS = 2
    mu, b_mu = C["mu"]; kkw, b_kkw = C["kkw"]; ka, b_ka = C["ka"]; rk, b_rk = C["rk"]
    gng, b_gng = C["gng"]; gnb, b_gnb = C["gnb"]
    w2e, b_w2e = C["w2e"]; a2e, b_a2e = C["a2e"]; (g2a, g2b), b_g2 = C["g2"]
    n = lambda s: "%s_%s" % (tag, s)
    xr = sb(n("xr"), [128, RW], F32); b_xr = Buf()
    xs = sb(n("xs"), [128, RW], F32); b_xs = Buf()
    lo = sb(n("lo"), [128, 2 * 65 + 160], F32); b_lo = Buf()
    lT = sb(n("lT"), [128, 4, 128], F32); b_lT = Buf()
    ops = sb(n("ops"), [128, 5, 1024], F32); b_ops = Buf()
    av = sb(n("a"), [128, 1024], F32); b_a = Buf()
    gv = sb(n("g"), [128, 1024], F32); b_g = Buf()
    t1 = sb(n("t1"), [128, 1024], F32); b_t1 = Buf()
    t2 = sb(n("t2"), [128, 1024], F32); b_t2 = Buf()
    sm = sb(n("sm"), [128, 64], F32); b_sm = Buf()
    VT = sb(n("VT"), [128, 8, 128], F32); b_VT = Buf()
    YT = sb(n("YT"), [128, 8, 128], F32); b_YT = Buf()
    SS = [sb(n("S%d" % i), [128, 512], F32) for i in range(2)]; b_SS = [Buf(), Buf()]
    S, b_S = SS[0], b_SS[0]
    sidx = [0]
    Sd = [sb(n("Sd%d" % i), [128, 512], F32) for i in range(2)]; b_Sd = [Buf(), Buf()]
    Ty = [sb(n("Ty%d" % i), [128, 512], F32) for i in range(2)]; b_Ty = [Buf(), Buf()]
    tmp = sb(n("tmp"), [128, 512], F32); b_tmp = Buf()
    tb = sb(n("tb"), [128, 512], F32); b_tb = Buf()
    t3 = [sb(n("t3%d" % i), [128, 512], F32) for i in range(2)]; b_t3 = [Buf(), Buf()]
    sk = sb(n("sk"), [128, 8], F32); b_sk = Buf()
    BC = [sb(n("BC%d" % i), [128, TS, 5, 512], F32) for i in range(2)]; b_BC = [Buf(), Buf()]
    yo = sb(n("yo"), [128, 1024], F32); b_yo = Buf()
    pz = [ps(n("pz%d" % i), [128, 512], F32) for i in range(4)]; b_pz = [Buf() for _ in range(4)]
    b_opsd = Buf()
    b_ymix = Buf()

    k.op(k.dve, lambda e: e.memset(lo[:, 64:65], 1.0), writes=[b_lo])
    k.op(k.dve, lambda e: e.memset(lo[:, 129:130], 1.0), writes=[b_lo])
    if s0 is None:
        k.op(k.dve, lambda e: e.memset(S[:], 0.0), writes=[b_S])
    else:
        k.dma(k.sp, S[:].rearrange("p (a c) -> p a c", a=8), s0.rearrange("(a b) v c -> (b v) a c", b=2), writes=[b_S])

    ntile = (T + 127) // 128
    for it in range(ntile):
        r0 = it * 128
        rows = min(128, T - r0)
        R = slice(0, rows)
        g0 = row0 + r0
        k.dma(k.sp, xr[R, :], proj[g0:g0 + rows, 0:RW], writes=[b_xr])
        if it == 0:
            k.dma(k.sp, xs[0:1, :], shift_prev, writes=[b_xs])
            if rows > 1:
                k.dma(k.sp, xs[1:rows, :], proj[g0:g0 + rows - 1, 0:RW], writes=[b_xs])
        else:
            k.dma(k.sp, xs[R, :], proj[g0 - 1:g0 + rows - 1, 0:RW], writes=[b_xs])
        k.op(k.dve, lambda e: e.tensor_tensor(out=xs[R, :], in0=xs[R, :], in1=xr[R, :], op=ALU.subtract), reads=[b_xr], writes=[b_xs])
        k.op(k.dve, lambda e: e.tensor_tensor(out=xs[R, :], in0=xs[R, :], in1=mu[R, :], op=ALU.mult), reads=[b_mu], writes=[b_xs])
        k.op(k.dve, lambda e: e.tensor_tensor(out=xs[R, :], in0=xs[R, :], in1=xr[R, :], op=ALU.add), reads=[b_xr], writes=[b_xs])
        r_, k_, v_ = xs[R, 0:1024], xs[R, 1024:2048], xs[R, 2048:3072]
        k.op(k.act, lambda e: e.activation(out=lo[R, 0:64], in_=xs[R, 3072:3136], func=AF.Tanh), reads=[b_xs], writes=[b_lo])
        k.op(k.act, lambda e: e.activation(out=lo[R, 130:290], in_=xs[R, 3200:3360], func=AF.Sigmoid), reads=[b_xs], writes=[b_lo])
        k.op(k.dve, lambda e: e.tensor_copy(out=lo[R, 65:129], in_=xs[R, 3136:3200]), reads=[b_xs], writes=[b_lo])
        segs = [(0, 65), (65, 65), (130, 128), (258, 32)]
        for j, (c0, w) in enumerate(segs):
            k.op(k.pe, lambda e: e.transpose(pz[0][0:w, j * 128:j * 128 + rows], lo[R, c0:c0 + w], ident[R, R]),
                 reads=[b_lo, b_ident], writes=[b_pz[0]])
        k.op(k.dve, lambda e: e.tensor_copy(out=lT[:, :, R], in_=pz[0][:].rearrange("p (j c) -> p j c", j=4)[:, :, R]),
             reads=[b_pz[0]], writes=[b_lT])
        for hf in range(2):
            cs = slice(hf * 512, (hf + 1) * 512)
            k.op(k.pe, lambda e: e.matmul(pz[1 + hf][R, :], lhsT=lT[0:65, 0, R], rhs=w2e[0:65, cs], start=True, stop=True),
                 reads=[b_lT, b_w2e], writes=[b_pz[1 + hf]])
            k.op(k.act, lambda e: e.activation(out=t1[R, cs], in_=pz[1 + hf][R, :], func=AF.Sigmoid), reads=[b_pz[1 + hf]], writes=[b_t1])
        k.op(k.act, lambda e: e.activation(out=ops[R, 1, :], in_=t1[R, :], func=AF.Exp, scale=-0.6065306597126334), reads=[b_t1], writes=[b_ops])
        for hf in range(2):
            cs = slice(hf * 512, (hf + 1) * 512)
            k.op(k.pe, lambda e: e.matmul(pz[1 + hf][R, :], lhsT=lT[0:65, 1, R], rhs=a2e[0:65, cs], start=True, stop=True),
                 reads=[b_lT, b_a2e], writes=[b_pz[1 + hf]])
            k.op(k.act, lambda e: e.activation(out=av[R, cs], in_=pz[1 + hf][R, :], func=AF.Sigmoid), reads=[b_pz[1 + hf]], writes=[b_a])
        for hf in range(2):
            cs = slice(hf * 512, (hf + 1) * 512)
            k.op(k.pe, lambda e: e.matmul(pz[1 + hf][R, :], lhsT=lT[0:128, 2, R], rhs=g2a[:, cs], start=True, stop=False),
                 reads=[b_lT, b_g2], writes=[b_pz[1 + hf]])
            k.op(k.pe, lambda e: e.matmul(pz[1 + hf][R, :], lhsT=lT[0:32, 3, R], rhs=g2b[:, cs], start=False, stop=True),
                 reads=[b_lT, b_g2], writes=[b_pz[1 + hf]])
            k.op(k.act, lambda e: e.activation(out=gv[R, cs], in_=pz[1 + hf][R, :], func=AF.Copy), reads=[b_pz[1 + hf]], writes=[b_g])
        k.op(k.dve, lambda e: e.tensor_tensor(out=t1[R, :], in0=k_, in1=kkw[R, :], op=ALU.mult), reads=[b_xs, b_kkw], writes=[b_t1])
        k.op(k.pool, lambda e: e.tensor_tensor(out=t2[R, :], in0=t1[R, :], in1=t1[R, :], op=ALU.mult), reads=[b_t1], writes=[b_t2])
        k.op(k.dve, lambda e: e.tensor_reduce(out=sm[R, 0:16], in_=t2[R, :].rearrange("p (h c) -> p h c", h=16), axis=mybir.AxisListType.X, op=ALU.add),
             reads=[b_t2], writes=[b_sm])
        k.op(k.dve, lambda e: e.tensor_scalar(out=sm[R, 0:16], in0=sm[R, 0:16], scalar1=1e-24, scalar2=None, op0=ALU.max), writes=[b_sm])
        k.op(k.act, lambda e: e.activation(out=sm[R, 0:16], in_=sm[R, 0:16], func=AF.Sqrt), writes=[b_sm])
        k.op(k.dve, lambda e: e.reciprocal(out=sm[R, 0:16], in_=sm[R, 0:16]), writes=[b_sm])
        k.op(k.dve, lambda e: e.tensor_tensor(out=ops[R, 0, :].rearrange("p (h c) -> p h c", h=16), in0=t1[R, :].rearrange("p (h c) -> p h c", h=16),
                                              in1=sm[R, 0:16].unsqueeze(2).to_broadcast([rows, 16, 64]), op=ALU.mult),
             reads=[b_t1, b_sm], writes=[b_ops])
        k.op(k.dve, lambda e: e.scalar_tensor_tensor(out=t2[R, :], in0=av[R, :], scalar=-1.0, in1=ka[R, :], op0=ALU.add, op1=ALU.mult),
             reads=[b_a, b_ka], writes=[b_t2])
        k.op(k.dve, lambda e: e.scalar_tensor_tensor(out=ops[R, 3, :], in0=t2[R, :], scalar=1.0, in1=k_, op0=ALU.add, op1=ALU.mult),
             reads=[b_t2, b_xs], writes=[b_ops])
        k.op(k.dve, lambda e: e.tensor_tensor(out=ops[R, 2, :], in0=ops[R, 0, :], in1=av[R, :], op=ALU.mult), reads=[b_a], writes=[b_ops])
        k.op(k.pool, lambda e: e.tensor_copy(out=ops[R, 4, :], in_=r_), reads=[b_xs], writes=[b_ops])
        k.op(k.dve, lambda e: e.tensor_tensor(out=t1[R, :], in0=r_, in1=ops[R, 3, :], op=ALU.mult), reads=[b_xs, b_ops], writes=[b_t1])
        k.op(k.dve, lambda e: e.tensor_tensor(out=t1[R, :], in0=t1[R, :], in1=rk[R, :], op=ALU.mult), reads=[b_rk], writes=[b_t1])
        k.op(k.dve, lambda e: e.tensor_reduce(out=sm[R, 16:32], in_=t1[R, :].rearrange("p (h c) -> p h c", h=16), axis=mybir.AxisListType.X, op=ALU.add),
             reads=[b_t1], writes=[b_sm])
        if dbg is not None and it == 0:
            k.dma(k.sp, dbg["ops"][R], ops[R, :, :], reads=[b_ops])
            k.dma(k.sp, dbg["a"][R], av[R, :], reads=[b_a])
            k.dma(k.sp, dbg["g"][R], gv[R, :], reads=[b_g])
            k.dma(k.sp, dbg["sm"][R], sm[R, :], reads=[b_sm])
            k.dma(k.sp, dbg["xs"][R], xs[R, :], reads=[b_xs])
        for o_ in range(5):
            for b_ in range(2):
                k.dma(k.sp, opsd[r0:r0 + rows, b_, o_, :].rearrange("t (a c) -> t a c", a=8),
                      ops[R, o_, :].rearrange("p (a b c) -> p a b c", a=8, b=2)[:, :, b_, :], reads=[b_ops], writes=[b_opsd])
        for g in range(2):
            for j in range(4):
                h8 = g * 4 + j
                k.op(k.pe, lambda e: e.transpose(pz[1 + g][:, j * 128:j * 128 + rows], xs[R, 2048 + h8 * 128:2048 + (h8 + 1) * 128], ident[R, R]),
                     reads=[b_xs, b_ident], writes=[b_pz[1 + g]])
            k.op(k.act, lambda e: e.activation(out=VT[:, g * 4:(g + 1) * 4, R], in_=pz[1 + g][:].rearrange("p (j c) -> p j c", j=4)[:, :, R], func=AF.Copy),
                 reads=[b_pz[1 + g]], writes=[b_VT])
        ngrp = (rows + TS - 1) // TS
        def load_grp(gi):
            q0 = r0 + gi * TS
            nn = min(TS, rows - gi * TS)
            bt, bb = BC[gi % 2], b_BC[gi % 2]
            for h2 in range(2):
                k.dma(k.sp, bt[h2 * 64:(h2 + 1) * 64, 0:nn, :, :], opsd[q0:q0 + nn, h2, :, :].partition_broadcast(64), reads=[b_opsd], writes=[bb])
        load_grp(0)
        v3 = lambda ap: ap.rearrange("p (a c) -> p a c", a=8)
        pend = None
        for gi in range(ngrp):
            if gi + 1 < ngrp:
                load_grp(gi + 1)
            bt, bb = BC[gi % 2], b_BC[gi % 2]
            nn = min(TS, rows - gi * TS)
            for s in range(nn):
                tl = gi * TS + s
                cur, b_cur = SS[sidx[0] % 2], b_SS[sidx[0] % 2]
                nxt, b_nxt = SS[(sidx[0] + 1) % 2], b_SS[(sidx[0] + 1) % 2]
                sidx[0] += 1
                tt, btt = t3[tl % 2], b_t3[tl % 2]
                sd, bsd = Sd[tl % 2], b_Sd[tl % 2]
                ty, bty = Ty[tl % 2], b_Ty[tl % 2]
                k.op(k.pool, lambda e: e.tensor_tensor(out=sd[:], in0=cur[:], in1=bt[:, s, 1, :], op=ALU.mult), reads=[b_cur, bb], writes=[bsd])
                k.op(k.pool, lambda e: e.tensor_tensor(out=v3(tt[:]), in0=v3(bt[:, s, 3, :]), in1=VT[:, :, tl:tl + 1].to_broadcast([128, 8, 64]), op=ALU.mult),
                     reads=[bb, b_VT], writes=[btt])
                k.op(k.dve, lambda e: e.tensor_tensor(out=tmp[:], in0=cur[:], in1=bt[:, s, 0, :], op=ALU.mult), reads=[b_cur, bb], writes=[b_tmp])
                k.op(k.dve, lambda e: e.tensor_reduce(out=sk[:], in_=v3(tmp[:]), axis=mybir.AxisListType.X, op=ALU.add), reads=[b_tmp], writes=[b_sk])
                k.op(k.dve, lambda e: e.tensor_tensor(out=v3(tb[:]), in0=v3(bt[:, s, 2, :]), in1=sk[:].unsqueeze(2).to_broadcast([128, 8, 64]), op=ALU.mult),
                     reads=[bb, b_sk], writes=[b_tb])
                if pend is not None:
                    pend()
                    pend = None
                k.op(k.dve, lambda e: e.tensor_tensor(out=nxt[:], in0=sd[:], in1=tb[:], op=ALU.subtract), reads=[bsd, b_tb], writes=[b_nxt])
                k.op(k.dve, lambda e: e.tensor_tensor(out=nxt[:], in0=nxt[:], in1=tt[:], op=ALU.add), reads=[btt], writes=[b_nxt])
                k.op(k.pool, lambda e: e.tensor_tensor(out=ty[:], in0=nxt[:], in1=bt[:, s, 4, :], op=ALU.mult), reads=[b_nxt, bb], writes=[bty])
                def mk(ty=ty, bty=bty, tl=tl):
                    k.op(k.dve, lambda e: e.tensor_reduce(out=YT[:, :, tl], in_=v3(ty[:]), axis=mybir.AxisListType.X, op=ALU.add), reads=[bty], writes=[b_YT])
                pend = mk
        if pend is not None:
            pend()
            pend = None
        if dbg is not None and it == 0:
            k.dma(k.sp, dbg["YT"], YT[:], reads=[b_YT])
            k.dma(k.sp, dbg["VT"], VT[:], reads=[b_VT])
        for g in range(2):
            for j in range(4):
                h8 = g * 4 + j
                k.op(k.pe, lambda e: e.transpose(pz[1 + g][R, j * 128:(j + 1) * 128], YT[:, h8, R], ident[:, :]),
                     reads=[b_YT, b_ident], writes=[b_pz[1 + g]])
            k.op(k.act, lambda e: e.activation(out=yo[R, g * 512:(g + 1) * 512], in_=pz[1 + g][R, :], func=AF.Copy), reads=[b_pz[1 + g]], writes=[b_yo])
        y3 = yo[R, :].rearrange("p (h c) -> p h c", h=16)
        k.op(k.dve, lambda e: e.tensor_reduce(out=sm[R, 32:48], in_=y3, axis=mybir.AxisListType.X, op=ALU.add), reads=[b_yo], writes=[b_sm])
        k.op(k.dve, lambda e: e.tensor_scalar(out=sm[R, 32:48], in0=sm[R, 32:48], scalar1=1.0 / 64, scalar2=None, op0=ALU.mult), writes=[b_sm])
        k.op(k.dve, lambda e: e.tensor_tensor(out=y3, in0=y3, in1=sm[R, 32:48].unsqueeze(2).to_broadcast([rows, 16, 64]), op=ALU.subtract), reads=[b_sm], writes=[b_yo])
        k.op(k.pool, lambda e: e.tensor_tensor(out=t1[R, :], in0=yo[R, :], in1=yo[R, :], op=ALU.mult), reads=[b_yo], writes=[b_t1])
        k.op(k.dve, lambda e: e.tensor_reduce(out=sm[R, 48:64], in_=t1[R, :].rearrange("p (h c) -> p h c", h=16), axis=mybir.AxisListType.X, op=ALU.add),
             reads=[b_t1], writes=[b_sm])
        k.op(k.dve, lambda e: e.tensor_scalar(out=sm[R, 48:64], in0=sm[R, 48:64], scalar1=1.0 / 64, scalar2=64e-5, op0=ALU.mult, op1=ALU.add), writes=[b_sm])
        k.op(k.act, lambda e: e.activation(out=sm[R, 48:64], in_=sm[R, 48:64], func=AF.Sqrt), writes=[b_sm])
        k.op(k.dve, lambda e: e.reciprocal(out=sm[R, 48:64], in_=sm[R, 48:64]), writes=[b_sm])
        k.op(k.dve, lambda e: e.tensor_tensor(out=y3, in0=y3, in1=sm[R, 48:64].unsqueeze(2).to_broadcast([rows, 16, 64]), op=ALU.mult), reads=[b_sm], writes=[b_yo])
        k.op(k.dve, lambda e: e.tensor_tensor(out=yo[R, :], in0=yo[R, :], in1=gng[R, :], op=ALU.mult), reads=[b_gng], writes=[b_yo])
        k.op(k.dve, lambda e: e.tensor_tensor(out=yo[R, :], in0=yo[R, :], in1=gnb[R, :], op=ALU.add), reads=[b_gnb], writes=[b_yo])
        k.op(k.dve, lambda e: e.tensor_tensor(out=t1[R, :].rearrange("p (h c) -> p h c", h=16), in0=v_.rearrange("p (h c) -> p h c", h=16),
                                              in1=sm[R, 16:32].unsqueeze(2).to_broadcast([rows, 16, 64]), op=ALU.mult),
             reads=[b_xs, b_sm], writes=[b_t1])
        k.op(k.dve, lambda e: e.tensor_tensor(out=yo[R, :], in0=yo[R, :], in1=t1[R, :], op=ALU.add), reads=[b_t1], writes=[b_yo])
        k.op(k.dve, lambda e: e.tensor_tensor(out=yo[R, :], in0=yo[R, :], in1=gv[R, :], op=ALU.mult), reads=[b_g], writes=[b_yo])
        k.dma(k.sp, ymix[r0:r0 + rows, 0:1024], yo[R, :], reads=[b_yo], writes=[b_ymix])
    Sf, b_Sf = SS[sidx[0] % 2], b_SS[sidx[0] % 2]
    k.dma(k.sp, o_state.rearrange("(a b) v c -> (b v) a c", b=2), Sf[:].rearrange("p (a c) -> p a c", a=8), reads=[b_Sf])
    return b_ymix
NEG = -1.0e30


def nsa_host_consts():
    p = np.arange(128)
    mask4 = (p[:, None] // 32 == np.arange(4)[None, :]).astype(np.float32)
    biasC = np.zeros((16, 128, 64), np.float32)
    FM = np.zeros((16, 128, 32), np.float32)
    FA = np.zeros((16, 128, 32), np.float32)
    for i in range(16):
        qpos = i * 128 + p
        cmp_end = (np.arange(64) + 1) * 32 - 1
        biasC[i] = np.where(cmp_end[None, :] <= qpos[:, None], 0.0, NEG)
        cur = qpos // 64
        blk = np.arange(32)[None, :]
        forced = (blk == 0) | (blk == cur[:, None]) | (blk == cur[:, None] - 1)
        fut = blk > cur[:, None]
        FM[i] = np.where(fut | forced, 0.0, 1.0)
        FA[i] = np.where(fut, -1.0, np.where(forced, 1.0e4, 0.0))
    tri = np.where(p[None, :] > p[:, None], NEG, 0.0).astype(np.float32)
    bw4 = np.where(p[None, :] >= p[:, None] + 1, 0.0, NEG).astype(np.float32)
    return {"nsa_mask4": mask4, "nsa_biasC": biasC, "nsa_FM": FM, "nsa_FA": FA, "nsa_tri": tri, "nsa_bw4": bw4}


def nsa_prompt_stage(k, nc, sb, ps, ident, b_ident, proj, row0, T, cmp_w, CN, ymix, tag):
    NT = T // 128
    X = mybir.AxisListType.X
    n = lambda s: "%s_%s" % (tag, s)
    PS = [ps(n("ps%d" % i), [128, 512], F32) for i in range(8)]
    bPS = [Buf() for _ in range(8)]
    mask4 = sb(n("mask4"), [128, 4], F32); tri = sb(n("tri"), [128, 128], F32); bw4 = sb(n("bw4"), [128, 128], F32)
    b_c = Buf()
    k.dma(k.sp, mask4[:], CN["nsa_mask4"], writes=[b_c])
    k.dma(k.sp, tri[:], CN["nsa_tri"], writes=[b_c])
    k.dma(k.sp, bw4[:], CN["nsa_bw4"], writes=[b_c])
    wcol = sb(n("wcol"), [128, 2], F32); b_wcol = Buf()
    for c in range(2):
        for j in range(4):
            k.dma(k.sp, wcol[j * 32:(j + 1) * 32, c:c + 1], cmp_w[c:c + 1, :].rearrange("o l -> l o"), writes=[b_wcol])
    Ww = sb(n("Ww"), [128, 2, 124], F32); b_Ww = Buf()
    k.op(k.dve, lambda e: e.memset(Ww[:], 0.0), writes=[b_Ww])
    for c in range(2):
        k.op(k.dve, lambda e: e.tensor_scalar(out=Ww[:, c, 60:64], in0=mask4[:], scalar1=wcol[:, c:c + 1], scalar2=None, op0=ALU.mult),
             reads=[b_c, b_wcol], writes=[b_Ww])
    KTs = sb(n("KTs"), [64, 4, T], F32); KTw = sb(n("KTw"), [64, 4, T], F32)
    Vs = sb(n("Vs"), [128, NT, 256], F32); Vw = sb(n("Vw"), [128, NT, 256], F32)
    b_KT = Buf(); b_V = Buf()
    kin = [sb(n("kin%d" % i), [128, 3, 256], F32) for i in range(2)]; b_kin = [Buf(), Buf()]
    kcbT = sb(n("kcbT"), [64, 4, 64], F32); vcb = sb(n("vcb"), [64, 256], F32); b_cb = Buf()
    for i in range(NT):
        g0 = row0 + i * 128
        kt, bk = kin[i % 2], b_kin[i % 2]
        for j, off in enumerate((0, 512, 1024)):
            k.dma(k.sp, kt[:, j, :], proj[g0:g0 + 128, KV0 + off:KV0 + off + 256], writes=[bk])
        k.dma(k.sp, Vs[:, i, :], proj[g0:g0 + 128, KV0 + 768:KV0 + 1024], writes=[b_V])
        k.dma(k.sp, Vw[:, i, :], proj[g0:g0 + 128, KV0 + 1280:KV0 + 1536], writes=[b_V])
        vct = sb(n("vct%d" % i), [128, 256], F32) if False else None
        for (j, dst, pi) in ((1, KTs, 0), (2, KTw, 1)):
            for kv in range(4):
                k.op(k.pe, lambda e: e.transpose(PS[pi][0:64, kv * 128:(kv + 1) * 128], kt[:, j, kv * 64:(kv + 1) * 64], ident[:, :]),
                     reads=[bk, b_ident], writes=[bPS[pi]])
            k.op(k.act if pi else k.dve,
                 (lambda e: e.activation(out=dst[:, :, i * 128:(i + 1) * 128], in_=PS[pi][0:64, :].rearrange("p (a c) -> p a c", a=4), func=AF.Copy)) if pi else
                 (lambda e: e.tensor_copy(out=dst[:, :, i * 128:(i + 1) * 128], in_=PS[pi][0:64, :].rearrange("p (a c) -> p a c", a=4))),
                 reads=[bPS[pi]], writes=[b_KT])
        for kv in range(4):
            k.op(k.pe, lambda e: e.matmul(PS[2][0:64, kv * 64:(kv + 1) * 64], lhsT=kt[:, 0, kv * 64:(kv + 1) * 64], rhs=Ww[:, 0, 60 - 4 * i:124 - 4 * i],
                                          start=(i == 0 and kv == 0), stop=(i == NT - 1 and kv == 3), skip_group_check=True),
                 reads=[bk, b_Ww], writes=[bPS[2]])
    for i in range(NT):
        g0 = row0 + i * 128
        kt, bk = kin[i % 2], b_kin[i % 2]
        k.dma(k.sp, kt[:, 0, :], proj[g0:g0 + 128, KV0 + 256:KV0 + 512], writes=[bk])
        k.op(k.pe, lambda e: e.matmul(PS[3][0:64, 0:256], lhsT=Ww[:, 1, 60 - 4 * i:124 - 4 * i], rhs=kt[:, 0, :], start=(i == 0), stop=(i == NT - 1)),
             reads=[bk, b_Ww], writes=[bPS[3]])
    k.op(k.dve, lambda e: e.tensor_copy(out=kcbT[:], in_=PS[2][0:64, 0:256].rearrange("p (a c) -> p a c", a=4)), reads=[bPS[2]], writes=[b_cb])
    k.op(k.dve, lambda e: e.tensor_copy(out=vcb[:], in_=PS[3][0:64, 0:256]), reads=[bPS[3]], writes=[b_cb])

    qin = sb(n("qin"), [128, 1024 + 48], F32); b_qin = Buf()
    qT = sb(n("qT"), [64, 16, 128], F32); b_qT = Buf()
    cst = sb(n("cst"), [128, 64 + 32 + 32], F32); b_cst = Buf()
    sc = sb(n("sc"), [128, T], F32); b_sc = Buf()
    st = sb(n("st"), [128, 8], F32); b_st = Buf()
    pc = sb(n("pc"), [128, 64], F32); b_pc = Buf()
    imp = sb(n("imp"), [128, 4, 32], F32); b_imp = Buf()
    impw = sb(n("impw"), [128, 4, 32], F32); b_impw = Buf()
    mx = sb(n("mx"), [128, 16], F32); b_mx = Buf()
    selb = sb(n("selb"), [128, 4, 32], F32); b_selb = Buf()
    eT = [sb(n("eT%d" % i), [128, 128], F32) for i in range(2)]; b_eT = [Buf(), Buf()]
    oc = sb(n("oc"), [128, 3, 16, 64], F32); b_oc = Buf()
    gt = sb(n("gt"), [128, 48], F32); b_gt = Buf()
    yo = sb(n("yo"), [128, 1024], F32); b_yo = Buf()
    b_ymix = Buf()
    net = [0]

    def softmax_pv(h, ncols, vtiles, slot):
        k.op(k.dve, lambda e: e.tensor_reduce(out=st[:, 0:1], in_=sc[:, 0:ncols], axis=X, op=ALU.max), reads=[b_sc], writes=[b_st])
        k.op(k.dve, lambda e: e.tensor_scalar(out=st[:, 0:1], in0=st[:, 0:1], scalar1=-1.0e20, scalar2=-1.0, op0=ALU.max, op1=ALU.mult), writes=[b_st])
        k.op(k.act, lambda e: e.activation(out=sc[:, 0:ncols], in_=sc[:, 0:ncols], func=AF.Exp, bias=st[:, 0:1], scale=1.0), reads=[b_st], writes=[b_sc])
        k.op(k.dve, lambda e: e.tensor_reduce(out=st[:, 1:2], in_=sc[:, 0:ncols], axis=X, op=ALU.add), reads=[b_sc], writes=[b_st])
        k.op(k.dve, lambda e: e.tensor_scalar(out=st[:, 1:2], in0=st[:, 1:2], scalar1=1.0e-30, scalar2=None, op0=ALU.max), writes=[b_st])
        k.op(k.dve, lambda e: e.reciprocal(out=st[:, 1:2], in_=st[:, 1:2]), writes=[b_st])
        for vi, (c0, w, vap) in enumerate(vtiles):
            pt, bpt = PS[4 + net[0] % 2], bPS[4 + net[0] % 2]
            et, bet = eT[net[0] % 2], b_eT[net[0] % 2]
            net[0] += 1
            k.op(k.pe, lambda e: e.transpose(pt[0:w, 0:128], sc[:, c0:c0 + w], ident[:, :]), reads=[b_sc, b_ident], writes=[bpt])
            k.op(k.act, lambda e: e.activation(out=et[0:w, :], in_=pt[0:w, 0:128], func=AF.Copy), reads=[bpt], writes=[bet])
            k.op(k.pe, lambda e: e.matmul(PS[6][:, 0:64], lhsT=et[0:w, :], rhs=vap, start=(vi == 0), stop=(vi == len(vtiles) - 1)),
                 reads=[bet, b_V, b_cb], writes=[bPS[6]])
        k.op(k.dve, lambda e: e.tensor_scalar(out=oc[:, slot, h, :], in0=PS[6][:, 0:64], scalar1=st[:, 1:2], scalar2=None, op0=ALU.mult),
             reads=[bPS[6], b_st], writes=[b_oc])

    for i in range(NT):
        g0 = row0 + i * 128
        k.dma(k.sp, qin[:, 0:1024], proj[g0:g0 + 128, NSA0:NSA0 + 1024], writes=[b_qin])
        k.dma(k.sp, qin[:, 1024:1072], proj[g0:g0 + 128, KV0 + 1536:KV0 + 1584], writes=[b_qin])
        k.dma(k.sp, cst[:, 0:64], CN["nsa_biasC"][i], writes=[b_cst])
        k.dma(k.sp, cst[:, 64:96], CN["nsa_FM"][i], writes=[b_cst])
        k.dma(k.sp, cst[:, 96:128], CN["nsa_FA"][i], writes=[b_cst])
        k.op(k.act, lambda e: e.activation(out=gt[:], in_=qin[:, 1024:1072], func=AF.Sigmoid), reads=[b_qin], writes=[b_gt])
        for g in range(4):
            for j in range(4):
                h = g * 4 + j
                k.op(k.pe, lambda e: e.transpose(PS[g % 2][0:64, j * 128:(j + 1) * 128], qin[:, h * 64:(h + 1) * 64], ident[:, :]),
                     reads=[b_qin, b_ident], writes=[bPS[g % 2]])
            k.op(k.act, lambda e: e.activation(out=qT[:, g * 4:(g + 1) * 4, :], in_=PS[g % 2][0:64, :].rearrange("p (a c) -> p a c", a=4), func=AF.Copy, scale=0.125),
                 reads=[bPS[g % 2]], writes=[b_qT])
        for h in range(16):
            kv, g = h // 4, h % 4
            k.op(k.pe, lambda e: e.matmul(PS[2][:, 0:64], lhsT=qT[:, h, :], rhs=kcbT[:, kv, :], start=True, stop=True), reads=[b_qT, b_cb], writes=[bPS[2]])
            k.op(k.dve, lambda e: e.tensor_tensor(out=sc[:, 0:64], in0=PS[2][:, 0:64], in1=cst[:, 0:64], op=ALU.add), reads=[bPS[2], b_cst], writes=[b_sc])
            softmax_pv(h, 64, [(0, 64, vcb[:, kv * 64:(kv + 1) * 64])], 0)
            k.op(k.dve, lambda e: e.tensor_scalar(out=pc[:], in0=sc[:, 0:64], scalar1=st[:, 1:2], scalar2=None, op0=ALU.mult), reads=[b_sc, b_st], writes=[b_pc])
            if g == 0:
                k.op(k.dve, lambda e: e.tensor_reduce(out=imp[:, kv, :], in_=pc[:].rearrange("p (a c) -> p a c", c=2), axis=X, op=ALU.add), reads=[b_pc], writes=[b_imp])
            else:
                k.op(k.dve, lambda e: e.tensor_reduce(out=mx[:, 0:0 + 32] if False else impw[:, 0, :], in_=pc[:].rearrange("p (a c) -> p a c", c=2), axis=X, op=ALU.add),
                     reads=[b_pc], writes=[b_impw])
                k.op(k.dve, lambda e: e.tensor_tensor(out=imp[:, kv, :], in0=imp[:, kv, :], in1=impw[:, 0, :], op=ALU.add), reads=[b_impw], writes=[b_imp])
        k.op(k.dve, lambda e: e.tensor_tensor(out=imp[:], in0=imp[:], in1=cst[:, 64:96].unsqueeze(1).to_broadcast([128, 4, 32]), op=ALU.mult), reads=[b_cst], writes=[b_imp])
        k.op(k.dve, lambda e: e.tensor_tensor(out=imp[:], in0=imp[:], in1=cst[:, 96:128].unsqueeze(1).to_broadcast([128, 4, 32]), op=ALU.add), reads=[b_cst], writes=[b_imp])
        for kv in range(4):
            k.op(k.dve, lambda e: e.max(out=mx[:, 0:8], in_=imp[:, kv, :]), reads=[b_imp], writes=[b_mx])
            k.op(k.dve, lambda e: e.match_replace(out=impw[:, kv, :], in_to_replace=mx[:, 0:8], in_values=imp[:, kv, :], imm_value=-2.0), reads=[b_imp, b_mx], writes=[b_impw])
            k.op(k.dve, lambda e: e.max(out=mx[:, 8:16], in_=impw[:, kv, :]), reads=[b_impw], writes=[b_mx])
            k.op(k.dve, lambda e: e.tensor_scalar(out=mx[:, 15:16], in0=mx[:, 15:16], scalar1=0.0, scalar2=None, op0=ALU.max), writes=[b_mx])
            k.op(k.dve, lambda e: e.tensor_scalar(out=selb[:, kv, :], in0=imp[:, kv, :], scalar1=mx[:, 15:16], scalar2=None, op0=ALU.is_ge), reads=[b_imp, b_mx], writes=[b_selb])
        k.op(k.dve, lambda e: e.tensor_scalar(out=selb[:], in0=selb[:], scalar1=-1.0, scalar2=1.0e30, op0=ALU.add, op1=ALU.mult), writes=[b_selb])
        nk = (i + 1) * 128
        for h in range(16):
            kv = h // 4
            for c0 in range(0, nk, 512):
                w = min(512, nk - c0)
                pt, bpt = PS[(c0 // 512) % 2], bPS[(c0 // 512) % 2]
                k.op(k.pe, lambda e: e.matmul(pt[:, 0:w], lhsT=qT[:, h, :], rhs=KTs[:, kv, c0:c0 + w], start=True, stop=True), reads=[b_qT, b_KT], writes=[bpt])
                nb = w // 64
                k.op(k.dve, lambda e: e.tensor_tensor(out=sc[:, c0:c0 + w].rearrange("p (a c) -> p a c", c=64), in0=pt[:, 0:w].rearrange("p (a c) -> p a c", c=64),
                                                      in1=selb[:, kv, c0 // 64:c0 // 64 + nb].unsqueeze(2).to_broadcast([128, nb, 64]), op=ALU.add),
                     reads=[bpt, b_selb], writes=[b_sc])
            k.op(k.dve, lambda e: e.tensor_tensor(out=sc[:, nk - 128:nk], in0=sc[:, nk - 128:nk], in1=tri[:], op=ALU.add), reads=[b_c], writes=[b_sc])
            softmax_pv(h, nk, [(j * 128, 128, Vs[:, j, kv * 64:(kv + 1) * 64]) for j in range(i + 1)], 1)
            j0 = max(0, i - 4)
            wk = (i + 1 - j0) * 128
            for c0 in range(0, wk, 512):
                w = min(512, wk - c0)
                pt, bpt = PS[(c0 // 512) % 2], bPS[(c0 // 512) % 2]
                k.op(k.pe, lambda e: e.matmul(pt[:, 0:w], lhsT=qT[:, h, :], rhs=KTw[:, kv, j0 * 128 + c0:j0 * 128 + c0 + w], start=True, stop=True), reads=[b_qT, b_KT], writes=[bpt])
                k.op(k.act, lambda e: e.activation(out=sc[:, c0:c0 + w], in_=pt[:, 0:w], func=AF.Copy), reads=[bpt], writes=[b_sc])
            k.op(k.dve, lambda e: e.tensor_tensor(out=sc[:, wk - 128:wk], in0=sc[:, wk - 128:wk], in1=tri[:], op=ALU.add), reads=[b_c], writes=[b_sc])
            if i >= 4:
                k.op(k.dve, lambda e: e.tensor_tensor(out=sc[:, 0:128], in0=sc[:, 0:128], in1=bw4[:], op=ALU.add), reads=[b_c], writes=[b_sc])
            softmax_pv(h, wk, [(jj * 128, 128, Vw[:, j0 + jj, kv * 64:(kv + 1) * 64]) for jj in range(i + 1 - j0)], 2)
        g3 = gt[:].rearrange("p (h c) -> p h c", c=3)
        y3 = yo[:].rearrange("p (h c) -> p h c", h=16)
        k.op(k.dve, lambda e: e.tensor_tensor(out=y3, in0=oc[:, 0, :, :], in1=g3[:, :, 0:1].to_broadcast([128, 16, 64]), op=ALU.mult), reads=[b_oc, b_gt], writes=[b_yo])
        for s_ in (1, 2):
            k.op(k.dve, lambda e: e.tensor_tensor(out=oc[:, s_, :, :], in0=oc[:, s_, :, :], in1=g3[:, :, s_:s_ + 1].to_broadcast([128, 16, 64]), op=ALU.mult), reads=[b_gt], writes=[b_oc])
            k.op(k.dve, lambda e: e.tensor_tensor(out=y3, in0=y3, in1=oc[:, s_, :, :], op=ALU.add), reads=[b_oc], writes=[b_yo])
        k.dma(k.sp, ymix[i * 128:(i + 1) * 128, 1024:2048], yo[:], reads=[b_yo], writes=[b_ymix])
    return b_ymix
def linear_stage(k, nc, sb, ps, ident, b_ident, src, T, W, N, dst, tag, b_src=None):
    n = lambda s: "%s_%s" % (tag, s)
    TCH = 1024
    CB = 512
    xT = sb(n("xT"), [128, KC, TCH + 4], F32)
    xin = [sb(n("xin%d" % i), [128, D], F32) for i in range(2)]; b_xin = [Buf(), Buf()]
    wst = [sb(n("w%d" % i), [128, KC, CB], F32) for i in range(2)]; b_w = [Buf(), Buf()]
    po = [sb(n("po%d" % i), [128, CB], F32) for i in range(2)]; b_po = [Buf(), Buf()]
    tp = [ps(n("tp%d" % i), [128, 512], F32) for i in range(2)]; b_tp = [Buf(), Buf()]
    pp = [ps(n("pp%d" % i), [128, CB], F32) for i in range(2)]; b_pp = [Buf(), Buf()]
    b_dst = Buf()
    w_v = W.rearrange("(kc p) c -> p kc c", p=128)
    nblk = (N + CB - 1) // CB
    cnt = [0, 0, 0]
    for t0 in range(0, T, TCH):
        tn = min(TCH + 4, T - t0) if T - t0 <= TCH + 4 else TCH
        tiles = [(a, min(128, tn - a)) for a in range(0, tn, 128)]
        b_xT = [Buf() for _ in tiles]
        for ti, (a, rows) in enumerate(tiles):
            xb, bx = xin[cnt[0] % 2], b_xin[cnt[0] % 2]
            cnt[0] += 1
            k.dma(k.sp, xb[0:rows, :], src[t0 + a:t0 + a + rows, :], reads=([b_src] if b_src is not None else []), writes=[bx])
            for g in range(KC // 4):
                pt, bp = tp[cnt[1] % 2], b_tp[cnt[1] % 2]
                cnt[1] += 1
                for j in range(4):
                    kc = g * 4 + j
                    k.op(k.pe, lambda e: e.transpose(pt[:, j * 128:j * 128 + rows], xb[0:rows, kc * 128:(kc + 1) * 128], ident[0:rows, 0:rows]),
                         reads=[bx, b_ident], writes=[bp])
                dstv = xT[:, g * 4:(g + 1) * 4, a:a + rows]
                srcp = pt[:].rearrange("p (j c) -> p j c", j=4)[:, :, 0:rows]
                if g % 2 == 0:
                    k.op(k.dve, lambda e: e.tensor_copy(out=dstv, in_=srcp), reads=[bp], writes=[b_xT[ti]])
                else:
                    k.op(k.act, lambda e: e.activation(out=dstv, in_=srcp, func=AF.Copy), reads=[bp], writes=[b_xT[ti]])
        for bi in range(nblk):
            c0 = bi * CB
            cw = min(CB, N - c0)
            wb, bw = wst[bi % 2], b_w[bi % 2]
            k.dma(k.sp, wb[:, :, 0:cw], w_v[:, :, c0:c0 + cw], writes=[bw])
            for ti, (a, rows) in enumerate(tiles):
                pt, bp = pp[cnt[2] % 2], b_pp[cnt[2] % 2]
                ot, bo = po[cnt[2] % 2], b_po[cnt[2] % 2]
                cnt[2] += 1
                for kc in range(KC):
                    k.op(k.pe, lambda e: e.matmul(pt[0:rows, 0:cw], lhsT=xT[:, kc, a:a + rows], rhs=wb[:, kc, 0:cw], start=(kc == 0), stop=(kc == KC - 1)),
                         reads=[b_xT[ti], bw], writes=[bp])
                if cnt[2] % 2:
                    k.op(k.act, lambda e: e.activation(out=ot[0:rows, 0:cw], in_=pt[0:rows, 0:cw], func=AF.Copy), reads=[bp], writes=[bo])
                else:
                    k.op(k.dve, lambda e: e.tensor_copy(out=ot[0:rows, 0:cw], in_=pt[0:rows, 0:cw]), reads=[bp], writes=[bo])
                k.dma(k.sp, dst[t0 + a:t0 + a + rows, c0:c0 + cw], ot[0:rows, 0:cw], reads=[bo], writes=[b_dst])
        if tn > TCH:
            break
    return b_dst


def ln_stage(k, nc, sb, x, h, T, g_ap, b_ap, out, tag, extra_out=None, b_in=()):
    ALPHA = (2 * DEPTH) ** 0.25
    n = lambda s: "%s_%s" % (tag, s)
    X = mybir.AxisListType.X
    gb = sb(n("g"), [128, D], F32); bb = sb(n("b"), [128, D], F32); b_c = Buf()
    k.dma(k.sp, gb[:], g_ap.partition_broadcast(128), writes=[b_c])
    k.dma(k.sp, bb[:], b_ap.partition_broadcast(128), writes=[b_c])
    xt = [sb(n("x%d" % i), [128, D], F32) for i in range(2)]; b_x = [Buf(), Buf()]
    ht = [sb(n("h%d" % i), [128, D], F32) for i in range(2)]; b_h = [Buf(), Buf()]
    sq = sb(n("sq"), [128, D], F32); b_sq = Buf()
    st = sb(n("st"), [128, 4], F32); b_st = Buf()
    b_out = Buf()
    for it, r0 in enumerate(range(0, T, 128)):
        rows = min(128, T - r0)
        R = slice(0, rows)
        xx, bx = xt[it % 2], b_x[it % 2]
        hh, bh = ht[it % 2], b_h[it % 2]
        k.dma(k.sp, xx[R, :], x[r0:r0 + rows, :], reads=list(b_in), writes=[bx])
        k.dma(k.sp, hh[R, :], h[r0:r0 + rows, :], reads=list(b_in), writes=[bh])
        k.op(k.dve, lambda e: e.scalar_tensor_tensor(out=xx[R, :], in0=xx[R, :], scalar=ALPHA, in1=hh[R, :], op0=ALU.mult, op1=ALU.add), reads=[bh], writes=[bx])
        k.op(k.dve, lambda e: e.tensor_reduce(out=st[R, 0:1], in_=xx[R, :], axis=X, op=ALU.add), reads=[bx], writes=[b_st])
        k.op(k.dve, lambda e: e.tensor_scalar(out=st[R, 0:1], in0=st[R, 0:1], scalar1=-1.0 / D, scalar2=None, op0=ALU.mult), writes=[b_st])
        k.op(k.dve, lambda e: e.tensor_scalar(out=xx[R, :], in0=xx[R, :], scalar1=st[R, 0:1], scalar2=None, op0=ALU.add), reads=[b_st], writes=[bx])
        k.op(k.pool, lambda e: e.tensor_tensor(out=sq[R, :], in0=xx[R, :], in1=xx[R, :], op=ALU.mult), reads=[bx], writes=[b_sq])
        k.op(k.dve, lambda e: e.tensor_reduce(out=st[R, 1:2], in_=sq[R, :], axis=X, op=ALU.add), reads=[b_sq], writes=[b_st])
        k.op(k.dve, lambda e: e.tensor_scalar(out=st[R, 1:2], in0=st[R, 1:2], scalar1=1.0 / D, scalar2=1e-5, op0=ALU.mult, op1=ALU.add), writes=[b_st])
        k.op(k.act, lambda e: e.activation(out=st[R, 1:2], in_=st[R, 1:2], func=AF.Sqrt), writes=[b_st])
        k.op(k.dve, lambda e: e.reciprocal(out=st[R, 1:2], in_=st[R, 1:2]), writes=[b_st])
        k.op(k.dve, lambda e: e.scalar_tensor_tensor(out=xx[R, :], in0=xx[R, :], scalar=st[R, 1:2], in1=gb[R, :], op0=ALU.mult, op1=ALU.mult), reads=[b_st, b_c], writes=[bx])
        k.op(k.dve, lambda e: e.tensor_tensor(out=xx[R, :], in0=xx[R, :], in1=bb[R, :], op=ALU.add), reads=[b_c], writes=[bx])
        k.dma(k.sp, out[r0:r0 + rows, :], xx[R, :], reads=[bx], writes=[b_out])
        if extra_out is not None:
            k.dma(k.sp, extra_out[r0:r0 + rows, :], xx[R, :], reads=[bx], writes=[b_out])
    return b_out
def peer_stage(k, nc, sb, ps, ident, b_ident, x, q, T, subkeys, u_tab, v_tab, f_out, tag, iota256, b_in=(), row_base=0):
    n = lambda s: "%s_%s" % (tag, s)
    X = mybir.AxisListType.X
    U32 = mybir.dt.uint32
    PS = [ps(n("ps%d" % i), [128, 512], F32) for i in range(8)]; bPS = [Buf() for _ in range(8)]
    skT = sb(n("skT"), [128, 16, 128], F32); b_sk = Buf()
    skin = sb(n("skin"), [128, 16, 128], F32); b_skin = Buf()
    k.dma(k.sp, skin[:], subkeys.rearrange("h c key d -> key (h c) d"), writes=[b_skin])
    for g in range(4):
        for j in range(4):
            k.op(k.pe, lambda e: e.transpose(PS[g % 2][:, j * 128:(j + 1) * 128], skin[:, g * 4 + j, :], ident[:, :]), reads=[b_skin, b_ident], writes=[bPS[g % 2]])
        k.op(k.dve, lambda e: e.tensor_copy(out=skT[:, g * 4:(g + 1) * 4, :], in_=PS[g % 2][:].rearrange("p (a c) -> p a c", a=4)), reads=[bPS[g % 2]], writes=[b_sk])
    qin = sb(n("qin"), [128, D], F32); b_qin = Buf()
    qT = sb(n("qT"), [128, 16, 128], F32); b_qT = Buf()
    s = sb(n("s"), [128, 16, 128], F32); b_s = Buf()
    s2 = sb(n("s2"), [128, 128], F32); b_s2 = Buf()
    hv = sb(n("hv"), [128, 16, 16], F32); b_hv = Buf()
    hiu = sb(n("hiu"), [128, 16, 16], U32); b_hiu = Buf()
    hi = sb(n("hi"), [128, 16, 16], F32); b_hi = Buf()
    cand = sb(n("cand"), [128, 256], F32); b_cand = Buf()
    cand2 = sb(n("cand2"), [128, 256], F32); b_cand2 = Buf()
    eg = sb(n("eg"), [128, 256], F32); b_eg = Buf()
    junk = sb(n("junk"), [128, 256], F32); b_junk = Buf()
    tv = sb(n("tv"), [128, 8, 16], F32); b_tv = Buf()
    posu = sb(n("posu"), [128, 8, 16], U32); b_posu = Buf()
    posf = sb(n("posf"), [128, 8, 16], F32); b_posf = Buf()
    iot = sb(n("iot"), [128, 256], F32); b_iot = Buf()
    k.dma(k.sp, iot[:], iota256, writes=[b_iot])
    ids = sb(n("ids"), [128, 128], F32); b_ids = Buf()
    gts = sb(n("gts"), [128, 8, 16], F32); b_gts = Buf()
    st = sb(n("st"), [128, 16], F32); b_st = Buf()
    idTs = [sb(n("idT%d" % i), [128, 128], I32) for i in range(2)]; b_idTs = [Buf(), Buf()]
    gT = sb(n("gT"), [128, 128], F32); b_gT = Buf()
    hTs = [sb(n("hT%d" % i), [128, 128], F32) for i in range(2)]; b_hTs = [Buf(), Buf()]
    Ug = [sb(n("Ug%d" % i), [128, D], F32) for i in range(2)]; b_Ug = [Buf(), Buf()]
    Vg = [sb(n("Vg%d" % i), [128, D], F32) for i in range(2)]; b_Vg = [Buf(), Buf()]
    xb = [sb(n("xb%d" % i), [128, D], F32) for i in range(2)]; b_xb = [Buf(), Buf()]
    big = sb(n("big"), [128, D], F32); b_big = Buf()
    wv = [sb(n("wv%d" % i), [128, 1], F32) for i in range(2)]; b_wv = [Buf(), Buf()]
    orow = [sb(n("orow%d" % i), [1, D], F32) for i in range(2)]; b_orow = [Buf(), Buf()]
    b_f = Buf()
    cnt = {"u": 0, "v": 0}

    def pass1_tok(it, r0, t):
        idT, b_idT = idTs[it % 2], b_idTs[it % 2]
        hT, b_hT = hTs[it % 2], b_hTs[it % 2]
        ug, bu = Ug[cnt["u"] % 2], b_Ug[cnt["u"] % 2]
        xx, bx = xb[cnt["u"] % 2], b_xb[cnt["u"] % 2]
        cnt["u"] += 1
        k.dma(k.sp, xx[:], x[r0 + t:r0 + t + 1, :].partition_broadcast(128), reads=list(b_in), writes=[bx])
        gather(k, ug[:], u_tab, idT[:, t:t + 1], reads=[b_idT], writes=[bu])
        k.op(k.dve, lambda e: e.scalar_tensor_tensor(out=big[:], in0=ug[:], scalar=1.0, in1=xx[:], op0=ALU.mult, op1=ALU.mult, accum_out=hT[:, t:t + 1]),
             reads=[bu, bx], writes=[b_big, b_hT])

    def pass2_tok(it, r0, t):
        idT, b_idT = idTs[it % 2], b_idTs[it % 2]
        hT, b_hT = hTs[it % 2], b_hTs[it % 2]
        vg, bv = Vg[cnt["v"] % 2], b_Vg[cnt["v"] % 2]
        orw, bo = orow[cnt["v"] % 2], b_orow[cnt["v"] % 2]
        cnt["v"] += 1
        gather(k, vg[:], v_tab, idT[:, t:t + 1], reads=[b_idT], writes=[bv])
        for c in range(4):
            k.op(k.pe, lambda e: e.matmul(PSrow(PS, 0, c), lhsT=hT[:, t:t + 1], rhs=vg[:, c * 512:(c + 1) * 512], start=True, stop=True),
                 reads=[b_hT, bv], writes=[bPS[4 + c]])
            k.op(k.act, lambda e: e.activation(out=orw[0:1, c * 512:(c + 1) * 512], in_=PSrow(PS, 0, c), func=AF.Copy), reads=[bPS[4 + c]], writes=[bo])
        k.dma(k.act, f_out[r0 + t:r0 + t + 1, :], orw[0:1, :], reads=[bo], writes=[b_f])

    prev = None
    for it, r0 in enumerate(range(0, T, 128)):
        rows = min(128, T - r0)
        R = slice(0, rows)
        idT, b_idT = idTs[it % 2], b_idTs[it % 2]
        hT, b_hT = hTs[it % 2], b_hTs[it % 2]
        k.dma(k.sp, qin[R, :], q[r0:r0 + rows, :], reads=list(b_in), writes=[b_qin])
        for g in range(4):
            for j in range(4):
                kc = g * 4 + j
                k.op(k.pe, lambda e: e.transpose(PS[g % 2][:, j * 128:j * 128 + rows], qin[R, kc * 128:(kc + 1) * 128], ident[R, R]), reads=[b_qin, b_ident], writes=[bPS[g % 2]])
            k.op(k.act, lambda e: e.activation(out=qT[:, g * 4:(g + 1) * 4, R], in_=PS[g % 2][:].rearrange("p (a c) -> p a c", a=4)[:, :, R], func=AF.Copy), reads=[bPS[g % 2]], writes=[b_qT])
        for g in range(4):
            for j in range(4):
                hc = g * 4 + j
                k.op(k.pe, lambda e: e.matmul(PS[2 + g % 2][R, j * 128:(j + 1) * 128], lhsT=qT[:, hc, R], rhs=skT[:, hc, :], start=True, stop=True), reads=[b_qT, b_sk], writes=[bPS[2 + g % 2]])
            k.op(k.dve, lambda e: e.tensor_copy(out=s[R, g * 4:(g + 1) * 4, :], in_=PS[2 + g % 2][R, :].rearrange("p (a c) -> p a c", a=4)), reads=[bPS[2 + g % 2]], writes=[b_s])
        for hc in range(16):
            k.op(k.dve, lambda e: e.max(out=hv[R, hc, 0:8], in_=s[R, hc, :]), reads=[b_s], writes=[b_hv])
            k.op(k.dve, lambda e: e.max_index(out=hiu[R, hc, 0:8], in_max=hv[R, hc, 0:8], in_values=s[R, hc, :]), reads=[b_s, b_hv], writes=[b_hiu])
            k.op(k.dve, lambda e: e.match_replace(out=s2[R, :], in_to_replace=hv[R, hc, 0:8], in_values=s[R, hc, :], imm_value=-1.0e30), reads=[b_s, b_hv], writes=[b_s2])
            k.op(k.dve, lambda e: e.max(out=hv[R, hc, 8:16], in_=s2[R, :]), reads=[b_s2], writes=[b_hv])
            k.op(k.dve, lambda e: e.max_index(out=hiu[R, hc, 8:16], in_max=hv[R, hc, 8:16], in_values=s2[R, :]), reads=[b_s2, b_hv], writes=[b_hiu])
        k.op(k.dve, lambda e: e.tensor_copy(out=hi[R], in_=hiu[R]), reads=[b_hiu], writes=[b_hi])
        for h in range(8):
            c3 = cand[R, :].rearrange("p (a b) -> p a b", a=16)
            k.op(k.dve, lambda e: e.tensor_tensor(out=c3, in0=hv[R, 2 * h, :].unsqueeze(2).to_broadcast([rows, 16, 16]),
                                                  in1=hv[R, 2 * h + 1, :].unsqueeze(1).to_broadcast([rows, 16, 16]), op=ALU.add), reads=[b_hv], writes=[b_cand])
            k.op(k.dve, lambda e: e.scalar_tensor_tensor(out=eg[R, :].rearrange("p (a b) -> p a b", a=16), in0=hi[R, 2 * h, :].unsqueeze(2).to_broadcast([rows, 16, 16]), scalar=128.0,
                                                         in1=hi[R, 2 * h + 1, :].unsqueeze(1).to_broadcast([rows, 16, 16]), op0=ALU.mult, op1=ALU.add), reads=[b_hi], writes=[b_eg])
            k.op(k.dve, lambda e: e.max(out=tv[R, h, 0:8], in_=cand[R, :]), reads=[b_cand], writes=[b_tv])
            k.op(k.dve, lambda e: e.max_index(out=posu[R, h, 0:8], in_max=tv[R, h, 0:8], in_values=cand[R, :]), reads=[b_cand, b_tv], writes=[b_posu])
            k.op(k.dve, lambda e: e.match_replace(out=cand2[R, :], in_to_replace=tv[R, h, 0:8], in_values=cand[R, :], imm_value=-1.0e30), reads=[b_cand, b_tv], writes=[b_cand2])
            k.op(k.dve, lambda e: e.max(out=tv[R, h, 8:16], in_=cand2[R, :]), reads=[b_cand2], writes=[b_tv])
            k.op(k.dve, lambda e: e.max_index(out=posu[R, h, 8:16], in_max=tv[R, h, 8:16], in_values=cand2[R, :]), reads=[b_cand2, b_tv], writes=[b_posu])
            k.op(k.dve, lambda e: e.tensor_copy(out=posf[R, h, :], in_=posu[R, h, :]), reads=[b_posu], writes=[b_posf])
            for kk_ in range(16):
                k.op(k.dve, lambda e: e.scalar_tensor_tensor(out=junk[R, :], in0=iot[R, :], scalar=posf[R, h, kk_:kk_ + 1], in1=eg[R, :], op0=ALU.is_equal, op1=ALU.mult,
                                                             accum_out=ids[R, h * 16 + kk_:h * 16 + kk_ + 1]), reads=[b_iot, b_posf, b_eg], writes=[b_junk, b_ids])
        k.op(k.dve, lambda e: e.tensor_tensor(out=gts[R], in0=tv[R], in1=tv[R, :, 0:1].to_broadcast([rows, 8, 16]), op=ALU.subtract), reads=[b_tv], writes=[b_gts])
        k.op(k.act, lambda e: e.activation(out=gts[R], in_=gts[R], func=AF.Exp), writes=[b_gts])
        k.op(k.dve, lambda e: e.tensor_reduce(out=st[R, 0:8], in_=gts[R], axis=X, op=ALU.add), reads=[b_gts], writes=[b_st])
        k.op(k.dve, lambda e: e.reciprocal(out=st[R, 0:8], in_=st[R, 0:8]), writes=[b_st])
        k.op(k.dve, lambda e: e.tensor_tensor(out=gts[R], in0=gts[R], in1=st[R, 0:8].unsqueeze(2).to_broadcast([rows, 8, 16]), op=ALU.mult), reads=[b_st], writes=[b_gts])
        k.op(k.pe, lambda e: e.transpose(PS[4][:, 0:rows], ids[R, :], ident[R, R]), reads=[b_ids, b_ident], writes=[bPS[4]])
        k.op(k.dve, lambda e: e.tensor_scalar(out=idT[:, R], in0=PS[4][:, 0:rows], scalar1=float(row_base), scalar2=None, op0=ALU.add), reads=[bPS[4]], writes=[b_idT])
        k.op(k.pe, lambda e: e.transpose(PS[5][:, 0:rows], gts[R].rearrange("p a b -> p (a b)"), ident[R, R]), reads=[b_gts, b_ident], writes=[bPS[5]])
        k.op(k.act, lambda e: e.activation(out=gT[:, R], in_=PS[5][:, 0:rows], func=AF.Copy), reads=[bPS[5]], writes=[b_gT])
        n2 = prev[2] if prev is not None else 0
        for t in range(max(rows, n2)):
            if t < rows:
                pass1_tok(it, r0, t)
            if t < n2:
                pass2_tok(prev[0], prev[1], t)
        k.op(k.act, lambda e: e.activation(out=hT[:, R], in_=hT[:, R], func=AF.Gelu), writes=[b_hT])
        k.op(k.dve, lambda e: e.tensor_tensor(out=hT[:, R], in0=hT[:, R], in1=gT[:, R], op=ALU.mult), reads=[b_gT], writes=[b_hT])
        prev = (it, r0, rows)
    for t in range(prev[2]):
        pass2_tok(prev[0], prev[1], t)
    return b_f


def PSrow(PS, tok, c):
    return PS[4 + c][0:1, 0:512]


def gather(k, out_ap, table, idx_ap, reads, writes):
    Q = k.pool
    ds = k.dsems[k.dnext]
    k.dnext = (k.dnext + 1) % len(k.dsems)
    toks = k._deps(reads, writes)
    if ds.val:
        toks.append((ds.h, ds.val, ds.key))
    last = k._waits(Q, toks)
    inst = Q.eng.indirect_dma_start(out=out_ap, out_offset=None, in_=table, in_offset=bass.IndirectOffsetOnAxis(ap=idx_ap, axis=0))
    if last is not None:
        inst.wait_op(last[0], last[1], "sem-ge")
    ds.val += 16
    inst.then_inc(ds.h, 16)
    k._mark((ds.h, ds.val, ds.key), reads, writes)
    return inst
_UID = itertools.count()


def nsas_host_consts(P):
    nsel = P // 64 + 1
    r = np.arange(16)
    tok = r % 4
    sel16 = (tok[:, None] == np.arange(4)[None, :]).astype(np.float32)
    biasN = np.where(np.arange(4)[None, :] <= tok[:, None], 0.0, NEG).astype(np.float32)
    biasW = np.zeros((16, 516), np.float32)
    biasW[:, 0:512] = np.where(np.arange(512)[None, :] >= tok[:, None] + 1, 0.0, NEG)
    biasW[:, 512:516] = biasN
    blk = np.arange(nsel)
    forced = (blk == 0) | (blk == nsel - 1) | (blk == nsel - 2)
    FM = np.tile(np.where(forced, 0.0, 1.0).astype(np.float32)[None, :], (4, 1))
    FA = np.tile(np.where(forced, 1.0e4, 0.0).astype(np.float32)[None, :], (4, 1))
    return {"ns_iota": np.arange(128, dtype=np.float32)[:, None].copy(), "ns_sel16": sel16, "ns_selT": sel16.T.copy(),
            "ns_biasN": biasN, "ns_biasW": biasW, "ns_FM": FM, "ns_FA": FA}


def nsa_sample_stage(k, nc, sb, ps, ident, b_ident, proj, row0, P, cmp_w, cache_cmp, cache_sel, page_row, swin, CN, CS,
                     past_cmp, past_sel, osc, ymix, tag, b_in=(), row_base=0):
    NP = P // 128
    NB = P // 32
    NSEL = P // 64 + 1
    X = mybir.AxisListType.X
    n = lambda s: "%s_%s" % (tag, s)
    PS = [ps(n("ps%d" % i), [128, 512], F32) for i in range(8)]; bPS = [Buf() for _ in range(8)]
    b_c = Buf()
    def cload(name, shape, src, dt=F32):
        t = sb(n(name), shape, dt)
        k.dma(k.sp, t[:], src, writes=[b_c])
        return t
    iota = cload("iota", [128, 1], CS["ns_iota"]); mask4 = cload("mask4", [128, 4], CN["nsa_mask4"])
    sel16 = cload("sel16", [16, 4], CS["ns_sel16"]); selT = cload("selT", [4, 16], CS["ns_selT"])
    biasN = cload("biasN", [16, 4], CS["ns_biasN"]); biasW = cload("biasW", [16, 516], CS["ns_biasW"])
    FM = cload("FM", [4, NSEL], CS["ns_FM"]); FA = cload("FA", [4, NSEL], CS["ns_FA"])
    wcol = sb(n("wcol"), [128, 2], F32); b_wcol = Buf()
    for c in range(2):
        for j in range(4):
            k.dma(k.sp, wcol[j * 32:(j + 1) * 32, c:c + 1], cmp_w[c:c + 1, :].rearrange("o l -> l o"), writes=[b_wcol])
    Wb = sb(n("Wb"), [128, 4], F32); Ww = sb(n("Ww"), [128, 252], F32); b_W = Buf()
    k.op(k.dve, lambda e: e.memset(Ww[:], 0.0), writes=[b_W])
    k.op(k.dve, lambda e: e.tensor_scalar(out=Wb[:], in0=mask4[:], scalar1=wcol[:, 0:1], scalar2=None, op0=ALU.mult), reads=[b_c, b_wcol], writes=[b_W])
    k.op(k.dve, lambda e: e.tensor_scalar(out=Ww[:, 124:128], in0=mask4[:], scalar1=wcol[:, 1:2], scalar2=None, op0=ALU.mult), reads=[b_c, b_wcol], writes=[b_W])
    ptb = sb(n("ptb"), [128, NP], I32); ptf = sb(n("ptf"), [128, NP], F32); idx = sb(n("idx"), [128, NP], I32); b_idx = Buf()
    k.dma(k.sp, ptb[:], page_row.partition_broadcast(128), writes=[b_idx])
    k.op(k.dve, lambda e: e.tensor_copy(out=ptf[:], in_=ptb[:]), writes=[b_idx])
    k.op(k.dve, lambda e: e.tensor_scalar(out=ptf[:], in0=ptf[:], scalar1=128.0, scalar2=iota[:, 0:1], op0=ALU.mult, op1=ALU.add), reads=[b_c], writes=[b_idx])
    k.op(k.dve, lambda e: e.tensor_scalar(out=idx[:], in0=ptf[:], scalar1=float(row_base), scalar2=None, op0=ALU.add), writes=[b_idx])
    qin = sb(n("qin"), [T_S, 1024], F32); b_qin = Buf()
    kvn = sb(n("kvn"), [T_S, 1536], F32); b_kvn = Buf()
    qT = sb(n("qT"), [64, 4, 16], F32); b_qT = Buf()
    kTn = sb(n("kTn"), [64, 2, 4, T_S], F32); b_kTn = Buf()
    st = sb(n("st"), [16, 4], F32); b_st = Buf()
    impv = sb(n("impv"), [4, NSEL], F32); impw = sb(n("impw"), [4, NSEL], F32); b_imp = Buf()
    mx = sb(n("mx"), [4, 16], F32); b_mx = Buf()
    s01 = sb(n("s01"), [4, NSEL], F32); b_s01 = Buf()
    selb = sb(n("selb"), [16, 4, NSEL], F32); b_selb = Buf()
    eT = [sb(n("eT%d" % i), [128, 16], F32) for i in range(2)]; b_eT = [Buf(), Buf()]
    ob = sb(n("ob"), [16, 64], F32); b_ob = Buf()
    oc = sb(n("oc"), [T_S, 3, 16, 64], F32); b_oc = Buf()
    gin = sb(n("gin"), [T_S, 48], F32); b_g = Buf()
    yo = sb(n("yo"), [T_S, 1024], F32); b_yo = Buf()
    phase = contextlib.ExitStack()
    sbx = lambda name, shape, dt: phase.enter_context(nc.sbuf_tensor(n(name) + "_x%d" % next(_UID), shape, dt))
    pg = [sbx("pg%d" % i, [128, 512], F32) for i in range(4)]; b_pg = [Buf() for _ in range(4)]
    b_past = Buf()
    for s_ in range(NP):
        for ci, (cache, past) in enumerate(((cache_cmp, past_cmp), (cache_sel, past_sel))):
            t_, bt = pg[(2 * s_ + ci) % 4], b_pg[(2 * s_ + ci) % 4]
            gather(k, t_[:], cache, idx[:, s_:s_ + 1], reads=[b_idx], writes=[bt])
            k.dma(k.sp, past[s_ * 128:(s_ + 1) * 128, :], t_[:], reads=[bt], writes=[b_past])
    k.barrier()
    phase.close()
    phase = contextlib.ExitStack()
    k.dma(k.sp, qin[:], proj[row0:row0 + T_S, NSA0:NSA0 + 1024], reads=list(b_in), writes=[b_qin])
    k.dma(k.sp, kvn[:], proj[row0:row0 + T_S, KV0:KV0 + 1536], reads=list(b_in), writes=[b_kvn])
    for h in range(16):
        k.op(k.pe, lambda e: e.transpose(PS[0][0:64, h * 4:h * 4 + 4], qin[:, h * 64:(h + 1) * 64], ident[0:T_S, 0:T_S]), reads=[b_qin, b_ident], writes=[bPS[0]])
    k.op(k.act, lambda e: e.activation(out=qT[:].rearrange("p a b -> p (a b)"), in_=PS[0][0:64, 0:64], func=AF.Copy, scale=0.125), reads=[bPS[0]], writes=[b_qT])
    for bi, off in enumerate((512, 1024)):
        for kv in range(4):
            k.op(k.pe, lambda e: e.transpose(PS[1][0:64, (bi * 4 + kv) * 4:(bi * 4 + kv) * 4 + 4], kvn[:, off + kv * 64:off + (kv + 1) * 64], ident[0:T_S, 0:T_S]),
                 reads=[b_kvn, b_ident], writes=[bPS[1]])
    k.op(k.dve, lambda e: e.tensor_copy(out=kTn[:].rearrange("p a b c -> p (a b c)"), in_=PS[1][0:64, 0:32]), reads=[bPS[1]], writes=[b_kTn])
    kcbT = sbx("kcbT", [64, 4, NB], F32); b_kcb = Buf()
    nvt = (NB + 127) // 128
    vcb = sbx("vcb", [128, nvt, 256], F32); b_vcb = Buf()
    pin = [sbx("pin%d" % i, [128, 512], F32) for i in range(2)]; b_pin = [Buf(), Buf()]
    for s_ in range(NP):
        pt_, bp = pin[s_ % 2], b_pin[s_ % 2]
        k.dma(k.sp, pt_[:], past_cmp[s_ * 128:(s_ + 1) * 128, :], reads=[b_past], writes=[bp])
        for kv in range(4):
            k.op(k.pe, lambda e: e.matmul(PS[kv][0:64, 4 * s_:4 * s_ + 4], lhsT=pt_[:, kv * 64:(kv + 1) * 64], rhs=Wb[:, :], start=True, stop=True, skip_group_check=True),
                 reads=[bp, b_W], writes=[bPS[kv]])
        vt_, off = s_ // 32, 4 * (s_ % 32)
        k.op(k.pe, lambda e: e.matmul(PS[4 + vt_ % 4][:, 0:256], lhsT=Ww[:, 124 - off:252 - off], rhs=pt_[:, 256:512], start=(s_ % 32 == 0), stop=(s_ % 32 == 31 or s_ == NP - 1)),
             reads=[bp, b_W], writes=[bPS[4 + vt_ % 4]])
    for kv in range(4):
        k.op(k.dve, lambda e: e.tensor_copy(out=kcbT[:, kv, :], in_=PS[kv][0:64, 0:NB]), reads=[bPS[kv]], writes=[b_kcb])
    for vt_ in range(nvt):
        k.op(k.act, lambda e: e.activation(out=vcb[:, vt_, :], in_=PS[4 + vt_ % 4][:, 0:256], func=AF.Copy), reads=[bPS[4 + vt_ % 4]], writes=[b_vcb])
    wst = sbx("wst", [128, 4, 512], F32); b_wst = Buf()
    k.dma(k.sp, wst[:], swin.rearrange("(a p) c -> p a c", p=128), writes=[b_wst])
    KTw = sbx("KTw", [64, 4, 516], F32); b_KTw = Buf()
    for a in range(4):
        for kv in range(4):
            k.op(k.pe, lambda e: e.transpose(PS[a % 2][0:64, kv * 128:(kv + 1) * 128], wst[:, a, kv * 64:(kv + 1) * 64], ident[:, :]), reads=[b_wst, b_ident], writes=[bPS[a % 2]])
        k.op(k.dve, lambda e: e.tensor_copy(out=KTw[:, :, a * 128:(a + 1) * 128], in_=PS[a % 2][0:64, :].rearrange("p (a c) -> p a c", a=4)), reads=[bPS[a % 2]], writes=[b_KTw])
    k.op(k.dve, lambda e: e.tensor_copy(out=KTw[:, :, 512:516], in_=kTn[:, 1, :, :]), reads=[b_kTn], writes=[b_KTw])
    scB = sbx("scB", [16, max(NB, 516)], F32)
    sc = scB; b_sc = Buf()
    pc = sbx("pc", [16, NB], F32); b_pc = Buf()
    b_osc = Buf()
    net = [0]

    def softmax_pv(ncols, vtiles, slot, kv):
        k.op(k.dve, lambda e: e.tensor_reduce(out=st[:, 0:1], in_=sc[:, 0:ncols], axis=X, op=ALU.max), reads=[b_sc], writes=[b_st])
        k.op(k.dve, lambda e: e.tensor_scalar(out=st[:, 0:1], in0=st[:, 0:1], scalar1=-1.0e20, scalar2=-1.0, op0=ALU.max, op1=ALU.mult), writes=[b_st])
        k.op(k.act, lambda e: e.activation(out=sc[:, 0:ncols], in_=sc[:, 0:ncols], func=AF.Exp, bias=st[:, 0:1], scale=1.0), reads=[b_st], writes=[b_sc])
        k.op(k.dve, lambda e: e.tensor_reduce(out=st[:, 1:2], in_=sc[:, 0:ncols], axis=X, op=ALU.add), reads=[b_sc], writes=[b_st])
        k.op(k.dve, lambda e: e.tensor_scalar(out=st[:, 1:2], in0=st[:, 1:2], scalar1=1.0e-30, scalar2=None, op0=ALU.max), writes=[b_st])
        k.op(k.dve, lambda e: e.reciprocal(out=st[:, 1:2], in_=st[:, 1:2]), writes=[b_st])
        for vi, (c0, w, vap, bv) in enumerate(vtiles):
            pt, bpt = PS[4 + net[0] % 2], bPS[4 + net[0] % 2]
            et, bet = eT[net[0] % 2], b_eT[net[0] % 2]
            net[0] += 1
            k.op(k.pe, lambda e: e.transpose(pt[0:w, 0:16], sc[:, c0:c0 + w], ident[0:16, 0:16]), reads=[b_sc, b_ident], writes=[bpt])
            k.op(k.act, lambda e: e.activation(out=et[0:w, :], in_=pt[0:w, 0:16], func=AF.Copy), reads=[bpt], writes=[bet])
            k.op(k.pe, lambda e: e.matmul(PS[6][0:16, 0:64], lhsT=et[0:w, :], rhs=vap, start=(vi == 0), stop=(vi == len(vtiles) - 1)),
                 reads=[bet, bv], writes=[bPS[6]])
        k.op(k.dve, lambda e: e.tensor_scalar(out=ob[:], in0=PS[6][0:16, 0:64], scalar1=st[:, 1:2], scalar2=None, op0=ALU.mult), reads=[bPS[6], b_st], writes=[b_ob])
        for g_ in range(4):
            k.dma(k.sp, osc[slot, :, kv * 4 + g_, :], ob[g_ * 4:(g_ + 1) * 4, :], reads=[b_ob], writes=[b_osc])

    for kv in range(4):
        q_kv = qT[:, kv, :]
        for c0 in range(0, NB, 512):
            w = min(512, NB - c0)
            k.op(k.pe, lambda e: e.matmul(PS[0][0:16, 0:w], lhsT=q_kv, rhs=kcbT[:, kv, c0:c0 + w], start=True, stop=True), reads=[b_qT, b_kcb], writes=[bPS[0]])
            k.op(k.act, lambda e: e.activation(out=sc[:, c0:c0 + w], in_=PS[0][0:16, 0:w], func=AF.Copy), reads=[bPS[0]], writes=[b_sc])
        softmax_pv(NB, [(j * 128, min(128, NB - j * 128), vcb[0:min(128, NB - j * 128), j, kv * 64:(kv + 1) * 64], b_vcb) for j in range(nvt)], 0, kv)
        k.op(k.dve, lambda e: e.tensor_scalar(out=pc[:], in0=sc[:, 0:NB], scalar1=st[:, 1:2], scalar2=None, op0=ALU.mult), reads=[b_sc, b_st], writes=[b_pc])
        for c0 in range(0, NB, 512):
            w = min(512, NB - c0)
            k.op(k.pe, lambda e: e.matmul(PS[1][0:4, 0:w], lhsT=sel16[:, :], rhs=pc[:, c0:c0 + w], start=True, stop=True), reads=[b_pc, b_c], writes=[bPS[1]])
            k.op(k.dve, lambda e: e.tensor_reduce(out=impv[:, c0 // 2:(c0 + w) // 2], in_=PS[1][0:4, 0:w].rearrange("p (a c) -> p a c", c=2), axis=X, op=ALU.add), reads=[bPS[1]], writes=[b_imp])
        k.op(k.dve, lambda e: e.memset(impv[:, NSEL - 1:NSEL], 0.0), writes=[b_imp])
        k.op(k.dve, lambda e: e.tensor_tensor(out=impv[:], in0=impv[:], in1=FM[:], op=ALU.mult), reads=[b_c], writes=[b_imp])
        k.op(k.dve, lambda e: e.tensor_tensor(out=impv[:], in0=impv[:], in1=FA[:], op=ALU.add), reads=[b_c], writes=[b_imp])
        k.op(k.dve, lambda e: e.max(out=mx[:, 0:8], in_=impv[:]), reads=[b_imp], writes=[b_mx])
        k.op(k.dve, lambda e: e.match_replace(out=impw[:], in_to_replace=mx[:, 0:8], in_values=impv[:], imm_value=-2.0), reads=[b_mx], writes=[b_imp])
        k.op(k.dve, lambda e: e.max(out=mx[:, 8:16], in_=impw[:]), reads=[b_imp], writes=[b_mx])
        k.op(k.dve, lambda e: e.tensor_scalar(out=mx[:, 15:16], in0=mx[:, 15:16], scalar1=0.0, scalar2=None, op0=ALU.max), writes=[b_mx])
        k.op(k.dve, lambda e: e.tensor_scalar(out=s01[:], in0=impv[:], scalar1=mx[:, 15:16], scalar2=None, op0=ALU.is_ge), reads=[b_imp, b_mx], writes=[b_s01])
        k.op(k.pe, lambda e: e.matmul(PS[1][0:16, 0:NSEL], lhsT=selT[:, :], rhs=s01[:, :], start=True, stop=True), reads=[b_s01, b_c], writes=[bPS[1]])
        k.op(k.dve, lambda e: e.tensor_scalar(out=selb[:, kv, :], in0=PS[1][0:16, 0:NSEL], scalar1=-1.0, scalar2=1.0e30, op0=ALU.add, op1=ALU.mult), reads=[bPS[1]], writes=[b_selb])
        for c0 in range(0, 516, 512):
            w = min(512, 516 - c0)
            k.op(k.pe, lambda e: e.matmul(PS[0][0:16, 0:w], lhsT=q_kv, rhs=KTw[:, kv, c0:c0 + w], start=True, stop=True), reads=[b_qT, b_KTw], writes=[bPS[0]])
            k.op(k.dve, lambda e: e.tensor_tensor(out=sc[:, c0:c0 + w], in0=PS[0][0:16, 0:w], in1=biasW[:, c0:c0 + w], op=ALU.add), reads=[bPS[0], b_c], writes=[b_sc])
        vt = [(a * 128, 128, wst[:, a, 256 + kv * 64:256 + (kv + 1) * 64], b_wst) for a in range(4)] + [(512, T_S, kvn[:, 1280 + kv * 64:1280 + (kv + 1) * 64], b_kvn)]
        softmax_pv(516, vt, 2, kv)
    k.barrier()
    phase.close()
    phase = contextlib.ExitStack()
    KTs = sbx("KTs", [64, P + T_S], F32); b_KTs = Buf()
    Vsp = sbx("Vsp", [128, NP, 64], F32); b_Vsp = Buf()
    scC = sbx("scC", [16, P + 64], F32)
    sc = scC; b_sc = Buf()
    pin = [sbx("pinC%d" % i, [128, 64], F32) for i in range(2)]; b_pin = [Buf(), Buf()]
    for kv in range(4):
        q_kv = qT[:, kv, :]
        for s_ in range(NP):
            pt_, bp = pin[s_ % 2], b_pin[s_ % 2]
            k.dma(k.sp, pt_[:, 0:64], past_sel[s_ * 128:(s_ + 1) * 128, kv * 64:(kv + 1) * 64], reads=[b_past], writes=[bp])
            k.dma(k.sp, Vsp[:, s_, :], past_sel[s_ * 128:(s_ + 1) * 128, 256 + kv * 64:256 + (kv + 1) * 64], reads=[b_past], writes=[b_Vsp])
            pp_, bpp = PS[2 + (s_ // 4) % 2], bPS[2 + (s_ // 4) % 2]
            k.op(k.pe, lambda e: e.transpose(pp_[0:64, (s_ % 4) * 128:(s_ % 4 + 1) * 128], pt_[:, 0:64], ident[:, :]), reads=[bp, b_ident], writes=[bpp])
            if s_ % 4 == 3 or s_ == NP - 1:
                a0 = (s_ // 4) * 4
                wd = (s_ - a0 + 1) * 128
                k.op(k.dve, lambda e: e.tensor_copy(out=KTs[:, a0 * 128:a0 * 128 + wd], in_=pp_[0:64, 0:wd]), reads=[bpp], writes=[b_KTs])
        k.op(k.dve, lambda e: e.tensor_copy(out=KTs[:, P:P + T_S], in_=kTn[:, 0, kv, :]), reads=[b_kTn], writes=[b_KTs])
        for c0 in range(0, P, 512):
            w = min(512, P - c0)
            pt, bpt = PS[(c0 // 512) % 2], bPS[(c0 // 512) % 2]
            k.op(k.pe, lambda e: e.matmul(pt[0:16, 0:w], lhsT=q_kv, rhs=KTs[:, c0:c0 + w], start=True, stop=True), reads=[b_qT, b_KTs], writes=[bpt])
            nb = w // 64
            k.op(k.dve, lambda e: e.tensor_tensor(out=sc[:, c0:c0 + w].rearrange("p (a c) -> p a c", c=64), in0=pt[0:16, 0:w].rearrange("p (a c) -> p a c", c=64),
                                                  in1=selb[:, kv, c0 // 64:c0 // 64 + nb].unsqueeze(2).to_broadcast([16, nb, 64]), op=ALU.add), reads=[bpt, b_selb], writes=[b_sc])
        k.op(k.pe, lambda e: e.matmul(PS[0][0:16, 0:T_S], lhsT=q_kv, rhs=KTs[:, P:P + T_S], start=True, stop=True), reads=[b_qT, b_KTs], writes=[bPS[0]])
        k.op(k.dve, lambda e: e.tensor_tensor(out=sc[:, P:P + T_S], in0=PS[0][0:16, 0:T_S], in1=biasN[:, :], op=ALU.add), reads=[bPS[0], b_c], writes=[b_sc])
        vt = [(s_ * 128, 128, Vsp[:, s_, :], b_Vsp) for s_ in range(NP)] + [(P, T_S, kvn[:, 768 + kv * 64:768 + (kv + 1) * 64], b_kvn)]
        softmax_pv(P + T_S, vt, 1, kv)
    k.barrier()
    phase.close()
    k.dma(k.sp, oc[:], osc.rearrange("s t h c -> t s h c"), reads=[b_osc], writes=[b_oc])
    k.dma(k.sp, gin[:], proj[row0:row0 + T_S, KV0 + 1536:KV0 + 1584], reads=list(b_in), writes=[b_g])
    k.op(k.act, lambda e: e.activation(out=gin[:], in_=gin[:], func=AF.Sigmoid), writes=[b_g])
    g3 = gin[:].rearrange("p (h c) -> p h c", c=3)
    y3 = yo[:].rearrange("p (h c) -> p h c", h=16)
    k.op(k.dve, lambda e: e.tensor_tensor(out=y3, in0=oc[:, 0, :, :], in1=g3[:, :, 0:1].to_broadcast([T_S, 16, 64]), op=ALU.mult), reads=[b_oc, b_g], writes=[b_yo])
    for s_ in (1, 2):
        k.op(k.dve, lambda e: e.tensor_tensor(out=oc[:, s_, :, :], in0=oc[:, s_, :, :], in1=g3[:, :, s_:s_ + 1].to_broadcast([T_S, 16, 64]), op=ALU.mult), reads=[b_g], writes=[b_oc])
        k.op(k.dve, lambda e: e.tensor_tensor(out=y3, in0=y3, in1=oc[:, s_, :, :], op=ALU.add), reads=[b_oc], writes=[b_yo])
    b_y = Buf()
    k.dma(k.sp, ymix[:, 1024:2048], yo[:], reads=[b_yo], writes=[b_y])
    return b_y
NCU = 4
RW_NAMES = ["rwkv_mu", "rwkv_w0", "rwkv_w2", "rwkv_a0", "rwkv_a2", "rwkv_g2", "rwkv_k_k", "rwkv_k_a", "rwkv_r_k", "rwkv_gn_g", "rwkv_gn_b"]


def build_nc(T_Pc=T_P, P=16384, n_pool=1280, NS=2):
    nc = bass.Bass("TRN2", target_bir_lowering=False)
    NP = P // 128
    NTOK = T_Pc + NS * T_S
    CNH = nsa_host_consts()
    CSH = nsas_host_consts(P)
    din = lambda name, shape, d=F32: nc.dram_tensor(name, list(shape), d, kind="ExternalInput").ap()
    dout = lambda name, shape: nc.dram_tensor(name, list(shape), F32, kind="ExternalOutput").ap()
    dscr = lambda name, shape: nc.dram_tensor(name, list(shape), F32, kind="Internal").ap()
    xp = din("xp", [T_Pc, D]); xs = din("xs", [NS * T_S, D])
    w_in = din("w_in", [DEPTH, D, IN_COLS]); w_out = din("w_out", [DEPTH, D, D]); wq = din("wq", [DEPTH, D, D])
    skeys = din("skeys", [DEPTH, 8, 2, 128, 128]); pu = din("pu", [DEPTH, 16384, D]); pv = din("pv", [DEPTH, 16384, D])
    lnp = {nm: din(nm, [DEPTH, 1, D]) for nm in ("ln1_g", "ln1_b", "ln2_g", "ln2_b")}
    rwp = {}
    for nm in RW_NAMES:
        shp = {"rwkv_mu": [1, RW], "rwkv_w2": [64, 1024], "rwkv_a2": [64, 1024], "rwkv_g2": [160, 1024]}.get(nm, [1, 1024])
        rwp[nm] = din(nm, [DEPTH] + shp)
    cmpw = din("cmp_w", [DEPTH, 2, 32])
    cc = din("cc", [DEPTH, n_pool * 128, 512]); cs = din("cs", [DEPTH, n_pool * 128, 512])
    ptab = din("ptab", [NS, 1, NP], I32)
    swin = din("swin", [DEPTH, NS, 512, 512]); srw = din("srw", [DEPTH, NS, 16, 64, 64]); ssh = din("ssh", [DEPTH, NS, 1, RW])
    zrow = din("zrow", [1, RW]); ident_d = din("ident", [128, 128]); iota256 = din("iota256", [128, 256])
    CN = {n_: din(n_, v.shape) for n_, v in CNH.items()}
    CS = {n_: din(n_, v.shape) for n_, v in CSH.items()}
    y_p = dout("y_p", [T_Pc, D]); y_s = dout("y_s", [NS * T_S, D])
    cmp_p = dout("cmp_p", [DEPTH, T_Pc, 512]); sel_p = dout("sel_p", [DEPTH, T_Pc, 512]); win_p = dout("win_p", [DEPTH, 512, 512])
    rw_p = dout("rw_p", [DEPTH, 16, 64, 64]); sh_p = dout("sh_p", [DEPTH, 1, RW])
    cmp_s = dout("cmp_s", [DEPTH, NS, T_S, 512]); sel_s = dout("sel_s", [DEPTH, NS, T_S, 512]); win_s = dout("win_s", [DEPTH, NS, 512, 512])
    rw_s = dout("rw_s", [DEPTH, NS, 16, 64, 64]); sh_s = dout("sh_s", [DEPTH, NS, 1, RW])
    Xa = dscr("Xa", [NTOK, D]); X1 = dscr("X1", [NTOK, D]); Hs = dscr("Hs", [NTOK, D]); Qs = dscr("Qs", [NTOK, D]); Fs = dscr("Fs", [NTOK, D])
    proj = dscr("proj", [NTOK, IN_COLS]); ymix = dscr("ymix", [NTOK, D]); opsd = dscr("opsd", [max(T_Pc, T_S), 2, 5, 512])
    pcmp = dscr("pcmp", [P, 512]); psel = dscr("psel", [P, 512]); osc = dscr("osc", [3, T_S, 16, 64])

    with contextlib.ExitStack() as ctx:
        k = K(ctx, nc)
        ident = ctx.enter_context(nc.sbuf_tensor("ident_sb", [128, 128], F32)); b_ident = Buf()
        k.dma(k.sp, ident[:], ident_d[:, :], writes=[b_ident])

        uid = [0]

        class Scope:
            def __enter__(self):
                self.st = contextlib.ExitStack()
                def uniq(name):
                    uid[0] += 1
                    return "%s_u%d" % (name, uid[0])
                self.sb = lambda name, shape, d: self.st.enter_context(nc.sbuf_tensor(uniq(name), shape, d))
                self.ps = lambda name, shape, d: self.st.enter_context(nc.psum_tensor(uniq(name), shape, d))
                return self

            def __exit__(self, *a):
                if a[0] is not None:
                    return False
                k.barrier()
                self.st.close()

        for r0 in range(0, T_Pc, 512):
            k.dma(k.sp, Xa[r0:r0 + 512, :], xp[r0:r0 + 512, :])
        k.dma(k.sp, Xa[T_Pc:NTOK, :], xs[:, :])
        k.barrier()
        for l in range(DEPTH):
            L = "l%d" % l
            with Scope() as s:
                linear_stage(k, nc, s.sb, s.ps, ident, b_ident, Xa, NTOK, w_in[l], IN_COLS, proj, L + "li")
            for r0 in range(0, T_Pc, 512):
                k.dma(k.sp, cmp_p[l, r0:r0 + 512, :], proj[r0:r0 + 512, KV0:KV0 + 512])
                k.dma(k.sp, sel_p[l, r0:r0 + 512, :], proj[r0:r0 + 512, KV0 + 512:KV0 + 1024])
            k.dma(k.sp, win_p[l, :, :], proj[T_Pc - 512:T_Pc, KV0 + 1024:KV0 + 1536])
            k.dma(k.sp, sh_p[l, :, :], proj[T_Pc - 1:T_Pc, 0:RW])
            for j in range(NS):
                s0 = T_Pc + j * T_S
                k.dma(k.sp, cmp_s[l, j, :, :], proj[s0:s0 + T_S, KV0:KV0 + 512])
                k.dma(k.sp, sel_s[l, j, :, :], proj[s0:s0 + T_S, KV0 + 512:KV0 + 1024])
                k.dma(k.sp, win_s[l, j, 512 - T_S:512, :], proj[s0:s0 + T_S, KV0 + 1024:KV0 + 1536])
                k.dma(k.sp, win_s[l, j, 0:512 - T_S, :], swin[l, j, T_S:512, :])
                k.dma(k.sp, sh_s[l, j, :, :], proj[s0 + T_S - 1:s0 + T_S, 0:RW])
            W = {nm: rwp[nm][l] for nm in RW_NAMES}
            with Scope() as s:
                C = load_rwkv_consts(k, nc, s.sb, W)
                with Scope() as s2:
                    rwkv_stage(k, nc, s2.sb, s2.ps, C, ident, b_ident, proj, 0, T_Pc, zrow, None, opsd, ymix[0:T_Pc], rw_p[l], L + "rp")
                for j in range(NS):
                    s0 = T_Pc + j * T_S
                    with Scope() as s2:
                        rwkv_stage(k, nc, s2.sb, s2.ps, C, ident, b_ident, proj, s0, T_S, ssh[l, j], srw[l, j], opsd, ymix[s0:s0 + T_S], rw_s[l, j], L + "r%d" % j)
            with Scope() as s:
                nsa_prompt_stage(k, nc, s.sb, s.ps, ident, b_ident, proj, 0, T_Pc, cmpw[l], CN, ymix[0:T_Pc], L + "np")
            for j in range(NS):
                s0 = T_Pc + j * T_S
                with Scope() as s:
                    nsa_sample_stage(k, nc, s.sb, s.ps, ident, b_ident, proj, s0, P, cmpw[l], cc.rearrange("l r c -> (l r) c"), cs.rearrange("l r c -> (l r) c"),
                                     ptab[j], swin[l, j], CN, CS, pcmp, psel, osc, ymix[s0:s0 + T_S], L + "q%d" % j, row_base=l * n_pool * 128)
            with Scope() as s:
                linear_stage(k, nc, s.sb, s.ps, ident, b_ident, ymix, NTOK, w_out[l], D, Hs, L + "lo")
            with Scope() as s:
                ln_stage(k, nc, s.sb, Xa, Hs, NTOK, lnp["ln1_g"][l], lnp["ln1_b"][l], X1, L + "n1")
            with Scope() as s:
                linear_stage(k, nc, s.sb, s.ps, ident, b_ident, X1, NTOK, wq[l], D, Qs, L + "lq")
            with Scope() as s:
                peer_stage(k, nc, s.sb, s.ps, ident, b_ident, X1, Qs, NTOK, skeys[l], pu.rearrange("l e d -> (l e) d"), pv.rearrange("l e d -> (l e) d"), Fs, L + "pe",
                           iota256, row_base=l * 16384)
            with Scope() as s:
                ln_stage(k, nc, s.sb, X1, Fs, NTOK, lnp["ln2_g"][l], lnp["ln2_b"][l], Xa, L + "n2")
        for r0 in range(0, T_Pc, 512):
            k.dma(k.sp, y_p[r0:r0 + 512, :], Xa[r0:r0 + 512, :])
        k.dma(k.sp, y_s[:, :], Xa[T_Pc:NTOK, :])
        k.finish()
    return nc


def make_in_maps(inputs, ncu, T_Pc, P, n_pool, NS):
    f = lambda nm: np.ascontiguousarray(np.asarray(inputs[nm], np.float32))
    xp, xs = f("x_prompt"), f("x_sample")
    shared = {
        "w_in": f("w_in"), "w_out": f("w_out"), "wq": f("peer_wq"), "skeys": f("peer_subkeys"), "pu": f("peer_u"), "pv": f("peer_v"),
        "cmp_w": f("nsa_cmp_w"), "cc": f("cache_cmp_kv").reshape(DEPTH, n_pool * 128, 512), "cs": f("cache_sel_kv").reshape(DEPTH, n_pool * 128, 512),
        "zrow": np.zeros((1, RW), np.float32), "ident": np.eye(128, dtype=np.float32),
        "iota256": np.tile(np.arange(256, dtype=np.float32)[None, :], (128, 1)),
    }
    for nm in ("ln1_g", "ln1_b", "ln2_g", "ln2_b"):
        shared[nm] = f(nm).reshape(DEPTH, 1, D)
    for nm in RW_NAMES:
        a = f(nm)
        shared[nm] = a if a.ndim == 3 and nm in ("rwkv_w2", "rwkv_a2", "rwkv_g2") else a.reshape(DEPTH, 1, -1)
    shared.update(nsa_host_consts())
    shared.update(nsas_host_consts(P))
    pt = np.asarray(inputs["page_table"], np.int32)
    swin, srw, ssh = f("state_win_kv"), f("state_rwkv"), f("state_shift")
    maps = []
    for c in range(ncu):
        sidx = [c + j * ncu for j in range(NS)]
        m = dict(shared)
        m["xp"] = np.ascontiguousarray(xp[c])
        m["xs"] = np.ascontiguousarray(xs[sidx].reshape(NS * T_S, D))
        m["ptab"] = np.ascontiguousarray(pt[sidx].reshape(NS, 1, -1))
        m["swin"] = np.ascontiguousarray(swin[:, sidx].reshape(DEPTH, NS, 512, 512))
        m["srw"] = np.ascontiguousarray(srw[:, sidx])
        m["ssh"] = np.ascontiguousarray(ssh[:, sidx])
        maps.append(m)
    return maps


def assemble(res, ncu, T_Pc, NS):
    kv = (2, 4, 64)
    B, BS = ncu, ncu * NS
    st = lambda key, shp: np.stack([res[c][key].reshape(shp) for c in range(ncu)], axis=0)
    y_p = st("y_p", (T_Pc, D))
    y_s = np.zeros((BS, T_S, D), np.float32)
    mv = lambda a: np.moveaxis(a, 0, 1)
    cmp_p = mv(st("cmp_p", (DEPTH, T_Pc) + kv)); sel_p = mv(st("sel_p", (DEPTH, T_Pc) + kv)); win_p = mv(st("win_p", (DEPTH, 512) + kv))
    rw_p = mv(st("rw_p", (DEPTH, 16, 64, 64))); sh_p = mv(st("sh_p", (DEPTH, 1, RW)))
    cmp_s = np.zeros((DEPTH, BS, T_S) + kv, np.float32); sel_s = np.zeros_like(cmp_s)
    win_s = np.zeros((DEPTH, BS, 512) + kv, np.float32); rw_s = np.zeros((DEPTH, BS, 16, 64, 64), np.float32); sh_s = np.zeros((DEPTH, BS, 1, RW), np.float32)
    for c in range(ncu):
        r = res[c]
        for j in range(NS):
            b = c + j * ncu
            y_s[b] = r["y_s"].reshape(NS, T_S, D)[j]
            cmp_s[:, b] = r["cmp_s"].reshape((DEPTH, NS, T_S) + kv)[:, j]
            sel_s[:, b] = r["sel_s"].reshape((DEPTH, NS, T_S) + kv)[:, j]
            win_s[:, b] = r["win_s"].reshape((DEPTH, NS, 512) + kv)[:, j]
            rw_s[:, b] = r["rw_s"].reshape(DEPTH, NS, 16, 64, 64)[:, j]
            sh_s[:, b] = r["sh_s"].reshape(DEPTH, NS, 1, RW)[:, j]
    return (y_p, y_s, cmp_p, sel_p, win_p, rw_p, sh_p, cmp_s, sel_s, win_s, rw_s, sh_s)


def kernel(**inputs):
    nc = build_nc()
    maps = make_in_maps(inputs, NCU, T_P, 16384, 1280, 2)
    res = run_bass_kernel_spmd(nc, maps, core_ids=list(range(NCU))).results
    return assemble(res, NCU, T_P, 2)
```
